# Optimizing a Trainium2 kernel written in Bass

```python
import jax, jax.numpy as jnp
from jax import lax
import numpy as np

D_MODEL = 1024
BATCH = 8
SEQ = 2048
DEPTH = 2
DEC_BATCH = 128
DEC_SEQ = 1
PAST_LEN = 16384
PAGE_SIZE = 128

N_MIXERS = 2
N_CONV_LAYERS = (DEPTH + 1) // 2
N_GMLP_LAYERS = DEPTH // 2
CONV_WIDTH = 31
CONV_CTX = CONV_WIDTH - 1
GMLP_CHUNK = 128
GMLP_WIDTH = 2 * D_MODEL
GMLP_GROUPS = 8
GMLP_GROUP_DIM = GMLP_WIDTH // GMLP_GROUPS
D_FF = -(-8 * D_MODEL // (3 * 256)) * 256
EPS = 1e-6

kernel_name = "hybrid_conformer_conv_chunk_gmlp_adaln_step"


def rmsnorm(x, g):
    xf = x.astype(jnp.float32)
    xf = xf * lax.rsqrt(jnp.mean(xf * xf, axis=-1, keepdims=True) + EPS)
    return (xf * g.astype(jnp.float32)).astype(x.dtype)


def layernorm(x, g, b):
    xf = x.astype(jnp.float32)
    mu = jnp.mean(xf, axis=-1, keepdims=True)
    xc = xf - mu
    var = jnp.mean(xc * xc, axis=-1, keepdims=True)
    y = xc * lax.rsqrt(var + EPS) * g.astype(jnp.float32) + b.astype(jnp.float32)
    return y.astype(x.dtype)


def modulate(h, shift, scale):
    return h * (1 + scale[:, None, :]) + shift[:, None, :]


def conv_module(h, ctx, w_pw1, b_pw1, w_dw, b_dw, ln_g, ln_b, w_pw2, b_pw2):
    a = h @ w_pw1 + b_pw1
    z = a[..., :D_MODEL] * jax.nn.sigmoid(a[..., D_MODEL:])
    full = jnp.concatenate([ctx.astype(z.dtype), z], axis=1)
    y = lax.conv_general_dilated(
        full, w_dw[:, None, :].astype(full.dtype), window_strides=(1,), padding='VALID',
        dimension_numbers=('NWC', 'WIO', 'NWC'), feature_group_count=D_MODEL) + b_dw
    y = jax.nn.silu(layernorm(y, ln_g, ln_b))
    out = y @ w_pw2 + b_pw2
    return out, full[:, -CONV_CTX:, :]


def chunk_mix(v, w_s, b_s):
    B, T, _ = v.shape
    w = jnp.tril(w_s).astype(v.dtype)
    if T < GMLP_CHUNK:
        L, n_chunks, vp = T, 1, v
        w, b = w[:, :T, :T], b_s[:, :T]
    else:
        L = GMLP_CHUNK
        n_chunks = -(-T // GMLP_CHUNK)
        vp = jnp.pad(v, ((0, 0), (0, n_chunks * L - T), (0, 0)))
        b = b_s
    vc = vp.reshape(B, n_chunks, L, GMLP_GROUPS, GMLP_GROUP_DIM)
    m = jnp.einsum('gts,bnsge->bntge', w, vc) + b.T.astype(v.dtype)[None, None, :, :, None]
    return m.reshape(B, n_chunks * L, GMLP_WIDTH)[:, :T]


def gmlp_module(h, w_in, b_in, ln_g, ln_b, w_s, b_s, w_out, b_out):
    a = jax.nn.gelu(h @ w_in + b_in)
    u, v = a[..., :GMLP_WIDTH], a[..., GMLP_WIDTH:]
    v = layernorm(v, ln_g, ln_b)
    mixed = chunk_mix(v, w_s, b_s)
    return (u * mixed) @ w_out + b_out, v


def swiglu(h, w_gate, w_up, w_down):
    return (jax.nn.silu(h @ w_gate) * (h @ w_up)) @ w_down


def trunk(x, c, conv_ctx, w_ada, b_ada, norm_mix_g, norm_ffn_g, final_norm_g,
          conv_w_pw1, conv_b_pw1, conv_w_dw, conv_b_dw, conv_ln_g, conv_ln_b, conv_w_pw2, conv_b_pw2,
          gmlp_w_in, gmlp_b_in, gmlp_ln_g, gmlp_ln_b, gmlp_w_s, gmlp_b_s, gmlp_w_out, gmlp_b_out,
          ffn_w_gate, ffn_w_up, ffn_w_down):
    conv_new, gmlp_v = [], []
    cs = jax.nn.silu(c)
    for i in range(DEPTH):
        mod = cs @ w_ada[i] + b_ada[i]
        sh1, sc1, g1, sh2, sc2, g2 = jnp.split(mod, 6, axis=-1)
        h = modulate(rmsnorm(x, norm_mix_g[i]), sh1, sc1)
        j = i // N_MIXERS
        if i % N_MIXERS == 0:
            out, ctx = conv_module(h, conv_ctx[j], conv_w_pw1[j], conv_b_pw1[j], conv_w_dw[j], conv_b_dw[j],
                                   conv_ln_g[j], conv_ln_b[j], conv_w_pw2[j], conv_b_pw2[j])
            conv_new.append(ctx)
        else:
            out, v = gmlp_module(h, gmlp_w_in[j], gmlp_b_in[j], gmlp_ln_g[j], gmlp_ln_b[j],
                                 gmlp_w_s[j], gmlp_b_s[j], gmlp_w_out[j], gmlp_b_out[j])
            gmlp_v.append(v)
        x = x + g1[:, None, :] * out
        h = modulate(rmsnorm(x, norm_ffn_g[i]), sh2, sc2)
        x = x + g2[:, None, :] * swiglu(h, ffn_w_gate[i], ffn_w_up[i], ffn_w_down[i])
    return rmsnorm(x, final_norm_g), jnp.stack(conv_new), jnp.stack(gmlp_v)


def setup_inputs(seed: int = 0) -> dict:
    key = jax.random.key(seed)
    ks = iter(jax.random.split(key, 40))
    f32 = jnp.float32
    nrm = lambda shape, s: jax.random.normal(next(ks), shape, f32) * s
    gain = lambda shape: 1.0 + nrm(shape, 0.02)
    D, E, F, C, G = D_MODEL, GMLP_WIDTH, D_FF, GMLP_CHUNK, GMLP_GROUPS
    NA, NB = N_CONV_LAYERS, N_GMLP_LAYERS
    return {
        "x_prompt": nrm((BATCH, SEQ, D), 1.0),
        "x_sample": nrm((DEC_BATCH, DEC_SEQ, D), 1.0),
        "c_prompt": nrm((BATCH, D), 1.0),
        "c_sample": nrm((DEC_BATCH, D), 1.0),
        "state_conv": nrm((NA, DEC_BATCH, CONV_CTX, D), 0.5),
        "w_ada": nrm((DEPTH, D, 6 * D), 0.5 * D ** -0.5),
        "b_ada": nrm((DEPTH, 6 * D), 0.02),
        "norm_mix_g": gain((DEPTH, D)),
        "norm_ffn_g": gain((DEPTH, D)),
        "final_norm_g": gain((D,)),
        "conv_w_pw1": nrm((NA, D, 2 * D), D ** -0.5),
        "conv_b_pw1": nrm((NA, 2 * D), 0.02),
        "conv_w_dw": nrm((NA, CONV_WIDTH, D), CONV_WIDTH ** -0.5),
        "conv_b_dw": nrm((NA, D), 0.02),
        "conv_ln_g": gain((NA, D)),
        "conv_ln_b": nrm((NA, D), 0.02),
        "conv_w_pw2": nrm((NA, D, D), D ** -0.5),
        "conv_b_pw2": nrm((NA, D), 0.02),
        "gmlp_w_in": nrm((NB, D, 2 * E), D ** -0.5),
        "gmlp_b_in": nrm((NB, 2 * E), 0.02),
        "gmlp_ln_g": gain((NB, E)),
        "gmlp_ln_b": nrm((NB, E), 0.02),
        "gmlp_w_s": nrm((NB, G, C, C), C ** -0.5),
        "gmlp_b_s": gain((NB, G, C)),
        "gmlp_w_out": nrm((NB, E, D), E ** -0.5),
        "gmlp_b_out": nrm((NB, D), 0.02),
        "ffn_w_gate": nrm((DEPTH, D, F), D ** -0.5),
        "ffn_w_up": nrm((DEPTH, D, F), D ** -0.5),
        "ffn_w_down": nrm((DEPTH, F, D), F ** -0.5),
    }


def reference(x_prompt, x_sample, c_prompt, c_sample, state_conv, w_ada, b_ada, norm_mix_g, norm_ffn_g,
              final_norm_g, conv_w_pw1, conv_b_pw1, conv_w_dw, conv_b_dw, conv_ln_g, conv_ln_b, conv_w_pw2,
              conv_b_pw2, gmlp_w_in, gmlp_b_in, gmlp_ln_g, gmlp_ln_b, gmlp_w_s, gmlp_b_s, gmlp_w_out,
              gmlp_b_out, ffn_w_gate, ffn_w_up, ffn_w_down):
    params = (w_ada, b_ada, norm_mix_g, norm_ffn_g, final_norm_g,
              conv_w_pw1, conv_b_pw1, conv_w_dw, conv_b_dw, conv_ln_g, conv_ln_b, conv_w_pw2, conv_b_pw2,
              gmlp_w_in, gmlp_b_in, gmlp_ln_g, gmlp_ln_b, gmlp_w_s, gmlp_b_s, gmlp_w_out, gmlp_b_out,
              ffn_w_gate, ffn_w_up, ffn_w_down)
    zero_ctx = jnp.zeros((N_CONV_LAYERS, x_prompt.shape[0], CONV_CTX, D_MODEL), x_prompt.dtype)
    y_prompt, conv_state_prompt, _ = trunk(x_prompt, c_prompt, zero_ctx, *params)
    y_sample, conv_state_sample, gmlp_v_sample = trunk(x_sample, c_sample, state_conv, *params)
    return (y_prompt, y_sample, conv_state_prompt, conv_state_sample, gmlp_v_sample)
```

```python
import numpy as np
import concourse.bass as bass
import concourse.mybir as mybir
from concourse.bass_utils import run_bass_kernel_spmd

F32 = mybir.dt.float32
BF16 = mybir.dt.bfloat16
AF = mybir.ActivationFunctionType
ALU = mybir.AluOpType
AX = mybir.AxisListType

NCORES = 8
D = 1024
KC = 8
SEQ = 2048
NT = 512
NTI = 4
NS = 16
E = 2048
EC = 16
FF = 2816
FCH = 22
CW = 31
CTX = 30
EPS = 1e-6
SLOT = 4096
TEMP_BYTES = 77824

PP_LAYOUT = [('nmg0', 8), ('nmg1', 8), ('nfg0', 8), ('nfg1', 8), ('fng', 8), ('bada0', 48), ('bada1', 48),
             ('b1', 16), ('bdw', 8), ('lng', 8), ('lnb', 8), ('b2', 8), ('binu', 16), ('binv', 16),
             ('glng', 16), ('glnb', 16), ('bout', 8), ('wdw', 248), ('ws00', 16), ('bs00', 16), ('negm', 1)]
PP_OFF = {}
_o = 0
for _n, _w in PP_LAYOUT:
    PP_OFF[_n] = (_o, _w)
    _o += _w
NPP = _o


class Buf:
    __slots__ = ('name', 'w', 'r', 'const')

    def __init__(self, name, seed=None, const=False):
        self.name = name
        self.w = None
        self.r = list(seed) if seed else []
        self.const = const


class Sched:
    ENG = ('pe', 'act', 'dve', 'pool', 'sp')

    def __init__(self):
        self.stream = {e: [] for e in self.ENG}
        self.cnt = {'pe': 0, 'act': 0, 'dve': 0}
        self.waited = {e: {} for e in self.ENG}
        self.dcnt = {}
        self.sp_rr = 0

    def _deps(self, reads, writes):
        d = {}

        def add(t):
            if t is not None and d.get(t[0], 0) < t[1]:
                d[t[0]] = t[1]
        for b in reads:
            add(b.w)
        for b in writes:
            add(b.w)
            for t in b.r:
                add(t)
        return d

    def _waits(self, eng, d):
        out = []
        for k, v in d.items():
            if eng == 'pe' and k == 'P_pe':
                continue
            if self.waited[eng].get(k, 0) >= v:
                continue
            self.waited[eng][k] = v
            out.append((k, v))
        return out

    def _upd(self, reads, writes, tok):
        for b in reads:
            if not b.const:
                b.r.append(tok)
        for b in writes:
            b.w = tok
            b.r = []

    def comp(self, eng, fn, reads=(), writes=()):
        d = self._deps(reads, writes)
        w = self._waits(eng, d)
        self.cnt[eng] += 1
        tok = ('P_' + eng, self.cnt[eng])
        self.stream[eng].append((w, fn, ('P_' + eng, 1)))
        self._upd(reads, writes, tok)
        return tok

    def dma(self, q, out, in_, reads=(), writes=(), sem=None):
        if sem is None:
            sem = ('S%d' if q == 'sp' else 'Q%d') % (self.sp_rr % 8 if q == 'sp' else self.sp_rr % 4)
            self.sp_rr += 1
        d = self._deps(reads, writes)
        if q == 'pool':
            hist = self.__dict__.setdefault('pool_hist', [])
            if len(hist) >= 3:
                t = hist[-3]
                if d.get(t[0], 0) < t[1]:
                    d[t[0]] = t[1]
        c = self.dcnt.get(sem, 0)
        if c:
            d[sem] = max(d.get(sem, 0), c)
        w = self._waits(q, d)
        self.dcnt[sem] = c + 16
        tok = (sem, c + 16)
        self.stream[q].append((w, (lambda e, o=out, i=in_: e.dma_start(out=o, in_=i)), (sem, 16)))
        if q == 'pool':
            self.pool_hist.append(tok)
        self._upd(reads, writes, tok)
        return tok

    def all_tokens(self):
        t = [('P_' + e, c) for e, c in self.cnt.items() if c]
        t += [(s, c) for s, c in self.dcnt.items() if c and not s.startswith('A')]
        return t

    def final_wait(self, q):
        d = {s: c for s, c in self.dcnt.items() if c}
        for e, c in self.cnt.items():
            if c:
                d['P_' + e] = c
        self.stream[q].append((self._waits(q, d), None, None))


def build_program(plan=None):
    nc = bass.Bass("TRN2", target_bir_lowering=False)
    S = Sched()

    def din(name, shape):
        return nc.dram_tensor(name, list(shape), F32, kind="ExternalInput").ap()

    def dout(name, shape):
        return nc.dram_tensor(name, list(shape), F32, kind="ExternalOutput").ap()

    xp_d = din("xp", [128, KC, SEQ])
    xs_d = din("xs", [128, KC, NS])
    cin_d = din("cin", [128, KC, 1 + NS])
    sc_d = din("sc", [NS, CTX, D])
    scT_d = din("scT", [128, KC, NS, CTX])
    pp_d = din("pp", [128, NPP])
    rw_d = din("rw", [128, 3 * E + D])
    wsT_d = din("wsT", [128, 8, 128])
    tril_d = din("tril", [128, 128])
    ident_d = din("ident", [128, 128])
    w_ada_d = din("w_ada", [2, D, 6 * D])
    w_pw1_d = din("w_pw1", [D, 2 * D])
    w_pw2_d = din("w_pw2", [D, D])
    w_in_d = din("w_in", [D, 2 * E])
    w_out_d = din("w_out", [E, D])
    w_gate_d = din("w_gate", [2, D, FF])
    w_up_d = din("w_up", [2, D, FF])
    w_down_d = din("w_down", [2, FF, D])

    yp_d = dout("yp", [128, KC, SEQ])
    ys_d = dout("ys", [128, KC, NS])
    csp_d = dout("csp", [128, KC, CTX])
    cssc_d = dout("cssc", [NS, CTX - 1, D])
    zs_d = dout("zs", [128, KC, NS])
    gv_d = dout("gv", [128, EC, NS])

    def sb(name, shape, dt=F32):
        return nc.alloc_sbuf_tensor(name, list(shape), dt)

    X = sb("X", [128, KC, SEQ])
    XS = sb("XS", [128, KC, NS])
    PP = sb("PP", [128, NPP])
    IDB = sb("IDB", [128, 128], BF16)
    ONESB = sb("ONESB", [128, 128], BF16)
    CS = sb("CS", [128, KC, 1 + NS], BF16)
    _M0 = sb("MOD0", [128, 6, KC, 1 + NS])
    _D0 = sb("DER0", [128, 3, KC, 1 + NS])
    MOD = [_M0, _M0]
    DER = [_D0, _D0]
    ZS = sb("ZS", [128, KC, NS])[:, :, :]
    UM_F32 = sb("VFs", [128, EC, NS])[:, :, :]
    ST2 = sb("ST2", [128, 3 * NS])
    HB1 = sb("HB1", [128, 8])
    ZL = sb("ZL", [128, KC, CTX])
    TEMP = sb("TEMP", [128, TEMP_BYTES // 4])
    nslot = int((nc.sbuf_bytes_remaining - 256) // (SLOT * 2))
    assert nslot >= 7, (nslot, nc.sbuf_bytes_remaining)
    nslot = min(nslot, 7)
    ARENA = sb("ARENA", [128, nslot * SLOT], BF16)
    PS = nc.alloc_psum_tensor("PS", [128, 8, 512], F32)

    def ppc(name, c):
        o, w = PP_OFF[name]
        assert 0 <= c < w
        return PP[:, o + c:o + c + 1]

    def ppv(name):
        o, w = PP_OFF[name]
        return PP[:, o:o + w]

    B_PP = Buf("PP", const=True)
    B_ID = Buf("ID", const=True)
    B_CIN = Buf("CIN")
    B_CS = Buf("CS")
    _bm = [Buf("MOD_%d" % j) for j in range(6)]
    _bd = [Buf("DER_%d" % j) for j in range(3)]
    B_MOD = [_bm, _bm]
    B_DER = [_bd, _bd]
    B_ZSb = [Buf("ZS%d" % c) for c in range(KC)]
    B_VF = Buf("VFs")
    B_ST2 = Buf("ST2")
    B_HB1 = Buf("HB1")
    B_ZL = Buf("ZL")
    B_X = [[Buf("X%d_%d" % (t, c)) for c in range(KC)] for t in range(NTI)]
    B_XS = [Buf("XS%d" % c) for c in range(KC)]
    B_PS = [Buf("PS%d" % b) for b in range(8)]

    ps_state = {'n': 0}

    def ps_next():
        b = ps_state['n'] % 8
        ps_state['n'] += 1
        return b

    ps4 = {'n': 0}

    def ps_next4():
        g = ps4['n'] % 2
        ps4['n'] += 1
        return g * 4

    def wrows(w2d, r0, nk, c0, ncol):
        return w2d[r0 * 128:(r0 + nk) * 128, c0:c0 + ncol].rearrange("(k p) m -> p k m", p=128)

    FUNITS = []
    _f = 0
    for _n in (2, 4, 4, 4, 4, 4):
        FUNITS.append((_f, _n))
        _f += _n
    assert _f == FCH

    def piece_spec(key):
        k = key[0]
        if k == 'ada':
            _, l, j, h = key
            return wrows(w_ada_d[l], 0, KC, j * D + h * 512, 512), KC, 512
        if k == 'pw1':
            return wrows(w_pw1_d, 0, KC, key[1] * 512, 512), KC, 512
        if k == 'pw2':
            return wrows(w_pw2_d, 0, KC, key[1] * 512, 512), KC, 512
        if k in ('g', 'u'):
            _, l, u = key
            f0, n = FUNITS[u]
            return wrows((w_gate_d if k == 'g' else w_up_d)[l], 0, KC, f0 * 128, n * 128), KC, n * 128
        if k == 'd':
            _, l, u = key
            f0, n = FUNITS[u]
            return wrows(w_down_d[l], f0, n, 0, D), n, D
        if k == 'wv':
            return wrows(w_in_d, 2 * key[1], 2, E, E), 2, E
        if k == 'wu':
            return wrows(w_in_d, 0, KC, key[2] * 512, 512), KC, 512
        if k == 'wo':
            return wrows(w_out_d, 0, EC, key[2] * 256, 256), EC, 256
        raise KeyError(key)

    SCR = {}

    class Arena:
        def __init__(self):
            self.free = list(range(nslot))
            self.bufs = [Buf("slot%d" % k) for k in range(nslot)]
            self.loaded = {}
            self.order = []
            self.plan = plan
            self.pi = 0
            self.wish = []

        def set_wish(self, keys):
            self.wish = list(keys)
            self._try_wish()

        def _try_wish(self):
            if self.plan is not None:
                self.pump()
                return
            while self.wish and self.free:
                k = self.wish.pop(0)
                if k not in self.loaded:
                    self._issue(k)

        def _issue(self, key):
            src, a, b = piece_spec(key)
            k = self.free.pop(0)
            v = ARENA[:, k * SLOT:k * SLOT + a * b].rearrange("p (a b) -> p a b", a=a)
            rd = []
            scr = None
            if key[0] in ('wu', 'wo'):
                skey = (key[0], key[2])
                if skey not in SCR:
                    SCR[skey] = (nc.dram_tensor("scr_%s%d" % skey, [128, a, b], BF16, kind="Internal").ap(),
                                 Buf("scr_%s%d" % skey))
                    scr = SCR[skey]
                else:
                    src = SCR[skey][0][:, :, :]
                    rd = [SCR[skey][1]]
            S.dma('pool', v, src, reads=rd, writes=[self.bufs[k]], sem='A%d' % k)
            if scr is not None:
                S.dma('sp', scr[0][:, :, :], v, reads=[self.bufs[k]], writes=[scr[1]], sem='AW%d' % (len(SCR) % 4))
            self.loaded[key] = (k, v)
            self.order.append(key)

        def pump(self):
            while self.free and self.pi < len(self.plan):
                key = self.plan[self.pi]
                self.pi += 1
                self._issue(key)

        def load(self, key, must=True):
            if key in self.loaded:
                return True
            if self.plan is not None:
                self.pump()
                assert key in self.loaded or not must, "arena full for %s" % (key,)
                return key in self.loaded
            if not self.free:
                assert not must, "arena full for %s" % (key,)
                return False
            self._issue(key)
            return True

        def get(self, key):
            self.load(key)
            k, v = self.loaded[key]
            return v, self.bufs[k]

        def release(self, key):
            k, _ = self.loaded.pop(key)
            self.free.append(k)
            self._try_wish()

    AR = Arena()

    tstate = {'off': 0}

    def phase_begin():
        tstate['off'] = 0
        return S.all_tokens()

    def talloc(shape, dt, seed):
        n = 1
        for s_ in shape[1:]:
            n *= s_
        nbytes = n * (4 if dt == F32 else 2)
        nbytes = (nbytes + 31) // 32 * 32
        o = tstate['off']
        assert o + nbytes <= TEMP_BYTES, ("TEMP overflow", o, nbytes)
        tstate['off'] = o + nbytes
        v = TEMP[:, o // 4:(o + nbytes) // 4]
        if dt != F32:
            v = v.bitcast(dt)
        v = v[:, 0:n]
        if len(shape) == 3:
            v = v.rearrange("p (a b) -> p a b", a=shape[1])
        elif len(shape) == 4:
            v = v.rearrange("p (a b c) -> p a b c", a=shape[1], b=shape[2])
        return v

    def talloc_at(off, shape, dt):
        save = tstate['off']
        tstate['off'] = off
        v = talloc(shape, dt, None)
        end = tstate['off']
        tstate['off'] = save
        return v, end

    def mm_group(out, pairs, reads, writes, first=True, last=True):
        n = len(pairs)

        def fn(pe, out=out, pairs=pairs, n=n, first=first, last=last):
            ins = None
            for i, (l, r) in enumerate(pairs):
                ins = pe.matmul(out, lhsT=l, rhs=r, start=(first and i == 0), stop=(last and i == n - 1))
            return ins
        return S.comp('pe', fn, reads, writes)

    def mm_multi(groups, reads, writes):
        def fn(pe, groups=groups):
            ins = None
            for (o, prs) in groups:
                for i, (l, r) in enumerate(prs):
                    ins = pe.matmul(o, lhsT=l, rhs=r, start=(i == 0), stop=(i == len(prs) - 1))
            return ins
        return S.comp('pe', fn, reads, writes)

    def act(out, in_, func, reads, writes, bias=None, scale=None, accum=None):
        kw = {}
        if bias is not None:
            kw['bias'] = bias
        if scale is not None:
            kw['scale'] = scale
        if accum is not None:
            kw['accum_out'] = accum
        return S.comp('act', lambda a, o=out, i=in_, f=func, kw=kw: a.activation(out=o, in_=i, func=f, **kw),
                      reads, writes)

    def tt(out, in0, in1, op, reads, writes):
        return S.comp('dve', lambda v, o=out, a=in0, b=in1, op=op: v.tensor_tensor(out=o, in0=a, in1=b, op=op),
                      reads, writes)

    def ts(out, in0, s1, op0, reads, writes, s2=None, op1=None):
        if s2 is None:
            return S.comp('dve', lambda v, o=out, a=in0, s1=s1, op0=op0:
                          v.tensor_scalar(out=o, in0=a, scalar1=s1, scalar2=None, op0=op0), reads, writes)
        return S.comp('dve', lambda v, o=out, a=in0, s1=s1, s2=s2, op0=op0, op1=op1:
                      v.tensor_scalar(out=o, in0=a, scalar1=s1, scalar2=s2, op0=op0, op1=op1), reads, writes)

    def stt(out, in0, sc, in1, op0, op1, reads, writes):
        return S.comp('dve', lambda v, o=out, a=in0, s=sc, b=in1, op0=op0, op1=op1:
                      v.scalar_tensor_tensor(out=o, in0=a, scalar=s, in1=b, op0=op0, op1=op1), reads, writes)

    def cp(out, in_, reads, writes):
        return S.comp('dve', lambda v, o=out, i=in_: v.tensor_copy(out=o, in_=i), reads, writes)

    def recip(out, in_, reads, writes):
        return S.comp('dve', lambda v, o=out, i=in_: v.reciprocal(out=o, in_=i), reads, writes)

    def memset(ap, val, writes):
        return S.comp('dve', lambda v, a=ap, c=val: v.memset(a, c), (), writes)

    class Tile:
        pass

    tiles = []
    for t in range(NTI):
        T = Tile()
        T.N = NT
        T.sample = False
        T.idx = t
        T.t0 = t * NT
        T.xb = B_X[t]
        T.x3 = X[:, :, t * NT:(t + 1) * NT]
        T.xc = [X[:, c, t * NT:(t + 1) * NT] for c in range(KC)]
        tiles.append(T)
    TS_ = Tile()
    TS_.N = NS
    TS_.sample = True
    TS_.idx = NTI
    TS_.t0 = 0
    TS_.xb = B_XS
    TS_.x3 = XS[:, :, :]
    TS_.xc = [XS[:, c, :] for c in range(KC)]
    tiles.insert(0, TS_)

    def modcol(l, j, c):
        return MOD[l][:, j, c, 0:1]

    def mods(l, j):
        return MOD[l][:, j, :, 1:1 + NS]

    def dercol(l, j, c):
        return DER[l][:, j, c, 0:1]

    def ders(l, j):
        return DER[l][:, j, :, 1:1 + NS]

    seed = phase_begin()
    IDF = talloc([128, 128], F32, seed)
    CIN = talloc([128, KC, 1 + NS], F32, seed)
    B_IDF = Buf("IDF", seed)
    S.dma('sp', PP[:, :], pp_d[:, :], writes=[B_PP])
    S.dma('sp', CIN, cin_d[:, :, :], writes=[B_CIN])
    S.dma('sp', IDF, ident_d[:, :], writes=[B_IDF])
    S.dma('sp', XS[:, :, :], xs_d[:, :, :], writes=B_XS)
    cp(IDB[:, :], IDF, [B_IDF], [B_ID])
    memset(ONESB[:, :], 1.0, [B_ID])
    B_ID.const = True
    o_b1, _ = PP_OFF['b1']
    ts(HB1[:, :], PP[:, o_b1 + 8:o_b1 + 16], 0.5, ALU.mult, [B_PP], [B_HB1])
    B_HB1.const = True
    act(CS[:, :, :], CIN, AF.Silu, [B_CIN], [B_CS])
    B_CS.const = True

    def load_x_tiles():
        for t in range(NTI):
            S.dma('pool', X[:, :, t * NT:(t + 1) * NT], xp_d[:, :, t * NT:(t + 1) * NT], writes=B_X[t])

    def ada_half(l, j, h):
        key = ('ada', l, j, h)
        wv, wb = AR.get(key)
        b = ps_next()
        pso = PS[:, b, 0:4 * (1 + NS)].rearrange("p (m n) -> p m n", m=4)
        for m4 in range(4):
            pairs = [(wv[:, kc, m4 * 128:(m4 + 1) * 128], CS[:, kc, :]) for kc in range(KC)]
            mm_group(pso[:, m4, :], pairs, [wb, B_CS], [B_PS[b]])
        AR.release(key)
        o, _ = PP_OFF['bada%d' % l]
        bcol = PP[:, o + j * 8 + 4 * h:o + j * 8 + 4 * h + 4].unsqueeze(2).broadcast_to([128, 4, 1 + NS])
        tt(MOD[l][:, j, 4 * h:4 * h + 4, :], pso, bcol, ALU.add, [B_PS[b], B_PP], [B_MOD[l][j]])
        if h == 1:
            if j == 1 or j == 4:
                ng = ppv('nmg%d' % l if j == 1 else 'nfg%d' % l).unsqueeze(2).broadcast_to([128, KC, 1 + NS])
                dj = 0 if j == 1 else 2
                stt(DER[l][:, dj, :, :], MOD[l][:, j, :, :], 1.0, ng, ALU.add, ALU.mult,
                    [B_MOD[l][j], B_PP], [B_DER[l][dj]])
            if j == 2:
                bo = ppv('b2' if l == 0 else 'bout').unsqueeze(2).broadcast_to([128, KC, 1 + NS])
                tt(DER[l][:, 1, :, :], MOD[l][:, 2, :, :], bo, ALU.mult, [B_MOD[l][2], B_PP], [B_DER[l][1]])

    class AdaStream:
        def __init__(self, items):
            self.items = [('ada',) + tuple(it) for it in items]
            self.i = 0
            if self.items:
                AR.set_wish([self.items[0]])

        def tick(self):
            if self.i < len(self.items):
                _, l, j, h = self.items[self.i]
                ada_half(l, j, h)
                self.i += 1
                if self.i < len(self.items):
                    AR.set_wish([self.items[self.i]])

        def drain(self):
            while self.i < len(self.items):
                self.tick()

    def rms_stats(T, SQ, B_SQ, RSTD, B_RSTD):
        N = T.N
        act(SQ[:, :, 0:N], T.x3, AF.Square, T.xb, [B_SQ])
        b = ps_next()
        mm_group(PS[:, b, 0:N], [(ONESB[:, :], SQ[:, c, 0:N]) for c in range(KC)], [B_SQ, B_ID], [B_PS[b]])
        act(RSTD[:, 0:N], PS[:, b, 0:N], AF.Sqrt, [B_PS[b]], [B_RSTD], bias=EPS, scale=1.0 / D)
        recip(RSTD[:, 0:N], RSTD[:, 0:N], [B_RSTD], [B_RSTD])

    def norm_mod(T, l, which, SQ, B_SQ, RSTD, B_RSTD, TMPF, B_TMPF, Hc, B_H, TMPF2=None, B_TMPF2=None):
        N = T.N
        rms_stats(T, SQ, B_SQ, RSTD, B_RSTD)
        jsh = 0 if which == 1 else 3
        dj = 0 if which == 1 else 2
        if not T.sample:
            for c in range(KC):
                tp, tb = (TMPF, B_TMPF) if (c % 2 == 0 or TMPF2 is None) else (TMPF2, B_TMPF2)
                stt(tp[:, 0:N], T.xc[c], dercol(l, dj, c), RSTD[:, 0:N], ALU.mult, ALU.mult,
                    [T.xb[c], B_DER[l][dj], B_RSTD], [tb])
                act(Hc(c), tp[:, 0:N], AF.Identity, [tb, B_MOD[l][jsh]], [B_H[c]], bias=modcol(l, jsh, c))
        else:
            T3 = TMPF[:, 0:KC * N].rearrange("p (c n) -> p c n", c=KC)
            tt(T3, T.x3, RSTD[:, 0:N].unsqueeze(1).broadcast_to([128, KC, N]), ALU.mult,
               list(T.xb) + [B_RSTD], [B_TMPF])
            tt(T3, T3, ders(l, dj), ALU.mult, [B_TMPF, B_DER[l][dj]], [B_TMPF])
            for c in range(KC):
                tt(Hc(c), T3[:, c, :], MOD[l][:, jsh, c, 1:1 + NS], ALU.add, [B_TMPF, B_MOD[l][jsh]], [B_H[c]])

    def norm_mod_gen(T, l, which, SQ, B_SQ, RSTD, B_RSTD, TMPF, B_TMPF, Hc, B_H, TMPF2=None, B_TMPF2=None):
        N = T.N
        assert not T.sample
        act(SQ[:, :, 0:N], T.x3, AF.Square, T.xb, [B_SQ])
        b = ps_next()
        mm_group(PS[:, b, 0:N], [(ONESB[:, :], SQ[:, c, 0:N]) for c in range(KC)], [B_SQ, B_ID], [B_PS[b]])
        yield
        act(RSTD[:, 0:N], PS[:, b, 0:N], AF.Sqrt, [B_PS[b]], [B_RSTD], bias=EPS, scale=1.0 / D)
        recip(RSTD[:, 0:N], RSTD[:, 0:N], [B_RSTD], [B_RSTD])
        yield
        jsh = 0 if which == 1 else 3
        dj = 0 if which == 1 else 2
        for c in range(KC):
            tp, tb = (TMPF, B_TMPF) if (c % 2 == 0 or TMPF2 is None) else (TMPF2, B_TMPF2)
            stt(tp[:, 0:N], T.xc[c], dercol(l, dj, c), RSTD[:, 0:N], ALU.mult, ALU.mult,
                [T.xb[c], B_DER[l][dj], B_RSTD], [tb])
            act(Hc(c), tp[:, 0:N], AF.Identity, [tb, B_MOD[l][jsh]], [B_H[c]], bias=modcol(l, jsh, c))
            yield

    def x_update(T, l, gj, m, psv, B_psv, TT_, B_TT, has_bias):
        N = T.N
        if not T.sample:
            if has_bias:
                act(TT_[:, 0:N], psv, AF.Identity, [B_psv, B_MOD[l][gj], B_DER[l][1]], [B_TT],
                    bias=dercol(l, 1, m), scale=modcol(l, gj, m))
                tt(T.xc[m], T.xc[m], TT_[:, 0:N], ALU.add, [B_TT, T.xb[m]], [T.xb[m]])
            else:
                stt(T.xc[m], psv, modcol(l, gj, m), T.xc[m], ALU.mult, ALU.add,
                    [B_psv, B_MOD[l][gj], T.xb[m]], [T.xb[m]])
        else:
            tt(TT_[:, 0:N], psv, MOD[l][:, gj, m, 1:1 + NS], ALU.mult, [B_psv, B_MOD[l][gj]], [B_TT])
            if has_bias:
                tt(TT_[:, 0:N], TT_[:, 0:N], DER[l][:, 1, m, 1:1 + NS], ALU.add, [B_TT, B_DER[l][1]], [B_TT])
            tt(T.xc[m], T.xc[m], TT_[:, 0:N], ALU.add, [B_TT, T.xb[m]], [T.xb[m]])

    def ln_stats(Yc, B_Y, YB, B_YB, YSQ, B_YSQ, nch, N, MEAN, B_MEAN, RS, B_RS, MSQ, B_MSQ, dim):
        b1 = ps_next()
        mm_group(PS[:, b1, 0:N], [(ONESB[:, :], YB[:, c, 0:N]) for c in range(nch)], [B_YB, B_ID], [B_PS[b1]])
        b2 = ps_next()
        mm_group(PS[:, b2, 0:N], [(ONESB[:, :], YSQ[:, c, 0:N]) for c in range(nch)], [B_YSQ, B_ID], [B_PS[b2]])
        ts(MEAN[:, 0:N], PS[:, b1, 0:N], 1.0 / dim, ALU.mult, [B_PS[b1]], [B_MEAN])
        tt(MSQ[:, 0:N], MEAN[:, 0:N], MEAN[:, 0:N], ALU.mult, [B_MEAN], [B_MSQ])
        stt(MSQ[:, 0:N], PS[:, b2, 0:N], 1.0 / dim, MSQ[:, 0:N], ALU.mult, ALU.subtract, [B_PS[b2], B_MSQ], [B_MSQ])
        act(RS[:, 0:N], MSQ[:, 0:N], AF.Sqrt, [B_MSQ], [B_RS], bias=EPS, scale=1.0)
        recip(RS[:, 0:N], RS[:, 0:N], [B_RS], [B_RS])

    def conv_phase():
        l = 0
        seed = phase_begin()
        BA = [talloc([128, KC, NT], BF16, seed) for _ in range(2)]
        SG = talloc([128, NT], F32, seed)
        ZT = talloc([128, KC, CTX + NT], BF16, seed)
        DGALL = talloc([128, 2 * CW * 128], BF16, seed)
        DG = [DGALL[:, i * CW * 128:(i + 1) * CW * 128].rearrange("p (k m) -> p k m", k=CW) for i in range(2)]
        CTXF = DGALL.bitcast(F32)[:, 0:KC * NS * CTX].rearrange("p (c s k) -> p c s k", c=KC, s=NS)
        Y = talloc([128, KC, NT], F32, seed)
        RSTD_A = talloc([128, NT], F32, seed)
        TMPF = talloc([128, NT], F32, seed)
        MEAN = talloc([128, NT], F32, seed)
        RSTD_D = talloc([128, NT], F32, seed)
        MSQ = TMPF
        assert tstate['off'] >= FFN_EARLY_BYTES, tstate['off']
        BB = talloc([128, KC, NT], BF16, seed)
        TT_ = talloc([128, NT], F32, seed)
        B_BA = [[Buf("BA%d_%d" % (i, c), seed) for c in range(KC)] for i in range(2)]
        B_BB = [Buf("BB%d" % c, seed) for c in range(KC)]
        B_SG = Buf("SG", seed)
        B_ZT = [Buf("ZT_%d" % c, seed) for c in range(KC)]
        B_ZTh = Buf("ZTh", seed)
        B_DG = [Buf("DG%d" % i, seed) for i in range(2)]
        B_Y = [Buf("Y%d" % c, seed) for c in range(KC)]
        B_RSTD_A = Buf("RSTD_A", seed)
        B_RSTD_D = Buf("RSTD_D", seed)
        B_MEAN = Buf("MEAN", seed)
        B_TMPF = Buf("TMPF", seed)
        B_MSQ = B_TMPF
        B_TT = Buf("TT", seed)
        B_CTXF = _Multi(B_DG)
        o_w, _ = PP_OFF['wdw']
        WDW = PP[:, o_w:o_w + 248].rearrange("p (c k) -> p c k", c=KC)

        def w1(m):
            v, b = AR.get(('pw1', m // 4))
            return v, (m % 4) * 128, b

        def w2(m):
            v, b = AR.get(('pw2', m // 4))
            return v, (m % 4) * 128, b

        def stA(T, bi):
            norm_mod(T, l, 1, BA[bi], _Multi(B_BA[bi]), RSTD_A, B_RSTD_A, TMPF, B_TMPF,
                     (lambda c, N=T.N, bi=bi: BA[bi][:, c, 0:N]), B_BA[bi], MEAN, B_MEAN)

        def stB(T, bi):
            N = T.N
            for j in range(KC):
                ba = ps_next()
                v, o, wb = w1(j)
                mm_group(PS[:, ba, 0:N], [(v[:, kc, o:o + 128], BA[bi][:, kc, 0:N]) for kc in range(KC)],
                         [wb] + B_BA[bi], [B_PS[ba]])
                bb = ps_next()
                v, o, wb = w1(j + 8)
                mm_group(PS[:, bb, 0:N], [(v[:, kc, o:o + 128], BA[bi][:, kc, 0:N]) for kc in range(KC)],
                         [wb] + B_BA[bi], [B_PS[bb]])
                act(SG[:, 0:N], PS[:, bb, 0:N], AF.Tanh, [B_PS[bb], B_HB1], [B_SG], bias=HB1[:, j:j + 1], scale=0.5)
                ts(SG[:, 0:N], SG[:, 0:N], 0.5, ALU.mult, [B_SG], [B_SG], s2=0.5, op1=ALU.add)
                if not T.sample:
                    stt(ZT[:, j, CTX:CTX + N], PS[:, ba, 0:N], ppc('b1', j), SG[:, 0:N], ALU.add, ALU.mult,
                        [B_PS[ba], B_SG, B_PP], [B_ZT[j]])
                    if T.idx == NTI - 1:
                        stt(ZL[:, j, :], PS[:, ba, N - CTX:N], ppc('b1', j), SG[:, N - CTX:N], ALU.add, ALU.mult,
                            [B_PS[ba], B_SG, B_PP], [B_ZL])
                else:
                    stt(ZS[:, j, :], PS[:, ba, 0:N], ppc('b1', j), SG[:, 0:N], ALU.add, ALU.mult,
                        [B_PS[ba], B_SG, B_PP], [B_ZSb[j]])

        def build_dg(j):
            dgi = j % 2
            tt(DG[dgi][:, :, :], IDB[:, :].unsqueeze(1).broadcast_to([128, CW, 128]),
               WDW[:, j, :].unsqueeze(2).broadcast_to([128, CW, 128]), ALU.mult,
               [B_ID, B_PP], [B_DG[dgi]])

        def stC(T, bi, gen=None):
            N = T.N
            nsteps = [1, 1, 1, 1, 2, 1, 2, 1]
            for j in range(KC):
                dgi = j % 2
                if j + 1 < KC:
                    build_dg(j + 1)
                b = ps_next()
                mm_group(PS[:, b, 0:N], [(DG[dgi][:, k, :], ZT[:, j, k:k + N]) for k in range(CW)],
                         [B_DG[dgi], B_ZT[j], B_ZTh], [B_PS[b]])
                act(Y[:, j, 0:N], PS[:, b, 0:N], AF.Identity, [B_PS[b], B_PP], [B_Y[j]], bias=ppc('bdw', j))
                act(BA[bi][:, j, 0:N], PS[:, b, 0:N], AF.Square, [B_PS[b], B_PP], [B_BA[bi][j]], bias=ppc('bdw', j))
                cp(BB[:, j, 0:N], Y[:, j, 0:N], [B_Y[j]], [B_BB[j]])
                if gen is not None:
                    for _ in range(nsteps[j]):
                        next(gen, None)
            if gen is not None:
                for _ in gen:
                    pass

        def stCsample(T, bi):
            N = T.N
            tt(CTXF, CTXF, WDW[:, :, 0:CTX].unsqueeze(2).broadcast_to([128, KC, NS, CTX]), ALU.mult,
               [B_CTXF, B_PP], [B_CTXF])
            Y3 = Y[:, :, 0:N]
            S.comp('dve', lambda v_, o=Y3, i=CTXF: v_.tensor_reduce(out=o, in_=i, axis=AX.X, op=ALU.add),
                   [B_CTXF], B_Y)
            T3 = TMPF[:, 0:KC * N].rearrange("p (c n) -> p c n", c=KC)
            tt(T3, ZS, WDW[:, :, CTX:CTX + 1].broadcast_to([128, KC, NS]), ALU.mult, B_ZSb + [B_PP], [B_TMPF])
            tt(Y3, Y3, T3, ALU.add, B_Y + [B_TMPF], B_Y)
            tt(Y3, Y3, ppv('bdw').unsqueeze(2).broadcast_to([128, KC, NS]), ALU.add, B_Y + [B_PP], B_Y)
            act(BA[bi][:, :, 0:N], Y3, AF.Square, B_Y, B_BA[bi])
            cp(BB[:, :, 0:N], Y3, B_Y, B_BB)
            S.dma('sp', zs_d[:, :, :], ZS, reads=B_ZSb)

        def stD(T, bi):
            N = T.N
            ln_stats(None, None, BB, _Multi(B_BB), BA[bi], _Multi(B_BA[bi]), KC, N, MEAN, B_MEAN,
                     RSTD_D, B_RSTD_D, MSQ, B_MSQ, D)
            for j in range(KC):
                tt(Y[:, j, 0:N], Y[:, j, 0:N], MEAN[:, 0:N], ALU.subtract, [B_Y[j], B_MEAN], [B_Y[j]])
                tt(Y[:, j, 0:N], Y[:, j, 0:N], RSTD_D[:, 0:N], ALU.mult, [B_Y[j], B_RSTD_D], [B_Y[j]])
                act(BB[:, j, 0:N], Y[:, j, 0:N], AF.Silu, [B_Y[j], B_PP], [B_BB[j]],
                    bias=ppc('lnb', j), scale=ppc('lng', j))

        def stE(T):
            N = T.N
            for m in range(KC):
                b = ps_next()
                v, o, wb = w2(m)
                mm_group(PS[:, b, 0:N], [(v[:, kc, o:o + 128], BB[:, kc, 0:N]) for kc in range(KC)],
                         [wb] + B_BB, [B_PS[b]])
                x_update(T, l, 2, m, PS[:, b, 0:N], B_PS[b], TT_, B_TT, True)

        samp = [T for T in tiles if T.sample][0]
        pt = [T for T in tiles if not T.sample]
        AR.set_wish([('ada', 0, 0, 0), ('ada', 0, 0, 1), ('ada', 0, 1, 0), ('ada', 0, 1, 1)]
                    + [('pw1', h) for h in range(4)])
        for j in (0, 1):
            for h in (0, 1):
                ada_half(0, j, h)
        S.dma('sp', X[:, :, 0:NT], xp_d[:, :, 0:NT], reads=[AR.get(('pw1', 1))[1]], writes=B_X[0])
        AR.set_wish([('pw2', 0), ('pw2', 1), ('ada', 0, 2, 0), ('ada', 0, 2, 1)])
        S.dma('sp', CTXF, scT_d[:, :, :, :], writes=[B_CTXF])
        for t in range(1, NTI):
            S.dma('sp', X[:, :, t * NT:(t + 1) * NT], xp_d[:, :, t * NT:(t + 1) * NT],
                  reads=[AR.get(('pw2', 1))[1]], writes=B_X[t])
        stA(samp, 0)
        stB(samp, 0)
        stA(pt[0], 1)
        stCsample(samp, 0)
        adas = None
        memset(ZT[:, :, 0:CTX], 0.0, [B_ZTh])
        for i, T in enumerate(pt):
            bi = (i + 1) % 2
            stB(T, bi)
            if i == len(pt) - 1:
                for h in range(4):
                    AR.release(('pw1', h))
            build_dg(0)
            if i == 0:
                stD(samp, 0)
                ada_half(0, 2, 0)
                ada_half(0, 2, 1)
                stE(samp)
                adas = AdaStream([(0, j, h) for j in (3, 4, 5) for h in (0, 1)])
            adas.tick()
            if i > 0:
                stE(pt[i - 1])
            if i in (1, 2):
                adas.tick()
            gen = None
            if i + 1 < len(pt):
                nb_ = 1 - bi
                gen = norm_mod_gen(pt[i + 1], l, 1, BA[nb_], _Multi(B_BA[nb_]), RSTD_A, B_RSTD_A, TMPF, B_TMPF,
                                   (lambda c, nb_=nb_: BA[nb_][:, c, 0:NT]), B_BA[nb_], MEAN, B_MEAN)
            stC(T, bi, gen)
            if i + 1 < len(pt):
                cp(ZT[:, :, 0:CTX], ZT[:, :, NT:NT + CTX], B_ZT, [B_ZTh])
            stD(T, bi)
        adas.drain()

        S.dma('sp', cssc_d[:, :, :], sc_d[:, 1:CTX, :])

        def tail():
            stE(pt[-1])
            S.dma('sp', csp_d[:, :, :], ZL[:, :, :], reads=[B_ZL])
            for h in range(2):
                AR.release(('pw2', h))
        return tail

    _Multi = list


    def _flat(bs):
        out = []
        for b in bs:
            if isinstance(b, (list, tuple)):
                out.extend(_flat(b))
            else:
                out.append(b)
        return out
    _orig_deps = S._deps
    _orig_upd = S._upd
    S._deps = lambda reads, writes: _orig_deps(_flat(reads), _flat(writes))
    S._upd = lambda reads, writes, tok: _orig_upd(_flat(reads), _flat(writes), tok)

    FFN_EARLY_BYTES = ((SEQ + NS) * KC * 2 + KC * NT * 2 + 2 * NT * 4 + 2 * NT * 4 + 2 * 4 * NT * 2 + NT * 4)

    def ffn_phase(l, prev_tail=None, mid_hook=None):
        seed = phase_begin()
        NTOT = SEQ + NS
        HF = talloc([128, KC, NTOT], BF16, seed)
        SQ = talloc([128, KC, NT], BF16, seed)
        RSTD = talloc([128, NT], F32, seed)
        TMPF = talloc([128, NT], F32, seed)
        SGT = [talloc([128, NT], F32, seed) for _ in range(2)]
        HID = [talloc([128, 4, NT], BF16, seed) for _ in range(2)]
        TT_ = talloc([128, NT], F32, seed)
        assert tstate['off'] == FFN_EARLY_BYTES, (tstate['off'], FFN_EARLY_BYTES)
        B_HF = [[Buf("HF%d_%d" % (t, c), seed) for c in range(KC)] for t in range(NTI + 1)]
        B_SQ = Buf("SQ", seed)
        B_RSTD = Buf("RSTD", seed)
        B_TMPF = Buf("TMPF", seed)
        B_SGT = [Buf("SGT%d" % i, seed) for i in range(2)]
        B_HID = [[Buf("HID%d_%d" % (i, hc), seed) for hc in range(4)] for i in range(2)]
        B_TT = Buf("TT", seed)

        def hcol(T):
            return SEQ if T.sample else T.t0

        units = FUNITS

        def load_unit(u, must):
            ok = AR.load(('g', l, u), must)
            ok = ok and AR.load(('u', l, u), must)
            ok = ok and AR.load(('d', l, u), must)
            return ok

        load_unit(0, True)
        adas = AdaStream([(1, j, h) for j in (0, 1, 2) for h in (0, 1)] if l == 0 else [])
        YF = talloc([128, KC, NT], F32, seed) if l == 1 else None
        B_YF = Buf("YF", seed)

        def final_tile(T):
            N = T.N
            rms_stats(T, SQ, B_SQ, RSTD, B_RSTD)
            for c in range(KC):
                stt(YF[:, c, 0:N], T.xc[c], ppc('fng', c), RSTD[:, 0:N], ALU.mult, ALU.mult,
                    [T.xb[c], B_PP, B_RSTD], [B_YF])
            if not T.sample:
                S.dma('sp', yp_d[:, :, T.t0:T.t0 + N], YF[:, :, 0:N], reads=[B_YF])
            else:
                S.dma('sp', ys_d[:, :, :], YF[:, :, 0:N], reads=[B_YF])

        def do_norm(T):
            h0 = hcol(T)
            norm_mod(T, l, 2, SQ, B_SQ, RSTD, B_RSTD, TMPF, B_TMPF,
                     (lambda c, h0=h0, N=T.N: HF[:, c, h0:h0 + N]), B_HF[T.idx], TT_, B_TT)

        do_norm(tiles[0])
        do_norm(tiles[1])
        if prev_tail is not None:
            prev_tail()
        it = 0
        for u, (f0, n) in enumerate(units):
            if mid_hook is not None and 1 <= u <= 3:
                mid_hook(u - 1)
            load_unit(u, True)
            if u + 1 < len(units):
                load_unit(u + 1, False)
            wg, bg = AR.get(('g', l, u))
            wu, bu = AR.get(('u', l, u))
            wd, bd = AR.get(('d', l, u))
            last_unit = (l == 1 and u == len(units) - 1)
            tord = tiles if not last_unit else ([T_ for T_ in tiles if not T_.sample] + [T_ for T_ in tiles if T_.sample])
            for tix, T in enumerate(tord):
                if u == 0 and tix >= 1 and tix + 1 < len(tiles):
                    do_norm(tiles[tix + 1])
                N = T.N
                h0 = hcol(T)
                hi = it % 2
                it += 1
                for hc in range(n):
                    b1 = ps_next()
                    mm_group(PS[:, b1, 0:N], [(wg[:, kc, hc * 128:(hc + 1) * 128], HF[:, kc, h0:h0 + N])
                                              for kc in range(KC)], [bg] + B_HF[T.idx], [B_PS[b1]])
                    b2 = ps_next()
                    mm_group(PS[:, b2, 0:N], [(wu[:, kc, hc * 128:(hc + 1) * 128], HF[:, kc, h0:h0 + N])
                                              for kc in range(KC)], [bu] + B_HF[T.idx], [B_PS[b2]])
                    si = hc % 2
                    act(SGT[si][:, 0:N], PS[:, b1, 0:N], AF.Silu, [B_PS[b1]], [B_SGT[si]])
                    tt(HID[hi][:, hc, 0:N], SGT[si][:, 0:N], PS[:, b2, 0:N], ALU.mult,
                       [B_SGT[si], B_PS[b2]], [B_HID[hi][hc]])
                for m0 in (0, 4):
                    banks = [ps_next() for _ in range(4)]
                    for mi in range(4):
                        m = m0 + mi
                        b = banks[mi]
                        if n > 1:
                            mm_group(PS[:, b, 0:N], [(wd[:, hc, m * 128:(m + 1) * 128], HID[hi][:, hc, 0:N])
                                                     for hc in range(n - 1)], [bd] + B_HID[hi][0:n - 1], [B_PS[b]],
                                     first=True, last=False)
                    for mi in range(4):
                        m = m0 + mi
                        b = banks[mi]
                        mm_group(PS[:, b, 0:N], [(wd[:, n - 1, m * 128:(m + 1) * 128], HID[hi][:, n - 1, 0:N])],
                                 [bd, B_HID[hi][n - 1]], [B_PS[b]], first=(n == 1), last=True)
                        x_update(T, l, 5, m, PS[:, b, 0:N], B_PS[b], TT_, B_TT, False)
                if l == 1 and u == len(units) - 1:
                    final_tile(T)
            adas.tick()
            for k in ('g', 'u', 'd'):
                AR.release((k, l, u))
        adas.drain()

    GC = {}

    def gmlp_setup(step):
        if step == 0:
            seed = S.all_tokens()
            o = FFN_EARLY_BYTES
            R, o = talloc_at(o, [128, EC, 128], F32)
            BVHL, o = talloc_at(o, [128, E], BF16)
            WT, o = talloc_at(o, [128, 8, 128], BF16)
            TRILB, o = talloc_at(o, [128, 128], BF16)
            assert o <= TEMP_BYTES, o
            GC.update(R=R, BVHL=BVHL, WT=WT, TRILB=TRILB, B_R=Buf("R", seed), B_HL=Buf("HL", seed),
                      B_WT=Buf("WT", seed), B_TRILB=Buf("TRILB", seed))
            S.dma('pool', WT, wsT_d[:, :, :], writes=[GC['B_WT']])
            S.dma('pool', TRILB, tril_d[:, :], writes=[GC['B_TRILB']])
            S.dma('pool', BVHL[0:2, :], rw_d[0:2, 0:E], writes=[GC['B_HL']])
            ROW = R[0:2, :, :].rearrange("p c t -> p (c t)")
            S.dma('sp', ROW, rw_d[0:2, 0:E], writes=[GC['B_R']])
        elif step == 1:
            R, BVHL, WT, TRILB = GC['R'], GC['BVHL'], GC['WT'], GC['TRILB']
            tt(WT, WT, TRILB.unsqueeze(1).broadcast_to([128, 8, 128]), ALU.mult, [GC['B_WT'], GC['B_TRILB']],
               [GC['B_WT']])
            o_n, _ = PP_OFF['negm']
            negm = PP[0:2, o_n:o_n + 1]
            ROW = R[0:2, :, :].rearrange("p c t -> p (c t)")
            stt(BVHL[0:2, :], BVHL[0:2, :], negm, ROW, ALU.mult, ALU.add, [GC['B_HL'], GC['B_R'], B_PP], [GC['B_HL']])
            Rv = R.rearrange("p (g two) t -> p g two t", two=2)
            S.dma('sp', Rv[:, :, 0, :], rw_d[:, 3 * E:3 * E + D].rearrange("p (g t) -> p g t", g=8), writes=[GC['B_R']])
        else:
            R, WT = GC['R'], GC['WT']
            for hb in range(2):
                b = ps_next()
                mm_multi([(PS[:, b, gg * 128:(gg + 1) * 128], [(ONESB[:, :], WT[:, hb * 4 + gg, :])])
                          for gg in range(4)], [GC['B_WT'], B_ID], [B_PS[b]])
                for gg in range(4):
                    g = hb * 4 + gg
                    for c in (2 * g + 1, 2 * g):
                        stt(R[:, c, :], PS[:, b, gg * 128:(gg + 1) * 128], ppc('glnb', c), R[:, 2 * g, :],
                            ALU.mult, ALU.add, [B_PS[b], B_PP, GC['B_R']], [GC['B_R']])

    def gmlp_phase():
        l = 1
        seed = phase_begin()
        H = talloc([128, KC, NT], BF16, seed)
        R, BVHL, WT, B_R, B_HL, B_WT = GC['R'], GC['BVHL'], GC['WT'], GC['B_R'], GC['B_HL'], GC['B_WT']
        V = [talloc([128, E], F32, seed) for _ in range(2)]
        VNB = talloc([128, 4, E], BF16, seed)
        UM = talloc([128, EC, NT], BF16, seed)
        ST = [talloc([128, 16], F32, seed) for _ in range(2)]
        assert tstate['off'] <= FFN_EARLY_BYTES, tstate['off']
        B_H = [Buf("H%d" % c, seed) for c in range(KC)]
        B_V = [Buf("V%d" % i, seed) for i in range(2)]
        B_VNB = [Buf("VNB%d" % q, seed) for q in range(4)]
        B_UM = [Buf("UM%d" % c, seed) for c in range(EC)]
        B_ST = [Buf("ST%d" % i, seed) for i in range(2)]
        Uf = [V[0][:, i * NT:(i + 1) * NT] for i in range(2)]
        MX = [V[0][:, (2 + i) * NT:(3 + i) * NT] for i in range(2)]
        TMPF = V[1][:, 0:NT]
        RSTD = V[1][:, NT:2 * NT]
        TTs = [V[1][:, (2 + i) * NT:(3 + i) * NT] for i in range(2)]
        B_Uf = [Buf("Uf%d" % i, seed) for i in range(2)]
        B_MX = [Buf("MX%d" % i, seed) for i in range(2)]
        B_TMPF = Buf("TMPFg", seed)
        B_RSTD = Buf("RSTDg", seed)
        B_TT = [Buf("TTg%d" % i, seed) for i in range(2)]
        ALIAS = [B_Uf + B_MX, [B_TMPF, B_RSTD] + B_TT]

        def vbufs(i):
            return [B_V[i]] + ALIAS[i]

        for h in range(4):
            AR.load(('wv', h))

        def wvk(kc):
            v, b = AR.get(('wv', kc // 2))
            return v[:, kc % 2, :], b

        wv_bufs = [AR.get(('wv', h))[1] for h in range(4)]

        def load_wu(ti, h, must=True):
            return AR.load(('wu', ti, h), must)

        def load_wo(ti, h, must=True):
            return AR.load(('wo', ti, h), must)

        def do_norm(T):
            norm_mod(T, l, 1, H, _Multi(B_H), RSTD, B_RSTD, TMPF, B_TMPF, (lambda c, N=T.N: H[:, c, 0:N]), B_H,
                     Uf[0], B_Uf[0])

        adas = AdaStream([(1, j, h) for j in (3, 4, 5) for h in (0, 1)])
        samp = [T for T in tiles if T.sample][0]
        pt = [T for T in tiles if not T.sample]
        HS = talloc([128, KC, NS], BF16, seed)
        UMS = talloc([128, EC, NS], BF16, seed)
        VBS = talloc([128, EC, NS], BF16, seed)
        B_VBS = Buf("VBS", seed)
        B_HS = [Buf("HS%d" % c, seed) for c in range(KC)]
        B_UMS = [Buf("UMS%d" % c, seed) for c in range(EC)]
        assert tstate['off'] <= FFN_EARLY_BYTES, tstate['off']

        def sample_norm():
            norm_mod(samp, l, 1, HS, _Multi(B_HS), RSTD, B_RSTD, TMPF, B_TMPF, (lambda c: HS[:, c, 0:NS]), B_HS)

        def sample_v():
            N = NS
            VF = UM_F32
            for c in range(EC):
                b = ps_next()
                pairs = []
                for kc in range(KC):
                    wv_, wb_ = wvk(kc)
                    pairs.append((wv_[:, c * 128:(c + 1) * 128], HS[:, kc, 0:N]))
                mm_group(PS[:, b, 0:N], pairs, wv_bufs + B_HS, [B_PS[b]])
                act(VF[:, c, :], PS[:, b, 0:N], AF.Gelu_apprx_tanh, [B_PS[b], B_PP], [B_VF], bias=ppc('binv', c))
            VSQ = UMS[:, :, 0:NS]
            VB = VBS
            act(VSQ, VF, AF.Square, [B_VF], B_UMS)
            cp(VB, VF, [B_VF], [B_VBS])
            MEAN = ST2[:, 0:NS]
            MSQ = ST2[:, NS:2 * NS]
            RS = ST2[:, 2 * NS:3 * NS]
            ln_stats(None, None, VB, B_VBS, VSQ, _Multi(B_UMS), EC, NS, MEAN, B_ST2, RS, B_ST2, MSQ, B_ST2, E)
            tt(VF, VF, MEAN.unsqueeze(1).broadcast_to([128, EC, NS]), ALU.subtract, [B_VF, B_ST2], [B_VF])
            tt(VF, VF, RS.unsqueeze(1).broadcast_to([128, EC, NS]), ALU.mult, [B_VF, B_ST2], [B_VF])
            tt(VF, VF, ppv('glng').unsqueeze(2).broadcast_to([128, EC, NS]), ALU.mult, [B_VF, B_PP], [B_VF])
            tt(VF, VF, ppv('glnb').unsqueeze(2).broadcast_to([128, EC, NS]), ALU.add, [B_VF, B_PP], [B_VF])
            S.dma('sp', gv_d[:, :, :], VF, reads=[B_VF])
            tt(VF, VF, ppv('ws00').unsqueeze(2).broadcast_to([128, EC, NS]), ALU.mult, [B_VF, B_PP], [B_VF])
            tt(VF, VF, ppv('bs00').unsqueeze(2).broadcast_to([128, EC, NS]), ALU.add, [B_VF, B_PP], [B_VF])

        sample_norm()
        do_norm(pt[0])
        uidx = 0
        for _pos, T in enumerate(pt):
            N = T.N
            ti = T.idx
            with_s = (_pos == 0)
            order = [('wu', ti, h) for h in range(4)] + [('wo', ti, h) for h in range(4)]
            if _pos + 1 < len(pt):
                order.append(('wu', pt[_pos + 1].idx, 0))

            def stream_get(key, order=order):
                (load_wu if key[0] == 'wu' else load_wo)(key[1], key[2], True)
                nx = order[order.index(key) + 1] if order.index(key) + 1 < len(order) else None
                if nx is not None:
                    (load_wu if nx[0] == 'wu' else load_wo)(nx[1], nx[2], False)
                return AR.get(key)
            for q in range(4):
                vi = q % 2
                b0 = ps_next4()
                for nb in range(4):
                    pairs = []
                    for kc in range(KC):
                        wv_, wb_ = wvk(kc)
                        pairs.append((H[:, kc, q * 128:(q + 1) * 128], wv_[:, nb * 512:(nb + 1) * 512]))
                    pairs.append((ONESB[0:2, :], BVHL[0:2, nb * 512:(nb + 1) * 512]))
                    mm_group(PS[:, b0 + nb, :], pairs, wv_bufs + B_H + [B_HL, B_ID], [B_PS[b0 + nb]])
                psv = PS[:, b0:b0 + 4, :].rearrange("p a b -> p (a b)")
                st = ST[vi]
                memset(st[:, 0:2], 0.0, [B_ST[vi]])
                act(V[vi], psv, AF.Gelu_apprx_tanh, [B_PS[b0 + i] for i in range(4)], vbufs(vi) + [B_ST[vi]],
                    accum=st[:, 0:1])
                act(VNB[:, q, :], V[vi], AF.Square, [B_V[vi]], [B_VNB[q], B_ST[vi]], accum=st[:, 1:2])
                ts(st[:, 2:3], st[:, 0:1], 1.0 / E, ALU.mult, [B_ST[vi]], [B_ST[vi]])
                tt(st[:, 3:4], st[:, 2:3], st[:, 2:3], ALU.mult, [B_ST[vi]], [B_ST[vi]])
                stt(st[:, 3:4], st[:, 1:2], 1.0 / E, st[:, 3:4], ALU.mult, ALU.subtract, [B_ST[vi]], [B_ST[vi]])
                act(st[:, 4:5], st[:, 3:4], AF.Sqrt, [B_ST[vi]], [B_ST[vi]], bias=EPS, scale=1.0)
                recip(st[:, 4:5], st[:, 4:5], [B_ST[vi]], [B_ST[vi]])
                ts(VNB[:, q, :], V[vi], st[:, 2:3], ALU.subtract, vbufs(vi) + [B_ST[vi]], [B_VNB[q]],
                   s2=st[:, 4:5], op1=ALU.mult)
                if with_s and q == 1:
                    sample_v()
            if T is pt[-1]:
                for h in range(4):
                    AR.release(('wv', h))
            for c in range(EC):
                g = c // 2
                ui = uidx % 2
                uidx += 1
                wu, bu = stream_get(('wu', ti, c // 4)) if c % 4 == 0 else AR.get(('wu', ti, c // 4))

                def issue_u(cc, wu=wu, bu=bu):
                    bb_ = ps_next()
                    mm_group(PS[:, bb_, 0:N], [(wu[:, kc, (cc % 4) * 128:(cc % 4 + 1) * 128], H[:, kc, 0:N])
                                               for kc in range(KC)], [bu] + B_H, [B_PS[bb_]])
                    return bb_
                if c % 4 == 0:
                    ubank = {c: issue_u(c)}
                if c % 4 != 3:
                    ubank[c + 1] = issue_u(c + 1)
                bu_ = ubank[c]
                bm = ps_next()
                mm_multi([(PS[:, bm, q * 128:(q + 1) * 128], [(VNB[:, q, c * 128:(c + 1) * 128], WT[:, g, :])])
                          for q in range(4)], B_VNB + [B_WT], [B_PS[bm]])
                act(Uf[ui], PS[:, bu_, 0:N], AF.Gelu_apprx_tanh, [B_PS[bu_], B_PP], [B_Uf[ui]], bias=ppc('binu', c))
                stt(MX[ui].rearrange("p (q t) -> p q t", q=4), PS[:, bm, :].rearrange("p (q t) -> p q t", q=4),
                    ppc('glng', c), R[:, c, :].unsqueeze(1).broadcast_to([128, 4, 128]), ALU.mult, ALU.add,
                    [B_PS[bm], B_PP, B_R], [B_MX[ui]])
                tt(UM[:, c, 0:N], Uf[ui], MX[ui], ALU.mult, [B_Uf[ui], B_MX[ui]], [B_UM[c]])
                if with_s:
                    ui = uidx % 2
                    uidx += 1
                    b = ps_next()
                    mm_group(PS[:, b, 0:NS], [(wu[:, kc, (c % 4) * 128:(c % 4 + 1) * 128], HS[:, kc, 0:NS])
                                              for kc in range(KC)], [bu] + B_HS, [B_PS[b]])
                    act(Uf[ui][:, 0:NS], PS[:, b, 0:NS], AF.Gelu_apprx_tanh, [B_PS[b], B_PP], [B_Uf[ui]],
                        bias=ppc('binu', c))
                    tt(UMS[:, c, 0:NS], Uf[ui][:, 0:NS], UM_F32[:, c, :], ALU.mult, [B_Uf[ui], B_VF], [B_UMS[c]])
                if c % 4 == 3:
                    AR.release(('wu', ti, c // 4))
            gen = None
            if _pos + 1 < len(pt):
                Tn = pt[_pos + 1]
                gen = norm_mod_gen(Tn, l, 1, H, _Multi(B_H), RSTD, B_RSTD, TMPF, B_TMPF,
                                   (lambda c: H[:, c, 0:NT]), B_H, Uf[0], B_Uf[0])
                next(gen, None)
            adas.tick()
            if _pos == 0:
                adas.tick()
                adas.tick()
            for m in range(KC):
                if gen is not None:
                    next(gen, None)
                    if m in (3, 7):
                        next(gen, None)
                wo, bo = stream_get(('wo', ti, m // 2)) if m % 2 == 0 else AR.get(('wo', ti, m // 2))
                b = ps_next()
                mm_group(PS[:, b, 0:N], [(wo[:, c, (m % 2) * 128:(m % 2 + 1) * 128], UM[:, c, 0:N])
                                         for c in range(EC)], [bo] + B_UM, [B_PS[b]])
                x_update(T, l, 2, m, PS[:, b, 0:N], B_PS[b], TTs[m % 2], B_TT[m % 2], True)
                if with_s:
                    b = ps_next()
                    mm_group(PS[:, b, 0:NS], [(wo[:, c, (m % 2) * 128:(m % 2 + 1) * 128], UMS[:, c, 0:NS])
                                              for c in range(EC)], [bo] + B_UMS, [B_PS[b]])
                    x_update(samp, l, 2, m, PS[:, b, 0:NS], B_PS[b], TTs[(m + 1) % 2], B_TT[(m + 1) % 2], True)
                if m % 2 == 1:
                    AR.release(('wo', ti, m // 2))
            if gen is not None:
                for _ in gen:
                    pass
            if T is pt[-1]:
                adas.drain()

    def final_phase():
        seed = phase_begin()
        SQ = talloc([128, KC, NT], BF16, seed)
        RSTD = talloc([128, NT], F32, seed)
        YF = [talloc([128, KC, NT], F32, seed) for _ in range(2)]
        B_SQ = Buf("SQ", seed)
        B_RSTD = Buf("RSTD", seed)
        B_YF = [Buf("YF%d" % i, seed) for i in range(2)]
        for T in tiles:
            N = T.N
            yi = T.idx % 2
            rms_stats(T, SQ, B_SQ, RSTD, B_RSTD)
            for c in range(KC):
                stt(YF[yi][:, c, 0:N], T.xc[c], ppc('fng', c), RSTD[:, 0:N], ALU.mult, ALU.mult,
                    [T.xb[c], B_PP, B_RSTD], [B_YF[yi]])
            if not T.sample:
                S.dma('sp', yp_d[:, :, T.t0:T.t0 + N], YF[yi][:, :, 0:N], reads=[B_YF[yi]])
            else:
                S.dma('sp', ys_d[:, :, :], YF[yi][:, :, 0:N], reads=[B_YF[yi]])

    conv_tail = conv_phase()
    ffn_phase(0, prev_tail=conv_tail, mid_hook=gmlp_setup)
    gmlp_phase()
    ffn_phase(1)
    S.final_wait('sp')

    semnames = set()
    for e in Sched.ENG:
        for w, fn, inc in S.stream[e]:
            for k, _ in w:
                semnames.add(k)
            if inc:
                semnames.add(inc[0])
    semnames = sorted(semnames)
    import contextlib
    with contextlib.ExitStack() as es:
        SEM = {n: es.enter_context(nc.semaphore(n)) for n in semnames}
        block = es.enter_context(nc.Block())

        def replay(eng_obj, name):
            for w, fn, inc in S.stream[name]:
                for k, v in w:
                    eng_obj.wait_ge(SEM[k], v)
                if fn is not None:
                    ins = fn(eng_obj)
                    ins.then_inc(SEM[inc[0]], inc[1])

        @block.tensor
        def _(pe):
            replay(pe, 'pe')

        @block.scalar
        def _(a):
            replay(a, 'act')

        @block.vector
        def _(v):
            replay(v, 'dve')

        @block.gpsimd
        def _(g):
            replay(g, 'pool')

        @block.sync
        def _(s):
            replay(s, 'sp')
    return nc, AR.order


_NC_CACHE = {}


def _cols(v):
    v = np.asarray(v, dtype=np.float32).reshape(-1, 128)
    return np.ascontiguousarray(v.T)


def kernel(x_prompt, x_sample, c_prompt, c_sample, state_conv, w_ada, b_ada, norm_mix_g, norm_ffn_g,
           final_norm_g, conv_w_pw1, conv_b_pw1, conv_w_dw, conv_b_dw, conv_ln_g, conv_ln_b, conv_w_pw2,
           conv_b_pw2, gmlp_w_in, gmlp_b_in, gmlp_ln_g, gmlp_ln_b, gmlp_w_s, gmlp_b_s, gmlp_w_out,
           gmlp_b_out, ffn_w_gate, ffn_w_up, ffn_w_down):
    f = lambda a: np.ascontiguousarray(np.asarray(a, dtype=np.float32))
    x_prompt, x_sample, c_prompt, c_sample, state_conv = map(f, (x_prompt, x_sample, c_prompt, c_sample, state_conv))
    parts = {
        'nmg0': _cols(norm_mix_g[0]), 'nmg1': _cols(norm_mix_g[1]),
        'nfg0': _cols(norm_ffn_g[0]), 'nfg1': _cols(norm_ffn_g[1]), 'fng': _cols(final_norm_g),
        'bada0': _cols(b_ada[0]), 'bada1': _cols(b_ada[1]),
        'b1': _cols(conv_b_pw1[0]), 'bdw': _cols(conv_b_dw[0]), 'lng': _cols(conv_ln_g[0]),
        'lnb': _cols(conv_ln_b[0]), 'b2': _cols(conv_b_pw2[0]),
        'binu': _cols(np.asarray(gmlp_b_in)[0, :E]), 'binv': _cols(np.asarray(gmlp_b_in)[0, E:]),
        'glng': _cols(gmlp_ln_g[0]), 'glnb': _cols(gmlp_ln_b[0]), 'bout': _cols(gmlp_b_out[0]),
        'wdw': np.ascontiguousarray(np.asarray(conv_w_dw, np.float32)[0].reshape(CW, KC, 128).transpose(2, 1, 0)).reshape(128, KC * CW),
        'ws00': np.ascontiguousarray(np.broadcast_to(np.repeat(np.asarray(gmlp_w_s, np.float32)[0, :, 0, 0], 2)[None, :], (128, EC))),
        'bs00': np.ascontiguousarray(np.broadcast_to(np.repeat(np.asarray(gmlp_b_s, np.float32)[0, :, 0], 2)[None, :], (128, EC))),
        'negm': np.array([0.0, -1.0] + [0.0] * 126, np.float32).reshape(128, 1),
    }
    pp = np.concatenate([parts[n].reshape(128, w) for n, w in PP_LAYOUT], axis=1).astype(np.float32)
    assert pp.shape == (128, NPP)
    gb = np.asarray(gmlp_b_in, np.float32)[0, E:]
    rw = np.concatenate([gb, np.asarray(gmlp_ln_g, np.float32)[0], np.asarray(gmlp_ln_b, np.float32)[0],
                         np.asarray(gmlp_b_s, np.float32)[0].reshape(-1)])
    rw = np.ascontiguousarray(np.broadcast_to(rw[None, :], (128, rw.shape[0]))).astype(np.float32)
    wsT = np.ascontiguousarray(np.asarray(gmlp_w_s, np.float32)[0].transpose(2, 0, 1))
    tril = np.ascontiguousarray(np.triu(np.ones((128, 128), np.float32)))
    ident = np.eye(128, dtype=np.float32)
    shared = {
        'pp': pp, 'rw': rw, 'wsT': wsT, 'tril': tril, 'ident': ident,
        'w_ada': f(w_ada), 'w_pw1': f(conv_w_pw1[0]), 'w_pw2': f(conv_w_pw2[0]),
        'w_in': f(gmlp_w_in[0]), 'w_out': f(gmlp_w_out[0]),
        'w_gate': f(ffn_w_gate), 'w_up': f(ffn_w_up), 'w_down': f(ffn_w_down),
    }
    in_maps = []
    for i in range(NCORES):
        xp = np.ascontiguousarray(x_prompt[i].reshape(SEQ, KC, 128).transpose(2, 1, 0))
        xs = np.ascontiguousarray(x_sample[i * NS:(i + 1) * NS, 0].reshape(NS, KC, 128).transpose(2, 1, 0))
        cc = np.concatenate([c_prompt[i:i + 1], c_sample[i * NS:(i + 1) * NS]], axis=0)
        cin = np.ascontiguousarray(cc.reshape(1 + NS, KC, 128).transpose(2, 1, 0))
        sc = np.ascontiguousarray(state_conv[0, i * NS:(i + 1) * NS])
        scT = np.ascontiguousarray(sc.reshape(NS, CTX, KC, 128).transpose(3, 2, 0, 1))
        m = dict(shared)
        m.update({'xp': xp, 'xs': xs, 'cin': cin, 'sc': sc, 'scT': scT})
        in_maps.append(m)
    if 'nc' not in _NC_CACHE:
        _, _order = build_program()
        _NC_CACHE['nc'] = build_program(_order)[0]
    nc = _NC_CACHE['nc']
    res = run_bass_kernel_spmd(nc, in_maps, core_ids=list(range(NCORES)))
    R = res.results
    y_prompt = np.stack([R[i]['yp'].transpose(2, 1, 0).reshape(SEQ, D) for i in range(NCORES)], 0)
    y_sample = np.concatenate([R[i]['ys'].transpose(2, 1, 0).reshape(NS, 1, D) for i in range(NCORES)], 0)
    csp = np.stack([R[i]['csp'].transpose(2, 1, 0).reshape(CTX, D) for i in range(NCORES)], 0)[None]
    css = np.concatenate([np.concatenate([R[i]['cssc'], R[i]['zs'].transpose(2, 1, 0).reshape(NS, 1, D)], axis=1)
                          for i in range(NCORES)], 0)[None]
    gv = np.concatenate([R[i]['gv'].transpose(2, 1, 0).reshape(NS, 1, E) for i in range(NCORES)], 0)[None]
    return (np.ascontiguousarray(y_prompt, dtype=np.float32), np.ascontiguousarray(y_sample, dtype=np.float32),
            np.ascontiguousarray(csp, dtype=np.float32), np.ascontiguousarray(css, dtype=np.float32),
            np.ascontiguousarray(gv, dtype=np.float32))
```

```python
import numpy as np
import concourse.bass as bass
import concourse.mybir as mybir
from concourse.bass_utils import run_bass_kernel_spmd

F32 = mybir.dt.float32
BF16 = mybir.dt.bfloat16
AF = mybir.ActivationFunctionType
ALU = mybir.AluOpType
AX = mybir.AxisListType

NCORES = 8
D = 1024
KC = 8
SEQ = 2048
NT = 512
NTI = 4
NS = 16
E = 2048
EC = 16
FF = 2816
FCH = 22
CW = 31
CTX = 30
EPS = 1e-6
SLOT = 4096
TEMP_BYTES = 77824

PP_LAYOUT = [('nmg0', 8), ('nmg1', 8), ('nfg0', 8), ('nfg1', 8), ('fng', 8), ('bada0', 48), ('bada1', 48),
             ('b1', 16), ('bdw', 8), ('lng', 8), ('lnb', 8), ('b2', 8), ('binu', 16), ('binv', 16),
             ('glng', 16), ('glnb', 16), ('bout', 8), ('wdw', 248), ('ws00', 16), ('bs00', 16), ('negm', 1)]
PP_OFF = {}
_o = 0
for _n, _w in PP_LAYOUT:
    PP_OFF[_n] = (_o, _w)
    _o += _w
NPP = _o


class Buf:
    __slots__ = ('name', 'w', 'r', 'const')

    def __init__(self, name, seed=None, const=False):
        self.name = name
        self.w = None
        self.r = list(seed) if seed else []
        self.const = const


class Sched:
    ENG = ('pe', 'act', 'dve', 'pool', 'sp')

    def __init__(self):
        self.stream = {e: [] for e in self.ENG}
        self.cnt = {'pe': 0, 'act': 0, 'dve': 0}
        self.waited = {e: {} for e in self.ENG}
        self.dcnt = {}
        self.sp_rr = 0

    def _deps(self, reads, writes):
        d = {}

        def add(t):
            if t is not None and d.get(t[0], 0) < t[1]:
                d[t[0]] = t[1]
        for b in reads:
            add(b.w)
        for b in writes:
            add(b.w)
            for t in b.r:
                add(t)
        return d

    def _waits(self, eng, d):
        out = []
        for k, v in d.items():
            if eng == 'pe' and k == 'P_pe':
                continue
            if self.waited[eng].get(k, 0) >= v:
                continue
            self.waited[eng][k] = v
            out.append((k, v))
        return out

    def _upd(self, reads, writes, tok):
        for b in reads:
            if not b.const:
                b.r.append(tok)
        for b in writes:
            b.w = tok
            b.r = []

    def comp(self, eng, fn, reads=(), writes=()):
        d = self._deps(reads, writes)
        w = self._waits(eng, d)
        self.cnt[eng] += 1
        tok = ('P_' + eng, self.cnt[eng])
        self.stream[eng].append((w, fn, ('P_' + eng, 1)))
        self._upd(reads, writes, tok)
        return tok

    def dma(self, q, out, in_, reads=(), writes=(), sem=None):
        if sem is None:
            sem = ('S%d' if q == 'sp' else 'Q%d') % (self.sp_rr % 8 if q == 'sp' else self.sp_rr % 4)
            self.sp_rr += 1
        d = self._deps(reads, writes)
        if q == 'pool':
            hist = self.__dict__.setdefault('pool_hist', [])
            if len(hist) >= 3:
                t = hist[-3]
                if d.get(t[0], 0) < t[1]:
                    d[t[0]] = t[1]
        c = self.dcnt.get(sem, 0)
        if c:
            d[sem] = max(d.get(sem, 0), c)
        w = self._waits(q, d)
        self.dcnt[sem] = c + 16
        tok = (sem, c + 16)
        self.stream[q].append((w, (lambda e, o=out, i=in_: e.dma_start(out=o, in_=i)), (sem, 16)))
        if q == 'pool':
            self.pool_hist.append(tok)
        self._upd(reads, writes, tok)
        return tok

    def all_tokens(self):
        t = [('P_' + e, c) for e, c in self.cnt.items() if c]
        t += [(s, c) for s, c in self.dcnt.items() if c and not s.startswith('A')]
        return t

    def final_wait(self, q):
        d = {s: c for s, c in self.dcnt.items() if c}
        for e, c in self.cnt.items():
            if c:
                d['P_' + e] = c
        self.stream[q].append((self._waits(q, d), None, None))


def build_program(plan=None):
    nc = bass.Bass("TRN2", target_bir_lowering=False)
    S = Sched()

    def din(name, shape):
        return nc.dram_tensor(name, list(shape), F32, kind="ExternalInput").ap()

    def dout(name, shape):
        return nc.dram_tensor(name, list(shape), F32, kind="ExternalOutput").ap()

    xp_d = din("xp", [128, KC, SEQ])
    xs_d = din("xs", [128, KC, NS])
    cin_d = din("cin", [128, KC, 1 + NS])
    sc_d = din("sc", [NS, CTX, D])
    scT_d = din("scT", [128, KC, NS, CTX])
    pp_d = din("pp", [128, NPP])
    rw_d = din("rw", [128, 3 * E + D])
    wsT_d = din("wsT", [128, 8, 128])
    tril_d = din("tril", [128, 128])
    ident_d = din("ident", [128, 128])
    w_ada_d = din("w_ada", [2, D, 6 * D])
    w_pw1_d = din("w_pw1", [D, 2 * D])
    w_pw2_d = din("w_pw2", [D, D])
    w_in_d = din("w_in", [D, 2 * E])
    w_out_d = din("w_out", [E, D])
    w_gate_d = din("w_gate", [2, D, FF])
    w_up_d = din("w_up", [2, D, FF])
    w_down_d = din("w_down", [2, FF, D])

    yp_d = dout("yp", [128, KC, SEQ])
    ys_d = dout("ys", [128, KC, NS])
    csp_d = dout("csp", [128, KC, CTX])
    cssc_d = dout("cssc", [NS, CTX - 1, D])
    zs_d = dout("zs", [128, KC, NS])
    gv_d = dout("gv", [128, EC, NS])

    def sb(name, shape, dt=F32):
        return nc.alloc_sbuf_tensor(name, list(shape), dt)

    X = sb("X", [128, KC, SEQ])
    XS = sb("XS", [128, KC, NS])
    PP = sb("PP", [128, NPP])
    IDB = sb("IDB", [128, 128], BF16)
    ONESB = sb("ONESB", [128, 128], BF16)
    CS = sb("CS", [128, KC, 1 + NS], BF16)
    _M0 = sb("MOD0", [128, 6, KC, 1 + NS])
    _D0 = sb("DER0", [128, 3, KC, 1 + NS])
    MOD = [_M0, _M0]
    DER = [_D0, _D0]
    ZS = sb("ZS", [128, KC, NS])[:, :, :]
    UM_F32 = sb("VFs", [128, EC, NS])[:, :, :]
    ST2 = sb("ST2", [128, 3 * NS])
    HB1 = sb("HB1", [128, 8])
    ZL = sb("ZL", [128, KC, CTX])
    TEMP = sb("TEMP", [128, TEMP_BYTES // 4])
    nslot = int((nc.sbuf_bytes_remaining - 256) // (SLOT * 2))
    assert nslot >= 7, (nslot, nc.sbuf_bytes_remaining)
    nslot = min(nslot, 7)
    ARENA = sb("ARENA", [128, nslot * SLOT], BF16)
    PS = nc.alloc_psum_tensor("PS", [128, 8, 512], F32)

    def ppc(name, c):
        o, w = PP_OFF[name]
        assert 0 <= c < w
        return PP[:, o + c:o + c + 1]

    def ppv(name):
        o, w = PP_OFF[name]
        return PP[:, o:o + w]

    B_PP = Buf("PP", const=True)
    B_ID = Buf("ID", const=True)
    B_CIN = Buf("CIN")
    B_CS = Buf("CS")
    _bm = [Buf("MOD_%d" % j) for j in range(6)]
    _bd = [Buf("DER_%d" % j) for j in range(3)]
    B_MOD = [_bm, _bm]
    B_DER = [_bd, _bd]
    B_ZSb = [Buf("ZS%d" % c) for c in range(KC)]
    B_VF = Buf("VFs")
    B_ST2 = Buf("ST2")
    B_HB1 = Buf("HB1")
    B_ZL = Buf("ZL")
    B_X = [[Buf("X%d_%d" % (t, c)) for c in range(KC)] for t in range(NTI)]
    B_XS = [Buf("XS%d" % c) for c in range(KC)]
    B_PS = [Buf("PS%d" % b) for b in range(8)]

    ps_state = {'n': 0}

    def ps_next():
        b = ps_state['n'] % 8
        ps_state['n'] += 1
        return b

    ps4 = {'n': 0}

    def ps_next4():
        g = ps4['n'] % 2
        ps4['n'] += 1
        return g * 4

    def wrows(w2d, r0, nk, c0, ncol):
        return w2d[r0 * 128:(r0 + nk) * 128, c0:c0 + ncol].rearrange("(k p) m -> p k m", p=128)

    FUNITS = []
    _f = 0
    for _n in (4, 4, 4, 2, 4, 4):
        FUNITS.append((_f, _n))
        _f += _n
    assert _f == FCH

    def piece_spec(key):
        k = key[0]
        if k == 'ada':
            _, l, j, h = key
            return wrows(w_ada_d[l], 0, KC, j * D + h * 512, 512), KC, 512
        if k == 'pw1':
            return wrows(w_pw1_d, 0, KC, key[1] * 512, 512), KC, 512
        if k == 'pw2':
            return wrows(w_pw2_d, 0, KC, key[1] * 512, 512), KC, 512
        if k in ('g', 'u'):
            _, l, u = key
            f0, n = FUNITS[u]
            return wrows((w_gate_d if k == 'g' else w_up_d)[l], 0, KC, f0 * 128, n * 128), KC, n * 128
        if k == 'd':
            _, l, u = key
            f0, n = FUNITS[u]
            return wrows(w_down_d[l], f0, n, 0, D), n, D
        if k == 'wv':
            return wrows(w_in_d, 2 * key[1], 2, E, E), 2, E
        if k == 'wu':
            return wrows(w_in_d, 0, KC, key[2] * 512, 512), KC, 512
        if k == 'wo':
            return wrows(w_out_d, 0, EC, key[2] * 256, 256), EC, 256
        raise KeyError(key)

    SCR = {}

    class Arena:
        def __init__(self):
            self.free = list(range(nslot))
            self.bufs = [Buf("slot%d" % k) for k in range(nslot)]
            self.loaded = {}
            self.order = []
            self.plan = plan
            self.pi = 0
            self.wish = []

        def set_wish(self, keys):
            self.wish = list(keys)
            self._try_wish()

        def _try_wish(self):
            if self.plan is not None:
                self.pump()
                return
            while self.wish and self.free:
                k = self.wish.pop(0)
                if k not in self.loaded:
                    self._issue(k)

        def _issue(self, key):
            src, a, b = piece_spec(key)
            k = self.free.pop(0)
            v = ARENA[:, k * SLOT:k * SLOT + a * b].rearrange("p (a b) -> p a b", a=a)
            rd = []
            scr = None
            if key[0] in ('wu', 'wo'):
                skey = (key[0], key[2])
                if skey not in SCR:
                    SCR[skey] = (nc.dram_tensor("scr_%s%d" % skey, [128, a, b], BF16, kind="Internal").ap(),
                                 Buf("scr_%s%d" % skey))
                    scr = SCR[skey]
                else:
                    src = SCR[skey][0][:, :, :]
                    rd = [SCR[skey][1]]
            S.dma('pool', v, src, reads=rd, writes=[self.bufs[k]], sem='A%d' % k)
            if scr is not None:
                S.dma('sp', scr[0][:, :, :], v, reads=[self.bufs[k]], writes=[scr[1]], sem='AW%d' % (len(SCR) % 4))
            self.loaded[key] = (k, v)
            self.order.append(key)

        def pump(self):
            while self.free and self.pi < len(self.plan):
                key = self.plan[self.pi]
                self.pi += 1
                self._issue(key)

        def load(self, key, must=True):
            if key in self.loaded:
                return True
            if self.plan is not None:
                self.pump()
                assert key in self.loaded or not must, "arena full for %s" % (key,)
                return key in self.loaded
            if not self.free:
                assert not must, "arena full for %s" % (key,)
                return False
            self._issue(key)
            return True

        def get(self, key):
            self.load(key)
            k, v = self.loaded[key]
            return v, self.bufs[k]

        def release(self, key):
            k, _ = self.loaded.pop(key)
            self.free.append(k)
            self._try_wish()

    AR = Arena()

    tstate = {'off': 0}

    def phase_begin():
        tstate['off'] = 0
        return S.all_tokens()

    def talloc(shape, dt, seed):
        n = 1
        for s_ in shape[1:]:
            n *= s_
        nbytes = n * (4 if dt == F32 else 2)
        nbytes = (nbytes + 31) // 32 * 32
        o = tstate['off']
        assert o + nbytes <= TEMP_BYTES, ("TEMP overflow", o, nbytes)
        tstate['off'] = o + nbytes
        v = TEMP[:, o // 4:(o + nbytes) // 4]
        if dt != F32:
            v = v.bitcast(dt)
        v = v[:, 0:n]
        if len(shape) == 3:
            v = v.rearrange("p (a b) -> p a b", a=shape[1])
        elif len(shape) == 4:
            v = v.rearrange("p (a b c) -> p a b c", a=shape[1], b=shape[2])
        return v

    def talloc_at(off, shape, dt):
        save = tstate['off']
        tstate['off'] = off
        v = talloc(shape, dt, None)
        end = tstate['off']
        tstate['off'] = save
        return v, end

    def mm_group(out, pairs, reads, writes, first=True, last=True):
        n = len(pairs)

        def fn(pe, out=out, pairs=pairs, n=n, first=first, last=last):
            ins = None
            for i, (l, r) in enumerate(pairs):
                ins = pe.matmul(out, lhsT=l, rhs=r, start=(first and i == 0), stop=(last and i == n - 1))
            return ins
        return S.comp('pe', fn, reads, writes)

    def mm_multi(groups, reads, writes):
        def fn(pe, groups=groups):
            ins = None
            for (o, prs) in groups:
                for i, (l, r) in enumerate(prs):
                    ins = pe.matmul(o, lhsT=l, rhs=r, start=(i == 0), stop=(i == len(prs) - 1))
            return ins
        return S.comp('pe', fn, reads, writes)

    def act(out, in_, func, reads, writes, bias=None, scale=None, accum=None):
        kw = {}
        if bias is not None:
            kw['bias'] = bias
        if scale is not None:
            kw['scale'] = scale
        if accum is not None:
            kw['accum_out'] = accum
        return S.comp('act', lambda a, o=out, i=in_, f=func, kw=kw: a.activation(out=o, in_=i, func=f, **kw),
                      reads, writes)

    def tt(out, in0, in1, op, reads, writes):
        return S.comp('dve', lambda v, o=out, a=in0, b=in1, op=op: v.tensor_tensor(out=o, in0=a, in1=b, op=op),
                      reads, writes)

    def ts(out, in0, s1, op0, reads, writes, s2=None, op1=None):
        if s2 is None:
            return S.comp('dve', lambda v, o=out, a=in0, s1=s1, op0=op0:
                          v.tensor_scalar(out=o, in0=a, scalar1=s1, scalar2=None, op0=op0), reads, writes)
        return S.comp('dve', lambda v, o=out, a=in0, s1=s1, s2=s2, op0=op0, op1=op1:
                      v.tensor_scalar(out=o, in0=a, scalar1=s1, scalar2=s2, op0=op0, op1=op1), reads, writes)

    def stt(out, in0, sc, in1, op0, op1, reads, writes):
        return S.comp('dve', lambda v, o=out, a=in0, s=sc, b=in1, op0=op0, op1=op1:
                      v.scalar_tensor_tensor(out=o, in0=a, scalar=s, in1=b, op0=op0, op1=op1), reads, writes)

    def cp(out, in_, reads, writes):
        return S.comp('dve', lambda v, o=out, i=in_: v.tensor_copy(out=o, in_=i), reads, writes)

    def recip(out, in_, reads, writes):
        return S.comp('dve', lambda v, o=out, i=in_: v.reciprocal(out=o, in_=i), reads, writes)

    def memset(ap, val, writes):
        return S.comp('dve', lambda v, a=ap, c=val: v.memset(a, c), (), writes)

    class Tile:
        pass

    tiles = []
    for t in range(NTI):
        T = Tile()
        T.N = NT
        T.sample = False
        T.idx = t
        T.t0 = t * NT
        T.xb = B_X[t]
        T.x3 = X[:, :, t * NT:(t + 1) * NT]
        T.xc = [X[:, c, t * NT:(t + 1) * NT] for c in range(KC)]
        tiles.append(T)
    TS_ = Tile()
    TS_.N = NS
    TS_.sample = True
    TS_.idx = NTI
    TS_.t0 = 0
    TS_.xb = B_XS
    TS_.x3 = XS[:, :, :]
    TS_.xc = [XS[:, c, :] for c in range(KC)]
    tiles.insert(0, TS_)

    def modcol(l, j, c):
        return MOD[l][:, j, c, 0:1]

    def mods(l, j):
        return MOD[l][:, j, :, 1:1 + NS]

    def dercol(l, j, c):
        return DER[l][:, j, c, 0:1]

    def ders(l, j):
        return DER[l][:, j, :, 1:1 + NS]

    seed = phase_begin()
    IDF = talloc([128, 128], F32, seed)
    CIN = talloc([128, KC, 1 + NS], F32, seed)
    B_IDF = Buf("IDF", seed)
    S.dma('sp', PP[:, :], pp_d[:, :], writes=[B_PP])
    S.dma('sp', CIN, cin_d[:, :, :], writes=[B_CIN])
    S.dma('sp', IDF, ident_d[:, :], writes=[B_IDF])
    S.dma('sp', XS[:, :, :], xs_d[:, :, :], writes=B_XS)
    cp(IDB[:, :], IDF, [B_IDF], [B_ID])
    memset(ONESB[:, :], 1.0, [B_ID])
    B_ID.const = True
    o_b1, _ = PP_OFF['b1']
    ts(HB1[:, :], PP[:, o_b1 + 8:o_b1 + 16], 0.5, ALU.mult, [B_PP], [B_HB1])
    B_HB1.const = True
    act(CS[:, :, :], CIN, AF.Silu, [B_CIN], [B_CS])
    B_CS.const = True

    def load_x_tiles():
        for t in range(NTI):
            S.dma('pool', X[:, :, t * NT:(t + 1) * NT], xp_d[:, :, t * NT:(t + 1) * NT], writes=B_X[t])

    def ada_half(l, j, h):
        key = ('ada', l, j, h)
        wv, wb = AR.get(key)
        b = ps_next()
        pso = PS[:, b, 0:4 * (1 + NS)].rearrange("p (m n) -> p m n", m=4)
        for m4 in range(4):
            pairs = [(wv[:, kc, m4 * 128:(m4 + 1) * 128], CS[:, kc, :]) for kc in range(KC)]
            mm_group(pso[:, m4, :], pairs, [wb, B_CS], [B_PS[b]])
        AR.release(key)
        o, _ = PP_OFF['bada%d' % l]
        bcol = PP[:, o + j * 8 + 4 * h:o + j * 8 + 4 * h + 4].unsqueeze(2).broadcast_to([128, 4, 1 + NS])
        tt(MOD[l][:, j, 4 * h:4 * h + 4, :], pso, bcol, ALU.add, [B_PS[b], B_PP], [B_MOD[l][j]])
        if h == 1:
            if j == 1 or j == 4:
                ng = ppv('nmg%d' % l if j == 1 else 'nfg%d' % l).unsqueeze(2).broadcast_to([128, KC, 1 + NS])
                dj = 0 if j == 1 else 2
                stt(DER[l][:, dj, :, :], MOD[l][:, j, :, :], 1.0, ng, ALU.add, ALU.mult,
                    [B_MOD[l][j], B_PP], [B_DER[l][dj]])
            if j == 2:
                bo = ppv('b2' if l == 0 else 'bout').unsqueeze(2).broadcast_to([128, KC, 1 + NS])
                tt(DER[l][:, 1, :, :], MOD[l][:, 2, :, :], bo, ALU.mult, [B_MOD[l][2], B_PP], [B_DER[l][1]])

    class AdaStream:
        def __init__(self, items):
            self.items = [('ada',) + tuple(it) for it in items]
            self.i = 0
            if self.items:
                AR.set_wish([self.items[0]])

        def tick(self):
            if self.i < len(self.items):
                _, l, j, h = self.items[self.i]
                ada_half(l, j, h)
                self.i += 1
                if self.i < len(self.items):
                    AR.set_wish([self.items[self.i]])

        def drain(self):
            while self.i < len(self.items):
                self.tick()

    def rms_stats(T, SQ, B_SQ, RSTD, B_RSTD):
        N = T.N
        act(SQ[:, :, 0:N], T.x3, AF.Square, T.xb, [B_SQ])
        b = ps_next()
        mm_group(PS[:, b, 0:N], [(ONESB[:, :], SQ[:, c, 0:N]) for c in range(KC)], [B_SQ, B_ID], [B_PS[b]])
        act(RSTD[:, 0:N], PS[:, b, 0:N], AF.Sqrt, [B_PS[b]], [B_RSTD], bias=EPS, scale=1.0 / D)
        recip(RSTD[:, 0:N], RSTD[:, 0:N], [B_RSTD], [B_RSTD])

    def norm_mod(T, l, which, SQ, B_SQ, RSTD, B_RSTD, TMPF, B_TMPF, Hc, B_H, TMPF2=None, B_TMPF2=None):
        N = T.N
        rms_stats(T, SQ, B_SQ, RSTD, B_RSTD)
        jsh = 0 if which == 1 else 3
        dj = 0 if which == 1 else 2
        if not T.sample:
            for c in range(KC):
                tp, tb = (TMPF, B_TMPF) if (c % 2 == 0 or TMPF2 is None) else (TMPF2, B_TMPF2)
                stt(tp[:, 0:N], T.xc[c], dercol(l, dj, c), RSTD[:, 0:N], ALU.mult, ALU.mult,
                    [T.xb[c], B_DER[l][dj], B_RSTD], [tb])
                act(Hc(c), tp[:, 0:N], AF.Identity, [tb, B_MOD[l][jsh]], [B_H[c]], bias=modcol(l, jsh, c))
        else:
            T3 = TMPF[:, 0:KC * N].rearrange("p (c n) -> p c n", c=KC)
            tt(T3, T.x3, RSTD[:, 0:N].unsqueeze(1).broadcast_to([128, KC, N]), ALU.mult,
               list(T.xb) + [B_RSTD], [B_TMPF])
            tt(T3, T3, ders(l, dj), ALU.mult, [B_TMPF, B_DER[l][dj]], [B_TMPF])
            for c in range(KC):
                tt(Hc(c), T3[:, c, :], MOD[l][:, jsh, c, 1:1 + NS], ALU.add, [B_TMPF, B_MOD[l][jsh]], [B_H[c]])

    def norm_mod_gen(T, l, which, SQ, B_SQ, RSTD, B_RSTD, TMPF, B_TMPF, Hc, B_H, TMPF2=None, B_TMPF2=None):
        N = T.N
        assert not T.sample
        act(SQ[:, :, 0:N], T.x3, AF.Square, T.xb, [B_SQ])
        b = ps_next()
        mm_group(PS[:, b, 0:N], [(ONESB[:, :], SQ[:, c, 0:N]) for c in range(KC)], [B_SQ, B_ID], [B_PS[b]])
        yield
        act(RSTD[:, 0:N], PS[:, b, 0:N], AF.Sqrt, [B_PS[b]], [B_RSTD], bias=EPS, scale=1.0 / D)
        recip(RSTD[:, 0:N], RSTD[:, 0:N], [B_RSTD], [B_RSTD])
        yield
        jsh = 0 if which == 1 else 3
        dj = 0 if which == 1 else 2
        for c in range(KC):
            tp, tb = (TMPF, B_TMPF) if (c % 2 == 0 or TMPF2 is None) else (TMPF2, B_TMPF2)
            stt(tp[:, 0:N], T.xc[c], dercol(l, dj, c), RSTD[:, 0:N], ALU.mult, ALU.mult,
                [T.xb[c], B_DER[l][dj], B_RSTD], [tb])
            act(Hc(c), tp[:, 0:N], AF.Identity, [tb, B_MOD[l][jsh]], [B_H[c]], bias=modcol(l, jsh, c))
            yield

    def x_update(T, l, gj, m, psv, B_psv, TT_, B_TT, has_bias):
        N = T.N
        if not T.sample:
            if has_bias:
                act(TT_[:, 0:N], psv, AF.Identity, [B_psv, B_MOD[l][gj], B_DER[l][1]], [B_TT],
                    bias=dercol(l, 1, m), scale=modcol(l, gj, m))
                tt(T.xc[m], T.xc[m], TT_[:, 0:N], ALU.add, [B_TT, T.xb[m]], [T.xb[m]])
            else:
                stt(T.xc[m], psv, modcol(l, gj, m), T.xc[m], ALU.mult, ALU.add,
                    [B_psv, B_MOD[l][gj], T.xb[m]], [T.xb[m]])
        else:
            tt(TT_[:, 0:N], psv, MOD[l][:, gj, m, 1:1 + NS], ALU.mult, [B_psv, B_MOD[l][gj]], [B_TT])
            if has_bias:
                tt(TT_[:, 0:N], TT_[:, 0:N], DER[l][:, 1, m, 1:1 + NS], ALU.add, [B_TT, B_DER[l][1]], [B_TT])
            tt(T.xc[m], T.xc[m], TT_[:, 0:N], ALU.add, [B_TT, T.xb[m]], [T.xb[m]])

    def ln_stats(Yc, B_Y, YB, B_YB, YSQ, B_YSQ, nch, N, MEAN, B_MEAN, RS, B_RS, MSQ, B_MSQ, dim):
        b1 = ps_next()
        mm_group(PS[:, b1, 0:N], [(ONESB[:, :], YB[:, c, 0:N]) for c in range(nch)], [B_YB, B_ID], [B_PS[b1]])
        b2 = ps_next()
        mm_group(PS[:, b2, 0:N], [(ONESB[:, :], YSQ[:, c, 0:N]) for c in range(nch)], [B_YSQ, B_ID], [B_PS[b2]])
        ts(MEAN[:, 0:N], PS[:, b1, 0:N], 1.0 / dim, ALU.mult, [B_PS[b1]], [B_MEAN])
        tt(MSQ[:, 0:N], MEAN[:, 0:N], MEAN[:, 0:N], ALU.mult, [B_MEAN], [B_MSQ])
        stt(MSQ[:, 0:N], PS[:, b2, 0:N], 1.0 / dim, MSQ[:, 0:N], ALU.mult, ALU.subtract, [B_PS[b2], B_MSQ], [B_MSQ])
        act(RS[:, 0:N], MSQ[:, 0:N], AF.Sqrt, [B_MSQ], [B_RS], bias=EPS, scale=1.0)
        recip(RS[:, 0:N], RS[:, 0:N], [B_RS], [B_RS])

    def conv_phase():
        l = 0
        seed = phase_begin()
        BA = [talloc([128, KC, NT], BF16, seed) for _ in range(2)]
        SG = talloc([128, NT], F32, seed)
        ZT = talloc([128, KC, CTX + NT], BF16, seed)
        DGALL = talloc([128, 2 * CW * 128], BF16, seed)
        DG = [DGALL[:, i * CW * 128:(i + 1) * CW * 128].rearrange("p (k m) -> p k m", k=CW) for i in range(2)]
        CTXF = DGALL.bitcast(F32)[:, 0:KC * NS * CTX].rearrange("p (c s k) -> p c s k", c=KC, s=NS)
        Y = talloc([128, KC, NT], F32, seed)
        RSTD_A = talloc([128, NT], F32, seed)
        TMPF = talloc([128, NT], F32, seed)
        MEAN = talloc([128, NT], F32, seed)
        RSTD_D = talloc([128, NT], F32, seed)
        MSQ = TMPF
        assert tstate['off'] >= FFN_EARLY_BYTES, tstate['off']
        BB = talloc([128, KC, NT], BF16, seed)
        TT_ = talloc([128, NT], F32, seed)
        B_BA = [[Buf("BA%d_%d" % (i, c), seed) for c in range(KC)] for i in range(2)]
        B_BB = [Buf("BB%d" % c, seed) for c in range(KC)]
        B_SG = Buf("SG", seed)
        B_ZT = [Buf("ZT_%d" % c, seed) for c in range(KC)]
        B_ZTh = Buf("ZTh", seed)
        B_DG = [Buf("DG%d" % i, seed) for i in range(2)]
        B_Y = [Buf("Y%d" % c, seed) for c in range(KC)]
        B_RSTD_A = Buf("RSTD_A", seed)
        B_RSTD_D = Buf("RSTD_D", seed)
        B_MEAN = Buf("MEAN", seed)
        B_TMPF = Buf("TMPF", seed)
        B_MSQ = B_TMPF
        B_TT = Buf("TT", seed)
        B_CTXF = _Multi(B_DG)
        o_w, _ = PP_OFF['wdw']
        WDW = PP[:, o_w:o_w + 248].rearrange("p (c k) -> p c k", c=KC)

        def w1(m):
            v, b = AR.get(('pw1', m // 4))
            return v, (m % 4) * 128, b

        def w2(m):
            v, b = AR.get(('pw2', m // 4))
            return v, (m % 4) * 128, b

        def stA(T, bi):
            norm_mod(T, l, 1, BA[bi], _Multi(B_BA[bi]), RSTD_A, B_RSTD_A, TMPF, B_TMPF,
                     (lambda c, N=T.N, bi=bi: BA[bi][:, c, 0:N]), B_BA[bi], MEAN, B_MEAN)

        def stB(T, bi):
            N = T.N
            for j in range(KC):
                ba = ps_next()
                v, o, wb = w1(j)
                mm_group(PS[:, ba, 0:N], [(v[:, kc, o:o + 128], BA[bi][:, kc, 0:N]) for kc in range(KC)],
                         [wb] + B_BA[bi], [B_PS[ba]])
                bb = ps_next()
                v, o, wb = w1(j + 8)
                mm_group(PS[:, bb, 0:N], [(v[:, kc, o:o + 128], BA[bi][:, kc, 0:N]) for kc in range(KC)],
                         [wb] + B_BA[bi], [B_PS[bb]])
                act(SG[:, 0:N], PS[:, bb, 0:N], AF.Tanh, [B_PS[bb], B_HB1], [B_SG], bias=HB1[:, j:j + 1], scale=0.5)
                ts(SG[:, 0:N], SG[:, 0:N], 0.5, ALU.mult, [B_SG], [B_SG], s2=0.5, op1=ALU.add)
                if not T.sample:
                    stt(ZT[:, j, CTX:CTX + N], PS[:, ba, 0:N], ppc('b1', j), SG[:, 0:N], ALU.add, ALU.mult,
                        [B_PS[ba], B_SG, B_PP], [B_ZT[j]])
                    if T.idx == NTI - 1:
                        stt(ZL[:, j, :], PS[:, ba, N - CTX:N], ppc('b1', j), SG[:, N - CTX:N], ALU.add, ALU.mult,
                            [B_PS[ba], B_SG, B_PP], [B_ZL])
                else:
                    stt(ZS[:, j, :], PS[:, ba, 0:N], ppc('b1', j), SG[:, 0:N], ALU.add, ALU.mult,
                        [B_PS[ba], B_SG, B_PP], [B_ZSb[j]])

        def build_dg(j):
            dgi = j % 2
            tt(DG[dgi][:, :, :], IDB[:, :].unsqueeze(1).broadcast_to([128, CW, 128]),
               WDW[:, j, :].unsqueeze(2).broadcast_to([128, CW, 128]), ALU.mult,
               [B_ID, B_PP], [B_DG[dgi]])

        def stC(T, bi, gen=None):
            N = T.N
            nsteps = [1, 1, 1, 1, 2, 1, 2, 1]
            for j in range(KC):
                dgi = j % 2
                if j + 1 < KC:
                    build_dg(j + 1)
                b = ps_next()
                mm_group(PS[:, b, 0:N], [(DG[dgi][:, k, :], ZT[:, j, k:k + N]) for k in range(CW)],
                         [B_DG[dgi], B_ZT[j], B_ZTh], [B_PS[b]])
                act(Y[:, j, 0:N], PS[:, b, 0:N], AF.Identity, [B_PS[b], B_PP], [B_Y[j]], bias=ppc('bdw', j))
                act(BA[bi][:, j, 0:N], PS[:, b, 0:N], AF.Square, [B_PS[b], B_PP], [B_BA[bi][j]], bias=ppc('bdw', j))
                cp(BB[:, j, 0:N], Y[:, j, 0:N], [B_Y[j]], [B_BB[j]])
                if gen is not None:
                    for _ in range(nsteps[j]):
                        next(gen, None)
            if gen is not None:
                for _ in gen:
                    pass

        def stCsample(T, bi):
            N = T.N
            tt(CTXF, CTXF, WDW[:, :, 0:CTX].unsqueeze(2).broadcast_to([128, KC, NS, CTX]), ALU.mult,
               [B_CTXF, B_PP], [B_CTXF])
            Y3 = Y[:, :, 0:N]
            S.comp('dve', lambda v_, o=Y3, i=CTXF: v_.tensor_reduce(out=o, in_=i, axis=AX.X, op=ALU.add),
                   [B_CTXF], B_Y)
            T3 = TMPF[:, 0:KC * N].rearrange("p (c n) -> p c n", c=KC)
            tt(T3, ZS, WDW[:, :, CTX:CTX + 1].broadcast_to([128, KC, NS]), ALU.mult, B_ZSb + [B_PP], [B_TMPF])
            tt(Y3, Y3, T3, ALU.add, B_Y + [B_TMPF], B_Y)
            tt(Y3, Y3, ppv('bdw').unsqueeze(2).broadcast_to([128, KC, NS]), ALU.add, B_Y + [B_PP], B_Y)
            act(BA[bi][:, :, 0:N], Y3, AF.Square, B_Y, B_BA[bi])
            cp(BB[:, :, 0:N], Y3, B_Y, B_BB)
            S.dma('sp', zs_d[:, :, :], ZS, reads=B_ZSb)

        def stD(T, bi):
            N = T.N
            ln_stats(None, None, BB, _Multi(B_BB), BA[bi], _Multi(B_BA[bi]), KC, N, MEAN, B_MEAN,
                     RSTD_D, B_RSTD_D, MSQ, B_MSQ, D)
            for j in range(KC):
                tt(Y[:, j, 0:N], Y[:, j, 0:N], MEAN[:, 0:N], ALU.subtract, [B_Y[j], B_MEAN], [B_Y[j]])
                tt(Y[:, j, 0:N], Y[:, j, 0:N], RSTD_D[:, 0:N], ALU.mult, [B_Y[j], B_RSTD_D], [B_Y[j]])
                act(BB[:, j, 0:N], Y[:, j, 0:N], AF.Silu, [B_Y[j], B_PP], [B_BB[j]],
                    bias=ppc('lnb', j), scale=ppc('lng', j))

        def stE(T):
            N = T.N
            for m in range(KC):
                b = ps_next()
                v, o, wb = w2(m)
                mm_group(PS[:, b, 0:N], [(v[:, kc, o:o + 128], BB[:, kc, 0:N]) for kc in range(KC)],
                         [wb] + B_BB, [B_PS[b]])
                x_update(T, l, 2, m, PS[:, b, 0:N], B_PS[b], TT_, B_TT, True)

        samp = [T for T in tiles if T.sample][0]
        pt = [T for T in tiles if not T.sample]
        AR.set_wish([('ada', 0, 0, 0), ('ada', 0, 0, 1), ('ada', 0, 1, 0), ('ada', 0, 1, 1)]
                    + [('pw1', h) for h in range(4)])
        for j in (0, 1):
            for h in (0, 1):
                ada_half(0, j, h)
        S.dma('sp', X[:, :, 0:NT], xp_d[:, :, 0:NT], reads=[AR.get(('pw1', 1))[1]], writes=B_X[0])
        AR.set_wish([('pw2', 0), ('pw2', 1), ('ada', 0, 2, 0), ('ada', 0, 2, 1)])
        S.dma('sp', CTXF, scT_d[:, :, :, :], writes=[B_CTXF])
        for t in range(1, NTI):
            S.dma('sp', X[:, :, t * NT:(t + 1) * NT], xp_d[:, :, t * NT:(t + 1) * NT],
                  reads=[AR.get(('pw2', 1))[1]], writes=B_X[t])
        stA(samp, 0)
        stB(samp, 0)
        stA(pt[0], 1)
        stCsample(samp, 0)
        adas = None
        memset(ZT[:, :, 0:CTX], 0.0, [B_ZTh])
        for i, T in enumerate(pt):
            bi = (i + 1) % 2
            stB(T, bi)
            if i == len(pt) - 1:
                for h in range(4):
                    AR.release(('pw1', h))
            build_dg(0)
            if i == 0:
                stD(samp, 0)
                ada_half(0, 2, 0)
                ada_half(0, 2, 1)
                stE(samp)
                adas = AdaStream([(0, j, h) for j in (3, 4, 5) for h in (0, 1)])
            adas.tick()
            if i > 0:
                stE(pt[i - 1])
            if i in (1, 2):
                adas.tick()
            gen = None
            if i + 1 < len(pt):
                nb_ = 1 - bi
                gen = norm_mod_gen(pt[i + 1], l, 1, BA[nb_], _Multi(B_BA[nb_]), RSTD_A, B_RSTD_A, TMPF, B_TMPF,
                                   (lambda c, nb_=nb_: BA[nb_][:, c, 0:NT]), B_BA[nb_], MEAN, B_MEAN)
            stC(T, bi, gen)
            if i + 1 < len(pt):
                cp(ZT[:, :, 0:CTX], ZT[:, :, NT:NT + CTX], B_ZT, [B_ZTh])
            stD(T, bi)
        adas.drain()

        S.dma('sp', cssc_d[:, :, :], sc_d[:, 1:CTX, :])

        def tail():
            stE(pt[-1])
            S.dma('sp', csp_d[:, :, :], ZL[:, :, :], reads=[B_ZL])
            for h in range(2):
                AR.release(('pw2', h))
        return tail

    _Multi = list


    def _flat(bs):
        out = []
        for b in bs:
            if isinstance(b, (list, tuple)):
                out.extend(_flat(b))
            else:
                out.append(b)
        return out
    _orig_deps = S._deps
    _orig_upd = S._upd
    S._deps = lambda reads, writes: _orig_deps(_flat(reads), _flat(writes))
    S._upd = lambda reads, writes, tok: _orig_upd(_flat(reads), _flat(writes), tok)

    FFN_EARLY_BYTES = ((SEQ + NS) * KC * 2 + KC * NT * 2 + 2 * NT * 4 + 2 * NT * 4 + 2 * 4 * NT * 2 + NT * 4)

    def ffn_phase(l, prev_tail=None, mid_hook=None):
        seed = phase_begin()
        NTOT = SEQ + NS
        HF = talloc([128, KC, NTOT], BF16, seed)
        SQ = talloc([128, KC, NT], BF16, seed)
        RSTD = talloc([128, NT], F32, seed)
        TMPF = talloc([128, NT], F32, seed)
        SGT = [talloc([128, NT], F32, seed) for _ in range(2)]
        HID = [talloc([128, 4, NT], BF16, seed) for _ in range(2)]
        TT_ = talloc([128, NT], F32, seed)
        assert tstate['off'] == FFN_EARLY_BYTES, (tstate['off'], FFN_EARLY_BYTES)
        B_HF = [[Buf("HF%d_%d" % (t, c), seed) for c in range(KC)] for t in range(NTI + 1)]
        B_SQ = Buf("SQ", seed)
        B_RSTD = Buf("RSTD", seed)
        B_TMPF = Buf("TMPF", seed)
        B_SGT = [Buf("SGT%d" % i, seed) for i in range(2)]
        B_HID = [[Buf("HID%d_%d" % (i, hc), seed) for hc in range(4)] for i in range(2)]
        B_TT = Buf("TT", seed)

        def hcol(T):
            return SEQ if T.sample else T.t0

        units = FUNITS

        def load_unit(u, must):
            ok = AR.load(('g', l, u), must)
            ok = ok and AR.load(('u', l, u), must)
            ok = ok and AR.load(('d', l, u), must)
            return ok

        load_unit(0, True)
        adas = AdaStream([(1, j, h) for j in (0, 1, 2) for h in (0, 1)] if l == 0 else [])
        YF = talloc([128, KC, NT], F32, seed) if l == 1 else None
        B_YF = Buf("YF", seed)

        def final_tile(T):
            N = T.N
            rms_stats(T, SQ, B_SQ, RSTD, B_RSTD)
            for c in range(KC):
                stt(YF[:, c, 0:N], T.xc[c], ppc('fng', c), RSTD[:, 0:N], ALU.mult, ALU.mult,
                    [T.xb[c], B_PP, B_RSTD], [B_YF])
            if not T.sample:
                S.dma('sp', yp_d[:, :, T.t0:T.t0 + N], YF[:, :, 0:N], reads=[B_YF])
            else:
                S.dma('sp', ys_d[:, :, :], YF[:, :, 0:N], reads=[B_YF])

        def do_norm(T):
            h0 = hcol(T)
            norm_mod(T, l, 2, SQ, B_SQ, RSTD, B_RSTD, TMPF, B_TMPF,
                     (lambda c, h0=h0, N=T.N: HF[:, c, h0:h0 + N]), B_HF[T.idx], TT_, B_TT)

        do_norm(tiles[0])
        do_norm(tiles[1])
        if prev_tail is not None:
            prev_tail()
        it = 0
        for u, (f0, n) in enumerate(units):
            if mid_hook is not None and 1 <= u <= 3:
                mid_hook(u - 1)
            load_unit(u, True)
            if u + 1 < len(units):
                load_unit(u + 1, False)
            wg, bg = AR.get(('g', l, u))
            wu, bu = AR.get(('u', l, u))
            wd, bd = AR.get(('d', l, u))
            last_unit = (l == 1 and u == len(units) - 1)
            tord = tiles if not last_unit else ([T_ for T_ in tiles if not T_.sample] + [T_ for T_ in tiles if T_.sample])
            for tix, T in enumerate(tord):
                if u == 0 and tix >= 1 and tix + 1 < len(tiles):
                    do_norm(tiles[tix + 1])
                N = T.N
                h0 = hcol(T)
                hi = it % 2
                it += 1
                for hc in range(n):
                    b1 = ps_next()
                    mm_group(PS[:, b1, 0:N], [(wg[:, kc, hc * 128:(hc + 1) * 128], HF[:, kc, h0:h0 + N])
                                              for kc in range(KC)], [bg] + B_HF[T.idx], [B_PS[b1]])
                    b2 = ps_next()
                    mm_group(PS[:, b2, 0:N], [(wu[:, kc, hc * 128:(hc + 1) * 128], HF[:, kc, h0:h0 + N])
                                              for kc in range(KC)], [bu] + B_HF[T.idx], [B_PS[b2]])
                    si = hc % 2
                    act(SGT[si][:, 0:N], PS[:, b1, 0:N], AF.Silu, [B_PS[b1]], [B_SGT[si]])
                    tt(HID[hi][:, hc, 0:N], SGT[si][:, 0:N], PS[:, b2, 0:N], ALU.mult,
                       [B_SGT[si], B_PS[b2]], [B_HID[hi][hc]])
                for m0 in (0, 4):
                    banks = [ps_next() for _ in range(4)]
                    for mi in range(4):
                        m = m0 + mi
                        b = banks[mi]
                        if n > 1:
                            mm_group(PS[:, b, 0:N], [(wd[:, hc, m * 128:(m + 1) * 128], HID[hi][:, hc, 0:N])
                                                     for hc in range(n - 1)], [bd] + B_HID[hi][0:n - 1], [B_PS[b]],
                                     first=True, last=False)
                    for mi in range(4):
                        m = m0 + mi
                        b = banks[mi]
                        mm_group(PS[:, b, 0:N], [(wd[:, n - 1, m * 128:(m + 1) * 128], HID[hi][:, n - 1, 0:N])],
                                 [bd, B_HID[hi][n - 1]], [B_PS[b]], first=(n == 1), last=True)
                        x_update(T, l, 5, m, PS[:, b, 0:N], B_PS[b], TT_, B_TT, False)
                if l == 1 and u == len(units) - 1:
                    final_tile(T)
            adas.tick()
            for k in ('g', 'u', 'd'):
                AR.release((k, l, u))
        adas.drain()

    GC = {}

    def gmlp_setup(step):
        if step == 0:
            seed = S.all_tokens()
            o = FFN_EARLY_BYTES
            R, o = talloc_at(o, [128, EC, 128], F32)
            BVHL, o = talloc_at(o, [128, E], BF16)
            WT, o = talloc_at(o, [128, 8, 128], BF16)
            TRILB, o = talloc_at(o, [128, 128], BF16)
            assert o <= TEMP_BYTES, o
            GC.update(R=R, BVHL=BVHL, WT=WT, TRILB=TRILB, B_R=Buf("R", seed), B_HL=Buf("HL", seed),
                      B_WT=Buf("WT", seed), B_TRILB=Buf("TRILB", seed))
            S.dma('pool', WT, wsT_d[:, :, :], writes=[GC['B_WT']])
            S.dma('pool', TRILB, tril_d[:, :], writes=[GC['B_TRILB']])
            S.dma('pool', BVHL[0:2, :], rw_d[0:2, 0:E], writes=[GC['B_HL']])
            ROW = R[0:2, :, :].rearrange("p c t -> p (c t)")
            S.dma('sp', ROW, rw_d[0:2, 0:E], writes=[GC['B_R']])
        elif step == 1:
            R, BVHL, WT, TRILB = GC['R'], GC['BVHL'], GC['WT'], GC['TRILB']
            tt(WT, WT, TRILB.unsqueeze(1).broadcast_to([128, 8, 128]), ALU.mult, [GC['B_WT'], GC['B_TRILB']],
               [GC['B_WT']])
            o_n, _ = PP_OFF['negm']
            negm = PP[0:2, o_n:o_n + 1]
            ROW = R[0:2, :, :].rearrange("p c t -> p (c t)")
            stt(BVHL[0:2, :], BVHL[0:2, :], negm, ROW, ALU.mult, ALU.add, [GC['B_HL'], GC['B_R'], B_PP], [GC['B_HL']])
            Rv = R.rearrange("p (g two) t -> p g two t", two=2)
            S.dma('sp', Rv[:, :, 0, :], rw_d[:, 3 * E:3 * E + D].rearrange("p (g t) -> p g t", g=8), writes=[GC['B_R']])
        else:
            R, WT = GC['R'], GC['WT']
            for hb in range(2):
                b = ps_next()
                mm_multi([(PS[:, b, gg * 128:(gg + 1) * 128], [(ONESB[:, :], WT[:, hb * 4 + gg, :])])
                          for gg in range(4)], [GC['B_WT'], B_ID], [B_PS[b]])
                for gg in range(4):
                    g = hb * 4 + gg
                    for c in (2 * g + 1, 2 * g):
                        stt(R[:, c, :], PS[:, b, gg * 128:(gg + 1) * 128], ppc('glnb', c), R[:, 2 * g, :],
                            ALU.mult, ALU.add, [B_PS[b], B_PP, GC['B_R']], [GC['B_R']])

    def gmlp_phase():
        l = 1
        seed = phase_begin()
        H = talloc([128, KC, NT], BF16, seed)
        R, BVHL, WT, B_R, B_HL, B_WT = GC['R'], GC['BVHL'], GC['WT'], GC['B_R'], GC['B_HL'], GC['B_WT']
        V = [talloc([128, E], F32, seed) for _ in range(2)]
        VNB = talloc([128, 4, E], BF16, seed)
        UM = talloc([128, EC, NT], BF16, seed)
        ST = [talloc([128, 16], F32, seed) for _ in range(2)]
        assert tstate['off'] <= FFN_EARLY_BYTES, tstate['off']
        B_H = [Buf("H%d" % c, seed) for c in range(KC)]
        B_V = [Buf("V%d" % i, seed) for i in range(2)]
        B_VNB = [Buf("VNB%d" % q, seed) for q in range(4)]
        B_UM = [Buf("UM%d" % c, seed) for c in range(EC)]
        B_ST = [Buf("ST%d" % i, seed) for i in range(2)]
        Uf = [V[0][:, i * NT:(i + 1) * NT] for i in range(2)]
        MX = [V[0][:, (2 + i) * NT:(3 + i) * NT] for i in range(2)]
        TMPF = V[1][:, 0:NT]
        RSTD = V[1][:, NT:2 * NT]
        TTs = [V[1][:, (2 + i) * NT:(3 + i) * NT] for i in range(2)]
        B_Uf = [Buf("Uf%d" % i, seed) for i in range(2)]
        B_MX = [Buf("MX%d" % i, seed) for i in range(2)]
        B_TMPF = Buf("TMPFg", seed)
        B_RSTD = Buf("RSTDg", seed)
        B_TT = [Buf("TTg%d" % i, seed) for i in range(2)]
        ALIAS = [B_Uf + B_MX, [B_TMPF, B_RSTD] + B_TT]

        def vbufs(i):
            return [B_V[i]] + ALIAS[i]

        for h in range(4):
            AR.load(('wv', h))

        def wvk(kc):
            v, b = AR.get(('wv', kc // 2))
            return v[:, kc % 2, :], b

        wv_bufs = [AR.get(('wv', h))[1] for h in range(4)]

        def load_wu(ti, h, must=True):
            return AR.load(('wu', ti, h), must)

        def load_wo(ti, h, must=True):
            return AR.load(('wo', ti, h), must)

        def do_norm(T):
            norm_mod(T, l, 1, H, _Multi(B_H), RSTD, B_RSTD, TMPF, B_TMPF, (lambda c, N=T.N: H[:, c, 0:N]), B_H,
                     Uf[0], B_Uf[0])

        adas = AdaStream([(1, j, h) for j in (3, 4, 5) for h in (0, 1)])
        samp = [T for T in tiles if T.sample][0]
        pt = [T for T in tiles if not T.sample]
        HS = talloc([128, KC, NS], BF16, seed)
        UMS = talloc([128, EC, NS], BF16, seed)
        VBS = talloc([128, EC, NS], BF16, seed)
        B_VBS = Buf("VBS", seed)
        B_HS = [Buf("HS%d" % c, seed) for c in range(KC)]
        B_UMS = [Buf("UMS%d" % c, seed) for c in range(EC)]
        assert tstate['off'] <= FFN_EARLY_BYTES, tstate['off']

        def sample_norm():
            norm_mod(samp, l, 1, HS, _Multi(B_HS), RSTD, B_RSTD, TMPF, B_TMPF, (lambda c: HS[:, c, 0:NS]), B_HS)

        def sample_v():
            N = NS
            VF = UM_F32
            for c in range(EC):
                b = ps_next()
                pairs = []
                for kc in range(KC):
                    wv_, wb_ = wvk(kc)
                    pairs.append((wv_[:, c * 128:(c + 1) * 128], HS[:, kc, 0:N]))
                mm_group(PS[:, b, 0:N], pairs, wv_bufs + B_HS, [B_PS[b]])
                act(VF[:, c, :], PS[:, b, 0:N], AF.Gelu_apprx_tanh, [B_PS[b], B_PP], [B_VF], bias=ppc('binv', c))
            VSQ = UMS[:, :, 0:NS]
            VB = VBS
            act(VSQ, VF, AF.Square, [B_VF], B_UMS)
            cp(VB, VF, [B_VF], [B_VBS])
            MEAN = ST2[:, 0:NS]
            MSQ = ST2[:, NS:2 * NS]
            RS = ST2[:, 2 * NS:3 * NS]
            ln_stats(None, None, VB, B_VBS, VSQ, _Multi(B_UMS), EC, NS, MEAN, B_ST2, RS, B_ST2, MSQ, B_ST2, E)
            tt(VF, VF, MEAN.unsqueeze(1).broadcast_to([128, EC, NS]), ALU.subtract, [B_VF, B_ST2], [B_VF])
            tt(VF, VF, RS.unsqueeze(1).broadcast_to([128, EC, NS]), ALU.mult, [B_VF, B_ST2], [B_VF])
            tt(VF, VF, ppv('glng').unsqueeze(2).broadcast_to([128, EC, NS]), ALU.mult, [B_VF, B_PP], [B_VF])
            tt(VF, VF, ppv('glnb').unsqueeze(2).broadcast_to([128, EC, NS]), ALU.add, [B_VF, B_PP], [B_VF])
            S.dma('sp', gv_d[:, :, :], VF, reads=[B_VF])
            tt(VF, VF, ppv('ws00').unsqueeze(2).broadcast_to([128, EC, NS]), ALU.mult, [B_VF, B_PP], [B_VF])
            tt(VF, VF, ppv('bs00').unsqueeze(2).broadcast_to([128, EC, NS]), ALU.add, [B_VF, B_PP], [B_VF])

        sample_norm()
        do_norm(pt[0])
        uidx = 0
        for _pos, T in enumerate(pt):
            N = T.N
            ti = T.idx
            with_s = (_pos == 0)
            order = [('wu', ti, h) for h in range(4)] + [('wo', ti, h) for h in range(4)]
            if _pos + 1 < len(pt):
                order.append(('wu', pt[_pos + 1].idx, 0))

            def stream_get(key, order=order):
                (load_wu if key[0] == 'wu' else load_wo)(key[1], key[2], True)
                nx = order[order.index(key) + 1] if order.index(key) + 1 < len(order) else None
                if nx is not None:
                    (load_wu if nx[0] == 'wu' else load_wo)(nx[1], nx[2], False)
                return AR.get(key)
            for q in range(4):
                vi = q % 2
                b0 = ps_next4()
                for nb in range(4):
                    pairs = []
                    for kc in range(KC):
                        wv_, wb_ = wvk(kc)
                        pairs.append((H[:, kc, q * 128:(q + 1) * 128], wv_[:, nb * 512:(nb + 1) * 512]))
                    pairs.append((ONESB[0:2, :], BVHL[0:2, nb * 512:(nb + 1) * 512]))
                    mm_group(PS[:, b0 + nb, :], pairs, wv_bufs + B_H + [B_HL, B_ID], [B_PS[b0 + nb]])
                psv = PS[:, b0:b0 + 4, :].rearrange("p a b -> p (a b)")
                st = ST[vi]
                memset(st[:, 0:2], 0.0, [B_ST[vi]])
                act(V[vi], psv, AF.Gelu_apprx_tanh, [B_PS[b0 + i] for i in range(4)], vbufs(vi) + [B_ST[vi]],
                    accum=st[:, 0:1])
                act(VNB[:, q, :], V[vi], AF.Square, [B_V[vi]], [B_VNB[q], B_ST[vi]], accum=st[:, 1:2])
                ts(st[:, 2:3], st[:, 0:1], 1.0 / E, ALU.mult, [B_ST[vi]], [B_ST[vi]])
                tt(st[:, 3:4], st[:, 2:3], st[:, 2:3], ALU.mult, [B_ST[vi]], [B_ST[vi]])
                stt(st[:, 3:4], st[:, 1:2], 1.0 / E, st[:, 3:4], ALU.mult, ALU.subtract, [B_ST[vi]], [B_ST[vi]])
                act(st[:, 4:5], st[:, 3:4], AF.Sqrt, [B_ST[vi]], [B_ST[vi]], bias=EPS, scale=1.0)
                recip(st[:, 4:5], st[:, 4:5], [B_ST[vi]], [B_ST[vi]])
                ts(VNB[:, q, :], V[vi], st[:, 2:3], ALU.subtract, vbufs(vi) + [B_ST[vi]], [B_VNB[q]],
                   s2=st[:, 4:5], op1=ALU.mult)
                if with_s and q == 1:
                    sample_v()
            if T is pt[-1]:
                for h in range(4):
                    AR.release(('wv', h))
            for c in range(EC):
                g = c // 2
                ui = uidx % 2
                uidx += 1
                wu, bu = stream_get(('wu', ti, c // 4)) if c % 4 == 0 else AR.get(('wu', ti, c // 4))

                def issue_u(cc, wu=wu, bu=bu):
                    bb_ = ps_next()
                    mm_group(PS[:, bb_, 0:N], [(wu[:, kc, (cc % 4) * 128:(cc % 4 + 1) * 128], H[:, kc, 0:N])
                                               for kc in range(KC)], [bu] + B_H, [B_PS[bb_]])
                    return bb_
                if c % 4 == 0:
                    ubank = {c: issue_u(c)}
                if c % 4 != 3:
                    ubank[c + 1] = issue_u(c + 1)
                bu_ = ubank[c]
                bm = ps_next()
                mm_multi([(PS[:, bm, q * 128:(q + 1) * 128], [(VNB[:, q, c * 128:(c + 1) * 128], WT[:, g, :])])
                          for q in range(4)], B_VNB + [B_WT], [B_PS[bm]])
                act(Uf[ui], PS[:, bu_, 0:N], AF.Gelu_apprx_tanh, [B_PS[bu_], B_PP], [B_Uf[ui]], bias=ppc('binu', c))
                stt(MX[ui].rearrange("p (q t) -> p q t", q=4), PS[:, bm, :].rearrange("p (q t) -> p q t", q=4),
                    ppc('glng', c), R[:, c, :].unsqueeze(1).broadcast_to([128, 4, 128]), ALU.mult, ALU.add,
                    [B_PS[bm], B_PP, B_R], [B_MX[ui]])
                tt(UM[:, c, 0:N], Uf[ui], MX[ui], ALU.mult, [B_Uf[ui], B_MX[ui]], [B_UM[c]])
                if with_s:
                    ui = uidx % 2
                    uidx += 1
                    b = ps_next()
                    mm_group(PS[:, b, 0:NS], [(wu[:, kc, (c % 4) * 128:(c % 4 + 1) * 128], HS[:, kc, 0:NS])
                                              for kc in range(KC)], [bu] + B_HS, [B_PS[b]])
                    act(Uf[ui][:, 0:NS], PS[:, b, 0:NS], AF.Gelu_apprx_tanh, [B_PS[b], B_PP], [B_Uf[ui]],
                        bias=ppc('binu', c))
                    tt(UMS[:, c, 0:NS], Uf[ui][:, 0:NS], UM_F32[:, c, :], ALU.mult, [B_Uf[ui], B_VF], [B_UMS[c]])
                if c % 4 == 3:
                    AR.release(('wu', ti, c // 4))
            gen = None
            if _pos + 1 < len(pt):
                Tn = pt[_pos + 1]
                gen = norm_mod_gen(Tn, l, 1, H, _Multi(B_H), RSTD, B_RSTD, TMPF, B_TMPF,
                                   (lambda c: H[:, c, 0:NT]), B_H, Uf[0], B_Uf[0])
                next(gen, None)
            adas.tick()
            if _pos == 0:
                adas.tick()
                adas.tick()
            for m in range(KC):
                if gen is not None:
                    next(gen, None)
                    if m in (3, 7):
                        next(gen, None)
                wo, bo = stream_get(('wo', ti, m // 2)) if m % 2 == 0 else AR.get(('wo', ti, m // 2))
                b = ps_next()
                if m < 2:
                    mm_group(PS[:, b, 0:N], [(wo[:, c, (m % 2) * 128:(m % 2 + 1) * 128], UM[:, c, 0:N])
                                             for c in range(12)], [bo] + B_UM[0:12], [B_PS[b]], first=True, last=False)
                    mm_group(PS[:, b, 0:N], [(wo[:, c, (m % 2) * 128:(m % 2 + 1) * 128], UM[:, c, 0:N])
                                             for c in range(12, EC)], [bo] + B_UM[12:], [B_PS[b]], first=False, last=True)
                else:
                    mm_group(PS[:, b, 0:N], [(wo[:, c, (m % 2) * 128:(m % 2 + 1) * 128], UM[:, c, 0:N])
                                             for c in range(EC)], [bo] + B_UM, [B_PS[b]])
                x_update(T, l, 2, m, PS[:, b, 0:N], B_PS[b], TTs[m % 2], B_TT[m % 2], True)
                if with_s:
                    b = ps_next()
                    mm_group(PS[:, b, 0:NS], [(wo[:, c, (m % 2) * 128:(m % 2 + 1) * 128], UMS[:, c, 0:NS])
                                              for c in range(EC)], [bo] + B_UMS, [B_PS[b]])
                    x_update(samp, l, 2, m, PS[:, b, 0:NS], B_PS[b], TTs[(m + 1) % 2], B_TT[(m + 1) % 2], True)
                if m % 2 == 1:
                    AR.release(('wo', ti, m // 2))
            if gen is not None:
                for _ in gen:
                    pass
            if T is pt[-1]:
                adas.drain()

    def final_phase():
        seed = phase_begin()
        SQ = talloc([128, KC, NT], BF16, seed)
        RSTD = talloc([128, NT], F32, seed)
        YF = [talloc([128, KC, NT], F32, seed) for _ in range(2)]
        B_SQ = Buf("SQ", seed)
        B_RSTD = Buf("RSTD", seed)
        B_YF = [Buf("YF%d" % i, seed) for i in range(2)]
        for T in tiles:
            N = T.N
            yi = T.idx % 2
            rms_stats(T, SQ, B_SQ, RSTD, B_RSTD)
            for c in range(KC):
                stt(YF[yi][:, c, 0:N], T.xc[c], ppc('fng', c), RSTD[:, 0:N], ALU.mult, ALU.mult,
                    [T.xb[c], B_PP, B_RSTD], [B_YF[yi]])
            if not T.sample:
                S.dma('sp', yp_d[:, :, T.t0:T.t0 + N], YF[yi][:, :, 0:N], reads=[B_YF[yi]])
            else:
                S.dma('sp', ys_d[:, :, :], YF[yi][:, :, 0:N], reads=[B_YF[yi]])

    conv_tail = conv_phase()
    ffn_phase(0, prev_tail=conv_tail, mid_hook=gmlp_setup)
    gmlp_phase()
    ffn_phase(1)
    S.final_wait('sp')

    semnames = set()
    for e in Sched.ENG:
        for w, fn, inc in S.stream[e]:
            for k, _ in w:
                semnames.add(k)
            if inc:
                semnames.add(inc[0])
    semnames = sorted(semnames)
    import contextlib
    with contextlib.ExitStack() as es:
        SEM = {n: es.enter_context(nc.semaphore(n)) for n in semnames}
        block = es.enter_context(nc.Block())

        def replay(eng_obj, name):
            for w, fn, inc in S.stream[name]:
                for k, v in w:
                    eng_obj.wait_ge(SEM[k], v)
                if fn is not None:
                    ins = fn(eng_obj)
                    ins.then_inc(SEM[inc[0]], inc[1])

        @block.tensor
        def _(pe):
            replay(pe, 'pe')

        @block.scalar
        def _(a):
            replay(a, 'act')

        @block.vector
        def _(v):
            replay(v, 'dve')

        @block.gpsimd
        def _(g):
            replay(g, 'pool')

        @block.sync
        def _(s):
            replay(s, 'sp')
    return nc, AR.order


_NC_CACHE = {}


def _cols(v):
    v = np.asarray(v, dtype=np.float32).reshape(-1, 128)
    return np.ascontiguousarray(v.T)


def kernel(x_prompt, x_sample, c_prompt, c_sample, state_conv, w_ada, b_ada, norm_mix_g, norm_ffn_g,
           final_norm_g, conv_w_pw1, conv_b_pw1, conv_w_dw, conv_b_dw, conv_ln_g, conv_ln_b, conv_w_pw2,
           conv_b_pw2, gmlp_w_in, gmlp_b_in, gmlp_ln_g, gmlp_ln_b, gmlp_w_s, gmlp_b_s, gmlp_w_out,
           gmlp_b_out, ffn_w_gate, ffn_w_up, ffn_w_down):
    f = lambda a: np.ascontiguousarray(np.asarray(a, dtype=np.float32))
    x_prompt, x_sample, c_prompt, c_sample, state_conv = map(f, (x_prompt, x_sample, c_prompt, c_sample, state_conv))
    parts = {
        'nmg0': _cols(norm_mix_g[0]), 'nmg1': _cols(norm_mix_g[1]),
        'nfg0': _cols(norm_ffn_g[0]), 'nfg1': _cols(norm_ffn_g[1]), 'fng': _cols(final_norm_g),
        'bada0': _cols(b_ada[0]), 'bada1': _cols(b_ada[1]),
        'b1': _cols(conv_b_pw1[0]), 'bdw': _cols(conv_b_dw[0]), 'lng': _cols(conv_ln_g[0]),
        'lnb': _cols(conv_ln_b[0]), 'b2': _cols(conv_b_pw2[0]),
        'binu': _cols(np.asarray(gmlp_b_in)[0, :E]), 'binv': _cols(np.asarray(gmlp_b_in)[0, E:]),
        'glng': _cols(gmlp_ln_g[0]), 'glnb': _cols(gmlp_ln_b[0]), 'bout': _cols(gmlp_b_out[0]),
        'wdw': np.ascontiguousarray(np.asarray(conv_w_dw, np.float32)[0].reshape(CW, KC, 128).transpose(2, 1, 0)).reshape(128, KC * CW),
        'ws00': np.ascontiguousarray(np.broadcast_to(np.repeat(np.asarray(gmlp_w_s, np.float32)[0, :, 0, 0], 2)[None, :], (128, EC))),
        'bs00': np.ascontiguousarray(np.broadcast_to(np.repeat(np.asarray(gmlp_b_s, np.float32)[0, :, 0], 2)[None, :], (128, EC))),
        'negm': np.array([0.0, -1.0] + [0.0] * 126, np.float32).reshape(128, 1),
    }
    pp = np.concatenate([parts[n].reshape(128, w) for n, w in PP_LAYOUT], axis=1).astype(np.float32)
    assert pp.shape == (128, NPP)
    gb = np.asarray(gmlp_b_in, np.float32)[0, E:]
    rw = np.concatenate([gb, np.asarray(gmlp_ln_g, np.float32)[0], np.asarray(gmlp_ln_b, np.float32)[0],
                         np.asarray(gmlp_b_s, np.float32)[0].reshape(-1)])
    rw = np.ascontiguousarray(np.broadcast_to(rw[None, :], (128, rw.shape[0]))).astype(np.float32)
    wsT = np.ascontiguousarray(np.asarray(gmlp_w_s, np.float32)[0].transpose(2, 0, 1))
    tril = np.ascontiguousarray(np.triu(np.ones((128, 128), np.float32)))
    ident = np.eye(128, dtype=np.float32)
    shared = {
        'pp': pp, 'rw': rw, 'wsT': wsT, 'tril': tril, 'ident': ident,
        'w_ada': f(w_ada), 'w_pw1': f(conv_w_pw1[0]), 'w_pw2': f(conv_w_pw2[0]),
        'w_in': f(gmlp_w_in[0]), 'w_out': f(gmlp_w_out[0]),
        'w_gate': f(ffn_w_gate), 'w_up': f(ffn_w_up), 'w_down': f(ffn_w_down),
    }
    in_maps = []
    for i in range(NCORES):
        xp = np.ascontiguousarray(x_prompt[i].reshape(SEQ, KC, 128).transpose(2, 1, 0))
        xs = np.ascontiguousarray(x_sample[i * NS:(i + 1) * NS, 0].reshape(NS, KC, 128).transpose(2, 1, 0))
        cc = np.concatenate([c_prompt[i:i + 1], c_sample[i * NS:(i + 1) * NS]], axis=0)
        cin = np.ascontiguousarray(cc.reshape(1 + NS, KC, 128).transpose(2, 1, 0))
        sc = np.ascontiguousarray(state_conv[0, i * NS:(i + 1) * NS])
        scT = np.ascontiguousarray(sc.reshape(NS, CTX, KC, 128).transpose(3, 2, 0, 1))
        m = dict(shared)
        m.update({'xp': xp, 'xs': xs, 'cin': cin, 'sc': sc, 'scT': scT})
        in_maps.append(m)
    if 'nc' not in _NC_CACHE:
        _, _order = build_program()
        _NC_CACHE['nc'] = build_program(_order)[0]
    nc = _NC_CACHE['nc']
    res = run_bass_kernel_spmd(nc, in_maps, core_ids=list(range(NCORES)))
    R = res.results
    y_prompt = np.stack([R[i]['yp'].transpose(2, 1, 0).reshape(SEQ, D) for i in range(NCORES)], 0)
    y_sample = np.concatenate([R[i]['ys'].transpose(2, 1, 0).reshape(NS, 1, D) for i in range(NCORES)], 0)
    csp = np.stack([R[i]['csp'].transpose(2, 1, 0).reshape(CTX, D) for i in range(NCORES)], 0)[None]
    css = np.concatenate([np.concatenate([R[i]['cssc'], R[i]['zs'].transpose(2, 1, 0).reshape(NS, 1, D)], axis=1)
                          for i in range(NCORES)], 0)[None]
    gv = np.concatenate([R[i]['gv'].transpose(2, 1, 0).reshape(NS, 1, E) for i in range(NCORES)], 0)[None]
    return (np.ascontiguousarray(y_prompt, dtype=np.float32), np.ascontiguousarray(y_sample, dtype=np.float32),
            np.ascontiguousarray(csp, dtype=np.float32), np.ascontiguousarray(css, dtype=np.float32),
            np.ascontiguousarray(gv, dtype=np.float32))
```

```python
import numpy as np
import concourse.bass as bass
import concourse.mybir as mybir
from concourse.bass_utils import run_bass_kernel_spmd

F32 = mybir.dt.float32
BF16 = mybir.dt.bfloat16
AF = mybir.ActivationFunctionType
ALU = mybir.AluOpType
AX = mybir.AxisListType

NCORES = 8
D = 1024
KC = 8
SEQ = 2048
NT = 512
NTI = 4
NS = 16
E = 2048
EC = 16
FF = 2816
FCH = 22
CW = 31
CTX = 30
EPS = 1e-6
SLOT = 4096
TEMP_BYTES = 77824

PP_LAYOUT = [('nmg0', 8), ('nmg1', 8), ('nfg0', 8), ('nfg1', 8), ('fng', 8), ('bada0', 48), ('bada1', 48),
             ('b1', 16), ('bdw', 8), ('lng', 8), ('lnb', 8), ('b2', 8), ('binu', 16), ('binv', 16),
             ('glng', 16), ('glnb', 16), ('bout', 8), ('wdw', 248), ('ws00', 16), ('bs00', 16), ('negm', 1)]
PP_OFF = {}
_o = 0
for _n, _w in PP_LAYOUT:
    PP_OFF[_n] = (_o, _w)
    _o += _w
NPP = _o


class Buf:
    __slots__ = ('name', 'w', 'r', 'const')

    def __init__(self, name, seed=None, const=False):
        self.name = name
        self.w = None
        self.r = list(seed) if seed else []
        self.const = const


class Sched:
    ENG = ('pe', 'act', 'dve', 'pool', 'sp')

    def __init__(self):
        self.stream = {e: [] for e in self.ENG}
        self.cnt = {'pe': 0, 'act': 0, 'dve': 0}
        self.waited = {e: {} for e in self.ENG}
        self.dcnt = {}
        self.sp_rr = 0

    def _deps(self, reads, writes):
        d = {}

        def add(t):
            if t is not None and d.get(t[0], 0) < t[1]:
                d[t[0]] = t[1]
        for b in reads:
            add(b.w)
        for b in writes:
            add(b.w)
            for t in b.r:
                add(t)
        return d

    def _waits(self, eng, d):
        out = []
        for k, v in d.items():
            if eng == 'pe' and k == 'P_pe':
                continue
            if self.waited[eng].get(k, 0) >= v:
                continue
            self.waited[eng][k] = v
            out.append((k, v))
        return out

    def _upd(self, reads, writes, tok):
        for b in reads:
            if not b.const:
                b.r.append(tok)
        for b in writes:
            b.w = tok
            b.r = []

    def comp(self, eng, fn, reads=(), writes=()):
        d = self._deps(reads, writes)
        w = self._waits(eng, d)
        self.cnt[eng] += 1
        tok = ('P_' + eng, self.cnt[eng])
        self.stream[eng].append((w, fn, ('P_' + eng, 1)))
        self._upd(reads, writes, tok)
        return tok

    def dma(self, q, out, in_, reads=(), writes=(), sem=None):
        if sem is None:
            sem = ('S%d' if q == 'sp' else 'Q%d') % (self.sp_rr % 8 if q == 'sp' else self.sp_rr % 4)
            self.sp_rr += 1
        d = self._deps(reads, writes)
        if q == 'pool':
            hist = self.__dict__.setdefault('pool_hist', [])
            if len(hist) >= 3:
                t = hist[-3]
                if d.get(t[0], 0) < t[1]:
                    d[t[0]] = t[1]
        c = self.dcnt.get(sem, 0)
        if c:
            d[sem] = max(d.get(sem, 0), c)
        w = self._waits(q, d)
        self.dcnt[sem] = c + 16
        tok = (sem, c + 16)
        self.stream[q].append((w, (lambda e, o=out, i=in_: e.dma_start(out=o, in_=i)), (sem, 16)))
        if q == 'pool':
            self.pool_hist.append(tok)
        self._upd(reads, writes, tok)
        return tok

    def all_tokens(self):
        t = [('P_' + e, c) for e, c in self.cnt.items() if c]
        t += [(s, c) for s, c in self.dcnt.items() if c and not s.startswith('A')]
        return t

    def final_wait(self, q):
        d = {s: c for s, c in self.dcnt.items() if c}
        for e, c in self.cnt.items():
            if c:
                d['P_' + e] = c
        self.stream[q].append((self._waits(q, d), None, None))


def build_program(plan=None):
    nc = bass.Bass("TRN2", target_bir_lowering=False)
    S = Sched()

    def din(name, shape):
        return nc.dram_tensor(name, list(shape), F32, kind="ExternalInput").ap()

    def dout(name, shape):
        return nc.dram_tensor(name, list(shape), F32, kind="ExternalOutput").ap()

    xp_d = din("xp", [128, KC, SEQ])
    xs_d = din("xs", [128, KC, NS])
    cin_d = din("cin", [128, KC, 1 + NS])
    sc_d = din("sc", [NS, CTX, D])
    scT_d = din("scT", [128, KC, NS, CTX])
    pp_d = din("pp", [128, NPP])
    rw_d = din("rw", [128, 3 * E + D])
    wsT_d = din("wsT", [128, 8, 128])
    tril_d = din("tril", [128, 128])
    ident_d = din("ident", [128, 128])
    w_ada_d = din("w_ada", [2, D, 6 * D])
    w_pw1_d = din("w_pw1", [D, 2 * D])
    w_pw2_d = din("w_pw2", [D, D])
    w_in_d = din("w_in", [D, 2 * E])
    w_out_d = din("w_out", [E, D])
    w_gate_d = din("w_gate", [2, D, FF])
    w_up_d = din("w_up", [2, D, FF])
    w_down_d = din("w_down", [2, FF, D])

    yp_d = dout("yp", [128, KC, SEQ])
    ys_d = dout("ys", [128, KC, NS])
    csp_d = dout("csp", [128, KC, CTX])
    cssc_d = dout("cssc", [NS, CTX - 1, D])
    zs_d = dout("zs", [128, KC, NS])
    gv_d = dout("gv", [128, EC, NS])

    def sb(name, shape, dt=F32):
        return nc.alloc_sbuf_tensor(name, list(shape), dt)

    X = sb("X", [128, KC, SEQ])
    XS = sb("XS", [128, KC, NS])
    PP = sb("PP", [128, NPP])
    IDB = sb("IDB", [128, 128], BF16)
    ONESB = sb("ONESB", [128, 128], BF16)
    CS = sb("CS", [128, KC, 1 + NS], BF16)
    _M0 = sb("MOD0", [128, 6, KC, 1 + NS])
    _D0 = sb("DER0", [128, 3, KC, 1 + NS])
    MOD = [_M0, _M0]
    DER = [_D0, _D0]
    ZS = sb("ZS", [128, KC, NS])[:, :, :]
    UM_F32 = sb("VFs", [128, EC, NS])[:, :, :]
    ST2 = sb("ST2", [128, 3 * NS])
    HB1 = sb("HB1", [128, 8])
    ZL = sb("ZL", [128, KC, CTX])
    TEMP = sb("TEMP", [128, TEMP_BYTES // 4])
    nslot = int((nc.sbuf_bytes_remaining - 256) // (SLOT * 2))
    assert nslot >= 7, (nslot, nc.sbuf_bytes_remaining)
    nslot = min(nslot, 7)
    ARENA = sb("ARENA", [128, nslot * SLOT], BF16)
    PS = nc.alloc_psum_tensor("PS", [128, 8, 512], F32)

    def ppc(name, c):
        o, w = PP_OFF[name]
        assert 0 <= c < w
        return PP[:, o + c:o + c + 1]

    def ppv(name):
        o, w = PP_OFF[name]
        return PP[:, o:o + w]

    B_PP = Buf("PP", const=True)
    B_ID = Buf("ID", const=True)
    B_CIN = Buf("CIN")
    B_CS = Buf("CS")
    _bm = [Buf("MOD_%d" % j) for j in range(6)]
    _bd = [Buf("DER_%d" % j) for j in range(3)]
    B_MOD = [_bm, _bm]
    B_DER = [_bd, _bd]
    B_ZSb = [Buf("ZS%d" % c) for c in range(KC)]
    B_VF = Buf("VFs")
    B_ST2 = Buf("ST2")
    B_HB1 = Buf("HB1")
    B_ZL = Buf("ZL")
    B_X = [[Buf("X%d_%d" % (t, c)) for c in range(KC)] for t in range(NTI)]
    B_XS = [Buf("XS%d" % c) for c in range(KC)]
    B_PS = [Buf("PS%d" % b) for b in range(8)]

    ps_state = {'n': 0}

    def ps_next():
        b = ps_state['n'] % 8
        ps_state['n'] += 1
        return b

    ps4 = {'n': 0}

    def ps_next4():
        g = ps4['n'] % 2
        ps4['n'] += 1
        return g * 4

    def wrows(w2d, r0, nk, c0, ncol):
        return w2d[r0 * 128:(r0 + nk) * 128, c0:c0 + ncol].rearrange("(k p) m -> p k m", p=128)

    FUNITS = []
    _f = 0
    for _n in (4, 4, 4, 2, 4, 4):
        FUNITS.append((_f, _n))
        _f += _n
    assert _f == FCH

    def piece_spec(key):
        k = key[0]
        if k == 'ada':
            _, l, j, h = key
            return wrows(w_ada_d[l], 0, KC, j * D + h * 512, 512), KC, 512
        if k == 'pw1':
            return wrows(w_pw1_d, 0, KC, key[1] * 512, 512), KC, 512
        if k == 'pw2':
            return wrows(w_pw2_d, 0, KC, key[1] * 512, 512), KC, 512
        if k in ('g', 'u'):
            _, l, u = key
            f0, n = FUNITS[u]
            return wrows((w_gate_d if k == 'g' else w_up_d)[l], 0, KC, f0 * 128, n * 128), KC, n * 128
        if k == 'd':
            _, l, u = key
            f0, n = FUNITS[u]
            return wrows(w_down_d[l], f0, n, 0, D), n, D
        if k == 'wv':
            return wrows(w_in_d, 2 * key[1], 2, E, E), 2, E
        if k == 'wu':
            return wrows(w_in_d, 0, KC, key[2] * 512, 512), KC, 512
        if k == 'wo':
            return wrows(w_out_d, 0, EC, key[2] * 256, 256), EC, 256
        raise KeyError(key)

    SCR = {}

    class Arena:
        def __init__(self):
            self.free = list(range(nslot))
            self.bufs = [Buf("slot%d" % k) for k in range(nslot)]
            self.loaded = {}
            self.order = []
            self.plan = plan
            self.pi = 0
            self.wish = []

        def set_wish(self, keys):
            self.wish = list(keys)
            self._try_wish()

        def _try_wish(self):
            if self.plan is not None:
                self.pump()
                return
            while self.wish and self.free:
                k = self.wish.pop(0)
                if k not in self.loaded:
                    self._issue(k)

        def _issue(self, key):
            src, a, b = piece_spec(key)
            k = self.free.pop(0)
            v = ARENA[:, k * SLOT:k * SLOT + a * b].rearrange("p (a b) -> p a b", a=a)
            rd = []
            scr = None
            if key[0] in ('wu', 'wo'):
                skey = (key[0], key[2])
                if skey not in SCR:
                    SCR[skey] = (nc.dram_tensor("scr_%s%d" % skey, [128, a, b], BF16, kind="Internal").ap(),
                                 Buf("scr_%s%d" % skey))
                    scr = SCR[skey]
                else:
                    src = SCR[skey][0][:, :, :]
                    rd = [SCR[skey][1]]
            S.dma('pool', v, src, reads=rd, writes=[self.bufs[k]], sem='A%d' % k)
            if scr is not None:
                S.dma('sp', scr[0][:, :, :], v, reads=[self.bufs[k]], writes=[scr[1]], sem='AW%d' % (len(SCR) % 4))
            self.loaded[key] = (k, v)
            self.order.append(key)

        def pump(self):
            while self.free and self.pi < len(self.plan):
                key = self.plan[self.pi]
                self.pi += 1
                self._issue(key)

        def load(self, key, must=True):
            if key in self.loaded:
                return True
            if self.plan is not None:
                self.pump()
                assert key in self.loaded or not must, "arena full for %s" % (key,)
                return key in self.loaded
            if not self.free:
                assert not must, "arena full for %s" % (key,)
                return False
            self._issue(key)
            return True

        def get(self, key):
            self.load(key)
            k, v = self.loaded[key]
            return v, self.bufs[k]

        def release(self, key):
            k, _ = self.loaded.pop(key)
            self.free.append(k)
            self._try_wish()

    AR = Arena()

    tstate = {'off': 0}

    def phase_begin():
        tstate['off'] = 0
        return S.all_tokens()

    def talloc(shape, dt, seed):
        n = 1
        for s_ in shape[1:]:
            n *= s_
        nbytes = n * (4 if dt == F32 else 2)
        nbytes = (nbytes + 31) // 32 * 32
        o = tstate['off']
        assert o + nbytes <= TEMP_BYTES, ("TEMP overflow", o, nbytes)
        tstate['off'] = o + nbytes
        v = TEMP[:, o // 4:(o + nbytes) // 4]
        if dt != F32:
            v = v.bitcast(dt)
        v = v[:, 0:n]
        if len(shape) == 3:
            v = v.rearrange("p (a b) -> p a b", a=shape[1])
        elif len(shape) == 4:
            v = v.rearrange("p (a b c) -> p a b c", a=shape[1], b=shape[2])
        return v

    def talloc_at(off, shape, dt):
        save = tstate['off']
        tstate['off'] = off
        v = talloc(shape, dt, None)
        end = tstate['off']
        tstate['off'] = save
        return v, end

    def mm_group(out, pairs, reads, writes, first=True, last=True):
        n = len(pairs)

        def fn(pe, out=out, pairs=pairs, n=n, first=first, last=last):
            ins = None
            for i, (l, r) in enumerate(pairs):
                ins = pe.matmul(out, lhsT=l, rhs=r, start=(first and i == 0), stop=(last and i == n - 1))
            return ins
        return S.comp('pe', fn, reads, writes)

    def mm_multi(groups, reads, writes):
        def fn(pe, groups=groups):
            ins = None
            for (o, prs) in groups:
                for i, (l, r) in enumerate(prs):
                    ins = pe.matmul(o, lhsT=l, rhs=r, start=(i == 0), stop=(i == len(prs) - 1))
            return ins
        return S.comp('pe', fn, reads, writes)

    def act(out, in_, func, reads, writes, bias=None, scale=None, accum=None):
        kw = {}
        if bias is not None:
            kw['bias'] = bias
        if scale is not None:
            kw['scale'] = scale
        if accum is not None:
            kw['accum_out'] = accum
        return S.comp('act', lambda a, o=out, i=in_, f=func, kw=kw: a.activation(out=o, in_=i, func=f, **kw),
                      reads, writes)

    def tt(out, in0, in1, op, reads, writes):
        return S.comp('dve', lambda v, o=out, a=in0, b=in1, op=op: v.tensor_tensor(out=o, in0=a, in1=b, op=op),
                      reads, writes)

    def ts(out, in0, s1, op0, reads, writes, s2=None, op1=None):
        if s2 is None:
            return S.comp('dve', lambda v, o=out, a=in0, s1=s1, op0=op0:
                          v.tensor_scalar(out=o, in0=a, scalar1=s1, scalar2=None, op0=op0), reads, writes)
        return S.comp('dve', lambda v, o=out, a=in0, s1=s1, s2=s2, op0=op0, op1=op1:
                      v.tensor_scalar(out=o, in0=a, scalar1=s1, scalar2=s2, op0=op0, op1=op1), reads, writes)

    def stt(out, in0, sc, in1, op0, op1, reads, writes):
        return S.comp('dve', lambda v, o=out, a=in0, s=sc, b=in1, op0=op0, op1=op1:
                      v.scalar_tensor_tensor(out=o, in0=a, scalar=s, in1=b, op0=op0, op1=op1), reads, writes)

    def cp(out, in_, reads, writes):
        return S.comp('dve', lambda v, o=out, i=in_: v.tensor_copy(out=o, in_=i), reads, writes)

    def recip(out, in_, reads, writes):
        return S.comp('dve', lambda v, o=out, i=in_: v.reciprocal(out=o, in_=i), reads, writes)

    def memset(ap, val, writes):
        return S.comp('dve', lambda v, a=ap, c=val: v.memset(a, c), (), writes)

    class Tile:
        pass

    tiles = []
    for t in range(NTI):
        T = Tile()
        T.N = NT
        T.sample = False
        T.idx = t
        T.t0 = t * NT
        T.xb = B_X[t]
        T.x3 = X[:, :, t * NT:(t + 1) * NT]
        T.xc = [X[:, c, t * NT:(t + 1) * NT] for c in range(KC)]
        tiles.append(T)
    TS_ = Tile()
    TS_.N = NS
    TS_.sample = True
    TS_.idx = NTI
    TS_.t0 = 0
    TS_.xb = B_XS
    TS_.x3 = XS[:, :, :]
    TS_.xc = [XS[:, c, :] for c in range(KC)]
    tiles.insert(0, TS_)

    def modcol(l, j, c):
        return MOD[l][:, j, c, 0:1]

    def mods(l, j):
        return MOD[l][:, j, :, 1:1 + NS]

    def dercol(l, j, c):
        return DER[l][:, j, c, 0:1]

    def ders(l, j):
        return DER[l][:, j, :, 1:1 + NS]

    seed = phase_begin()
    IDF = talloc([128, 128], F32, seed)
    CIN = talloc([128, KC, 1 + NS], F32, seed)
    B_IDF = Buf("IDF", seed)
    S.dma('sp', PP[:, :], pp_d[:, :], writes=[B_PP])
    S.dma('sp', CIN, cin_d[:, :, :], writes=[B_CIN])
    S.dma('sp', IDF, ident_d[:, :], writes=[B_IDF])
    S.dma('sp', XS[:, :, :], xs_d[:, :, :], writes=B_XS)
    cp(IDB[:, :], IDF, [B_IDF], [B_ID])
    memset(ONESB[:, :], 1.0, [B_ID])
    B_ID.const = True
    o_b1, _ = PP_OFF['b1']
    ts(HB1[:, :], PP[:, o_b1 + 8:o_b1 + 16], 0.5, ALU.mult, [B_PP], [B_HB1])
    B_HB1.const = True
    act(CS[:, :, :], CIN, AF.Silu, [B_CIN], [B_CS])
    B_CS.const = True

    def load_x_tiles():
        for t in range(NTI):
            S.dma('pool', X[:, :, t * NT:(t + 1) * NT], xp_d[:, :, t * NT:(t + 1) * NT], writes=B_X[t])

    def ada_half(l, j, h):
        key = ('ada', l, j, h)
        wv, wb = AR.get(key)
        b = ps_next()
        pso = PS[:, b, 0:4 * (1 + NS)].rearrange("p (m n) -> p m n", m=4)
        for m4 in range(4):
            pairs = [(wv[:, kc, m4 * 128:(m4 + 1) * 128], CS[:, kc, :]) for kc in range(KC)]
            mm_group(pso[:, m4, :], pairs, [wb, B_CS], [B_PS[b]])
        AR.release(key)
        o, _ = PP_OFF['bada%d' % l]
        bcol = PP[:, o + j * 8 + 4 * h:o + j * 8 + 4 * h + 4].unsqueeze(2).broadcast_to([128, 4, 1 + NS])
        tt(MOD[l][:, j, 4 * h:4 * h + 4, :], pso, bcol, ALU.add, [B_PS[b], B_PP], [B_MOD[l][j]])
        if h == 1:
            if j == 1 or j == 4:
                ng = ppv('nmg%d' % l if j == 1 else 'nfg%d' % l).unsqueeze(2).broadcast_to([128, KC, 1 + NS])
                dj = 0 if j == 1 else 2
                stt(DER[l][:, dj, :, :], MOD[l][:, j, :, :], 1.0, ng, ALU.add, ALU.mult,
                    [B_MOD[l][j], B_PP], [B_DER[l][dj]])
            if j == 2:
                bo = ppv('b2' if l == 0 else 'bout').unsqueeze(2).broadcast_to([128, KC, 1 + NS])
                tt(DER[l][:, 1, :, :], MOD[l][:, 2, :, :], bo, ALU.mult, [B_MOD[l][2], B_PP], [B_DER[l][1]])

    class AdaStream:
        def __init__(self, items):
            self.items = [('ada',) + tuple(it) for it in items]
            self.i = 0
            if self.items:
                AR.set_wish([self.items[0]])

        def tick(self):
            if self.i < len(self.items):
                _, l, j, h = self.items[self.i]
                ada_half(l, j, h)
                self.i += 1
                if self.i < len(self.items):
                    AR.set_wish([self.items[self.i]])

        def drain(self):
            while self.i < len(self.items):
                self.tick()

    def rms_stats(T, SQ, B_SQ, RSTD, B_RSTD):
        N = T.N
        act(SQ[:, :, 0:N], T.x3, AF.Square, T.xb, [B_SQ])
        b = ps_next()
        mm_group(PS[:, b, 0:N], [(ONESB[:, :], SQ[:, c, 0:N]) for c in range(KC)], [B_SQ, B_ID], [B_PS[b]])
        act(RSTD[:, 0:N], PS[:, b, 0:N], AF.Sqrt, [B_PS[b]], [B_RSTD], bias=EPS, scale=1.0 / D)
        recip(RSTD[:, 0:N], RSTD[:, 0:N], [B_RSTD], [B_RSTD])

    def norm_mod(T, l, which, SQ, B_SQ, RSTD, B_RSTD, TMPF, B_TMPF, Hc, B_H, TMPF2=None, B_TMPF2=None):
        N = T.N
        rms_stats(T, SQ, B_SQ, RSTD, B_RSTD)
        jsh = 0 if which == 1 else 3
        dj = 0 if which == 1 else 2
        if not T.sample:
            for c in range(KC):
                tp, tb = (TMPF, B_TMPF) if (c % 2 == 0 or TMPF2 is None) else (TMPF2, B_TMPF2)
                stt(tp[:, 0:N], T.xc[c], dercol(l, dj, c), RSTD[:, 0:N], ALU.mult, ALU.mult,
                    [T.xb[c], B_DER[l][dj], B_RSTD], [tb])
                act(Hc(c), tp[:, 0:N], AF.Identity, [tb, B_MOD[l][jsh]], [B_H[c]], bias=modcol(l, jsh, c))
        else:
            T3 = TMPF[:, 0:KC * N].rearrange("p (c n) -> p c n", c=KC)
            tt(T3, T.x3, RSTD[:, 0:N].unsqueeze(1).broadcast_to([128, KC, N]), ALU.mult,
               list(T.xb) + [B_RSTD], [B_TMPF])
            tt(T3, T3, ders(l, dj), ALU.mult, [B_TMPF, B_DER[l][dj]], [B_TMPF])
            for c in range(KC):
                tt(Hc(c), T3[:, c, :], MOD[l][:, jsh, c, 1:1 + NS], ALU.add, [B_TMPF, B_MOD[l][jsh]], [B_H[c]])

    def norm_mod_gen(T, l, which, SQ, B_SQ, RSTD, B_RSTD, TMPF, B_TMPF, Hc, B_H, TMPF2=None, B_TMPF2=None):
        N = T.N
        assert not T.sample
        act(SQ[:, :, 0:N], T.x3, AF.Square, T.xb, [B_SQ])
        yield
        b = ps_next()
        mm_group(PS[:, b, 0:N], [(ONESB[:, :], SQ[:, c, 0:N]) for c in range(KC)], [B_SQ, B_ID], [B_PS[b]])
        yield
        act(RSTD[:, 0:N], PS[:, b, 0:N], AF.Sqrt, [B_PS[b]], [B_RSTD], bias=EPS, scale=1.0 / D)
        recip(RSTD[:, 0:N], RSTD[:, 0:N], [B_RSTD], [B_RSTD])
        yield
        jsh = 0 if which == 1 else 3
        dj = 0 if which == 1 else 2
        for c in range(KC):
            tp, tb = (TMPF, B_TMPF) if (c % 2 == 0 or TMPF2 is None) else (TMPF2, B_TMPF2)
            stt(tp[:, 0:N], T.xc[c], dercol(l, dj, c), RSTD[:, 0:N], ALU.mult, ALU.mult,
                [T.xb[c], B_DER[l][dj], B_RSTD], [tb])
            act(Hc(c), tp[:, 0:N], AF.Identity, [tb, B_MOD[l][jsh]], [B_H[c]], bias=modcol(l, jsh, c))
            yield

    def x_update(T, l, gj, m, psv, B_psv, TT_, B_TT, has_bias):
        N = T.N
        if not T.sample:
            if has_bias:
                act(TT_[:, 0:N], psv, AF.Identity, [B_psv, B_MOD[l][gj], B_DER[l][1]], [B_TT],
                    bias=dercol(l, 1, m), scale=modcol(l, gj, m))
                tt(T.xc[m], T.xc[m], TT_[:, 0:N], ALU.add, [B_TT, T.xb[m]], [T.xb[m]])
            else:
                stt(T.xc[m], psv, modcol(l, gj, m), T.xc[m], ALU.mult, ALU.add,
                    [B_psv, B_MOD[l][gj], T.xb[m]], [T.xb[m]])
        else:
            tt(TT_[:, 0:N], psv, MOD[l][:, gj, m, 1:1 + NS], ALU.mult, [B_psv, B_MOD[l][gj]], [B_TT])
            if has_bias:
                tt(TT_[:, 0:N], TT_[:, 0:N], DER[l][:, 1, m, 1:1 + NS], ALU.add, [B_TT, B_DER[l][1]], [B_TT])
            tt(T.xc[m], T.xc[m], TT_[:, 0:N], ALU.add, [B_TT, T.xb[m]], [T.xb[m]])

    def ln_stats(Yc, B_Y, YB, B_YB, YSQ, B_YSQ, nch, N, MEAN, B_MEAN, RS, B_RS, MSQ, B_MSQ, dim):
        b1 = ps_next()
        mm_group(PS[:, b1, 0:N], [(ONESB[:, :], YB[:, c, 0:N]) for c in range(nch)], [B_YB, B_ID], [B_PS[b1]])
        b2 = ps_next()
        mm_group(PS[:, b2, 0:N], [(ONESB[:, :], YSQ[:, c, 0:N]) for c in range(nch)], [B_YSQ, B_ID], [B_PS[b2]])
        ts(MEAN[:, 0:N], PS[:, b1, 0:N], 1.0 / dim, ALU.mult, [B_PS[b1]], [B_MEAN])
        tt(MSQ[:, 0:N], MEAN[:, 0:N], MEAN[:, 0:N], ALU.mult, [B_MEAN], [B_MSQ])
        stt(MSQ[:, 0:N], PS[:, b2, 0:N], 1.0 / dim, MSQ[:, 0:N], ALU.mult, ALU.subtract, [B_PS[b2], B_MSQ], [B_MSQ])
        act(RS[:, 0:N], MSQ[:, 0:N], AF.Sqrt, [B_MSQ], [B_RS], bias=EPS, scale=1.0)
        recip(RS[:, 0:N], RS[:, 0:N], [B_RS], [B_RS])

    def conv_phase():
        l = 0
        seed = phase_begin()
        BA = [talloc([128, KC, NT], BF16, seed) for _ in range(2)]
        SG = talloc([128, NT], F32, seed)
        ZT = talloc([128, KC, CTX + NT], BF16, seed)
        DGALL = talloc([128, 2 * CW * 128], BF16, seed)
        DG = [DGALL[:, i * CW * 128:(i + 1) * CW * 128].rearrange("p (k m) -> p k m", k=CW) for i in range(2)]
        CTXF = DGALL.bitcast(F32)[:, 0:KC * NS * CTX].rearrange("p (c s k) -> p c s k", c=KC, s=NS)
        Y = talloc([128, KC, NT], F32, seed)
        RSTD_A = talloc([128, NT], F32, seed)
        TMPF = talloc([128, NT], F32, seed)
        MEAN = talloc([128, NT], F32, seed)
        RSTD_D = talloc([128, NT], F32, seed)
        MSQ = TMPF
        assert tstate['off'] >= FFN_EARLY_BYTES, tstate['off']
        BB = talloc([128, KC, NT], BF16, seed)
        TT_ = talloc([128, NT], F32, seed)
        B_BA = [[Buf("BA%d_%d" % (i, c), seed) for c in range(KC)] for i in range(2)]
        B_BB = [Buf("BB%d" % c, seed) for c in range(KC)]
        B_SG = Buf("SG", seed)
        B_ZT = [Buf("ZT_%d" % c, seed) for c in range(KC)]
        B_ZTh = Buf("ZTh", seed)
        B_DG = [Buf("DG%d" % i, seed) for i in range(2)]
        B_Y = [Buf("Y%d" % c, seed) for c in range(KC)]
        B_RSTD_A = Buf("RSTD_A", seed)
        B_RSTD_D = Buf("RSTD_D", seed)
        B_MEAN = Buf("MEAN", seed)
        B_TMPF = Buf("TMPF", seed)
        B_MSQ = B_TMPF
        B_TT = Buf("TT", seed)
        B_CTXF = _Multi(B_DG)
        o_w, _ = PP_OFF['wdw']
        WDW = PP[:, o_w:o_w + 248].rearrange("p (c k) -> p c k", c=KC)

        def w1(m):
            v, b = AR.get(('pw1', m // 4))
            return v, (m % 4) * 128, b

        def w2(m):
            v, b = AR.get(('pw2', m // 4))
            return v, (m % 4) * 128, b

        def stA(T, bi):
            norm_mod(T, l, 1, BA[bi], _Multi(B_BA[bi]), RSTD_A, B_RSTD_A, TMPF, B_TMPF,
                     (lambda c, N=T.N, bi=bi: BA[bi][:, c, 0:N]), B_BA[bi], MEAN, B_MEAN)

        def stB(T, bi):
            N = T.N
            for j in range(KC):
                ba = ps_next()
                v, o, wb = w1(j)
                mm_group(PS[:, ba, 0:N], [(v[:, kc, o:o + 128], BA[bi][:, kc, 0:N]) for kc in range(KC)],
                         [wb] + B_BA[bi], [B_PS[ba]])
                bb = ps_next()
                v, o, wb = w1(j + 8)
                mm_group(PS[:, bb, 0:N], [(v[:, kc, o:o + 128], BA[bi][:, kc, 0:N]) for kc in range(KC)],
                         [wb] + B_BA[bi], [B_PS[bb]])
                act(SG[:, 0:N], PS[:, bb, 0:N], AF.Tanh, [B_PS[bb], B_HB1], [B_SG], bias=HB1[:, j:j + 1], scale=0.5)
                ts(SG[:, 0:N], SG[:, 0:N], 0.5, ALU.mult, [B_SG], [B_SG], s2=0.5, op1=ALU.add)
                if not T.sample:
                    stt(ZT[:, j, CTX:CTX + N], PS[:, ba, 0:N], ppc('b1', j), SG[:, 0:N], ALU.add, ALU.mult,
                        [B_PS[ba], B_SG, B_PP], [B_ZT[j]])
                    if T.idx == NTI - 1:
                        stt(ZL[:, j, :], PS[:, ba, N - CTX:N], ppc('b1', j), SG[:, N - CTX:N], ALU.add, ALU.mult,
                            [B_PS[ba], B_SG, B_PP], [B_ZL])
                else:
                    stt(ZS[:, j, :], PS[:, ba, 0:N], ppc('b1', j), SG[:, 0:N], ALU.add, ALU.mult,
                        [B_PS[ba], B_SG, B_PP], [B_ZSb[j]])

        def build_dg(j):
            dgi = j % 2
            tt(DG[dgi][:, :, :], IDB[:, :].unsqueeze(1).broadcast_to([128, CW, 128]),
               WDW[:, j, :].unsqueeze(2).broadcast_to([128, CW, 128]), ALU.mult,
               [B_ID, B_PP], [B_DG[dgi]])

        def stC(T, bi, gen=None):
            N = T.N
            nsteps = [1, 1, 1, 1, 2, 1, 2, 1]
            for j in range(KC):
                dgi = j % 2
                if j + 1 < KC:
                    build_dg(j + 1)
                b = ps_next()
                mm_group(PS[:, b, 0:N], [(DG[dgi][:, k, :], ZT[:, j, k:k + N]) for k in range(CW)],
                         [B_DG[dgi], B_ZT[j], B_ZTh], [B_PS[b]])
                act(Y[:, j, 0:N], PS[:, b, 0:N], AF.Identity, [B_PS[b], B_PP], [B_Y[j]], bias=ppc('bdw', j))
                act(BA[bi][:, j, 0:N], PS[:, b, 0:N], AF.Square, [B_PS[b], B_PP], [B_BA[bi][j]], bias=ppc('bdw', j))
                cp(BB[:, j, 0:N], Y[:, j, 0:N], [B_Y[j]], [B_BB[j]])
                if gen is not None:
                    for _ in range(nsteps[j]):
                        next(gen, None)
            if gen is not None:
                for _ in gen:
                    pass

        def stCsample(T, bi):
            N = T.N
            tt(CTXF, CTXF, WDW[:, :, 0:CTX].unsqueeze(2).broadcast_to([128, KC, NS, CTX]), ALU.mult,
               [B_CTXF, B_PP], [B_CTXF])
            Y3 = Y[:, :, 0:N]
            S.comp('dve', lambda v_, o=Y3, i=CTXF: v_.tensor_reduce(out=o, in_=i, axis=AX.X, op=ALU.add),
                   [B_CTXF], B_Y)
            T3 = TMPF[:, 0:KC * N].rearrange("p (c n) -> p c n", c=KC)
            tt(T3, ZS, WDW[:, :, CTX:CTX + 1].broadcast_to([128, KC, NS]), ALU.mult, B_ZSb + [B_PP], [B_TMPF])
            tt(Y3, Y3, T3, ALU.add, B_Y + [B_TMPF], B_Y)
            tt(Y3, Y3, ppv('bdw').unsqueeze(2).broadcast_to([128, KC, NS]), ALU.add, B_Y + [B_PP], B_Y)
            act(BA[bi][:, :, 0:N], Y3, AF.Square, B_Y, B_BA[bi])
            cp(BB[:, :, 0:N], Y3, B_Y, B_BB)
            S.dma('sp', zs_d[:, :, :], ZS, reads=B_ZSb)

        def stD(T, bi):
            N = T.N
            ln_stats(None, None, BB, _Multi(B_BB), BA[bi], _Multi(B_BA[bi]), KC, N, MEAN, B_MEAN,
                     RSTD_D, B_RSTD_D, MSQ, B_MSQ, D)
            for j in range(KC):
                tt(Y[:, j, 0:N], Y[:, j, 0:N], MEAN[:, 0:N], ALU.subtract, [B_Y[j], B_MEAN], [B_Y[j]])
                tt(Y[:, j, 0:N], Y[:, j, 0:N], RSTD_D[:, 0:N], ALU.mult, [B_Y[j], B_RSTD_D], [B_Y[j]])
                act(BB[:, j, 0:N], Y[:, j, 0:N], AF.Silu, [B_Y[j], B_PP], [B_BB[j]],
                    bias=ppc('lnb', j), scale=ppc('lng', j))

        def stE(T):
            N = T.N
            for m in range(KC):
                b = ps_next()
                v, o, wb = w2(m)
                mm_group(PS[:, b, 0:N], [(v[:, kc, o:o + 128], BB[:, kc, 0:N]) for kc in range(KC)],
                         [wb] + B_BB, [B_PS[b]])
                x_update(T, l, 2, m, PS[:, b, 0:N], B_PS[b], TT_, B_TT, True)

        samp = [T for T in tiles if T.sample][0]
        pt = [T for T in tiles if not T.sample]
        AR.set_wish([('ada', 0, 0, 0), ('ada', 0, 0, 1), ('ada', 0, 1, 0), ('ada', 0, 1, 1)]
                    + [('pw1', h) for h in range(4)])
        for j in (0, 1):
            for h in (0, 1):
                ada_half(0, j, h)
        S.dma('sp', X[:, :, 0:NT], xp_d[:, :, 0:NT], reads=[AR.get(('pw1', 1))[1]], writes=B_X[0])
        AR.set_wish([('pw2', 0), ('pw2', 1), ('ada', 0, 2, 0), ('ada', 0, 2, 1)])
        S.dma('sp', CTXF, scT_d[:, :, :, :], writes=[B_CTXF])
        for t in range(1, NTI):
            S.dma('sp', X[:, :, t * NT:(t + 1) * NT], xp_d[:, :, t * NT:(t + 1) * NT],
                  reads=[AR.get(('pw2', 1))[1]], writes=B_X[t])
        stA(samp, 0)
        stB(samp, 0)
        stA(pt[0], 1)
        stCsample(samp, 0)
        adas = None
        memset(ZT[:, :, 0:CTX], 0.0, [B_ZTh])
        for i, T in enumerate(pt):
            bi = (i + 1) % 2
            stB(T, bi)
            if i == len(pt) - 1:
                for h in range(4):
                    AR.release(('pw1', h))
            build_dg(0)
            if i == 0:
                stD(samp, 0)
                ada_half(0, 2, 0)
                ada_half(0, 2, 1)
                stE(samp)
                adas = AdaStream([(0, j, h) for j in (3, 4, 5) for h in (0, 1)])
            adas.tick()
            gen = None
            if i + 1 < len(pt):
                nb_ = 1 - bi
                gen = norm_mod_gen(pt[i + 1], l, 1, BA[nb_], _Multi(B_BA[nb_]), RSTD_A, B_RSTD_A, TMPF, B_TMPF,
                                   (lambda c, nb_=nb_: BA[nb_][:, c, 0:NT]), B_BA[nb_], MEAN, B_MEAN)
                next(gen, None)
            if i > 0:
                stE(pt[i - 1])
            if gen is not None:
                next(gen, None)
                next(gen, None)
            if i in (1, 2):
                adas.tick()
            stC(T, bi, gen)
            if i + 1 < len(pt):
                cp(ZT[:, :, 0:CTX], ZT[:, :, NT:NT + CTX], B_ZT, [B_ZTh])
            stD(T, bi)
        adas.drain()

        S.dma('sp', cssc_d[:, :, :], sc_d[:, 1:CTX, :])

        def tail():
            stE(pt[-1])
            S.dma('sp', csp_d[:, :, :], ZL[:, :, :], reads=[B_ZL])
            for h in range(2):
                AR.release(('pw2', h))
        return tail

    _Multi = list


    def _flat(bs):
        out = []
        for b in bs:
            if isinstance(b, (list, tuple)):
                out.extend(_flat(b))
            else:
                out.append(b)
        return out
    _orig_deps = S._deps
    _orig_upd = S._upd
    S._deps = lambda reads, writes: _orig_deps(_flat(reads), _flat(writes))
    S._upd = lambda reads, writes, tok: _orig_upd(_flat(reads), _flat(writes), tok)

    FFN_EARLY_BYTES = ((SEQ + NS) * KC * 2 + KC * NT * 2 + 2 * NT * 4 + 2 * NT * 4 + 2 * 4 * NT * 2 + NT * 4)

    def ffn_phase(l, prev_tail=None, mid_hook=None):
        seed = phase_begin()
        NTOT = SEQ + NS
        HF = talloc([128, KC, NTOT], BF16, seed)
        SQ = talloc([128, KC, NT], BF16, seed)
        RSTD = talloc([128, NT], F32, seed)
        TMPF = talloc([128, NT], F32, seed)
        SGT = [talloc([128, NT], F32, seed) for _ in range(2)]
        HID = [talloc([128, 4, NT], BF16, seed) for _ in range(2)]
        TT_ = talloc([128, NT], F32, seed)
        assert tstate['off'] == FFN_EARLY_BYTES, (tstate['off'], FFN_EARLY_BYTES)
        B_HF = [[Buf("HF%d_%d" % (t, c), seed) for c in range(KC)] for t in range(NTI + 1)]
        B_SQ = Buf("SQ", seed)
        B_RSTD = Buf("RSTD", seed)
        B_TMPF = Buf("TMPF", seed)
        B_SGT = [Buf("SGT%d" % i, seed) for i in range(2)]
        B_HID = [[Buf("HID%d_%d" % (i, hc), seed) for hc in range(4)] for i in range(2)]
        B_TT = Buf("TT", seed)

        def hcol(T):
            return SEQ if T.sample else T.t0

        units = FUNITS

        def load_unit(u, must):
            ok = AR.load(('g', l, u), must)
            ok = ok and AR.load(('u', l, u), must)
            ok = ok and AR.load(('d', l, u), must)
            return ok

        load_unit(0, True)
        adas = AdaStream([(1, j, h) for j in (0, 1, 2) for h in (0, 1)] if l == 0 else [])
        YF = talloc([128, KC, NT], F32, seed) if l == 1 else None
        B_YF = Buf("YF", seed)

        def final_tile(T):
            N = T.N
            rms_stats(T, SQ, B_SQ, RSTD, B_RSTD)
            for c in range(KC):
                stt(YF[:, c, 0:N], T.xc[c], ppc('fng', c), RSTD[:, 0:N], ALU.mult, ALU.mult,
                    [T.xb[c], B_PP, B_RSTD], [B_YF])
            if not T.sample:
                S.dma('sp', yp_d[:, :, T.t0:T.t0 + N], YF[:, :, 0:N], reads=[B_YF])
            else:
                S.dma('sp', ys_d[:, :, :], YF[:, :, 0:N], reads=[B_YF])

        def do_norm(T):
            h0 = hcol(T)
            norm_mod(T, l, 2, SQ, B_SQ, RSTD, B_RSTD, TMPF, B_TMPF,
                     (lambda c, h0=h0, N=T.N: HF[:, c, h0:h0 + N]), B_HF[T.idx], TT_, B_TT)

        do_norm(tiles[0])
        do_norm(tiles[1])
        if prev_tail is not None:
            prev_tail()
        it = 0
        for u, (f0, n) in enumerate(units):
            if mid_hook is not None and 1 <= u <= 3:
                mid_hook(u - 1)
            load_unit(u, True)
            if u + 1 < len(units):
                load_unit(u + 1, False)
            wg, bg = AR.get(('g', l, u))
            wu, bu = AR.get(('u', l, u))
            wd, bd = AR.get(('d', l, u))
            last_unit = (l == 1 and u == len(units) - 1)
            tord = tiles if not last_unit else ([T_ for T_ in tiles if not T_.sample] + [T_ for T_ in tiles if T_.sample])
            for tix, T in enumerate(tord):
                if u == 0 and tix >= 1 and tix + 1 < len(tiles):
                    do_norm(tiles[tix + 1])
                N = T.N
                h0 = hcol(T)
                hi = it % 2
                it += 1
                for hc in range(n):
                    b1 = ps_next()
                    mm_group(PS[:, b1, 0:N], [(wg[:, kc, hc * 128:(hc + 1) * 128], HF[:, kc, h0:h0 + N])
                                              for kc in range(KC)], [bg] + B_HF[T.idx], [B_PS[b1]])
                    b2 = ps_next()
                    mm_group(PS[:, b2, 0:N], [(wu[:, kc, hc * 128:(hc + 1) * 128], HF[:, kc, h0:h0 + N])
                                              for kc in range(KC)], [bu] + B_HF[T.idx], [B_PS[b2]])
                    si = hc % 2
                    act(SGT[si][:, 0:N], PS[:, b1, 0:N], AF.Silu, [B_PS[b1]], [B_SGT[si]])
                    tt(HID[hi][:, hc, 0:N], SGT[si][:, 0:N], PS[:, b2, 0:N], ALU.mult,
                       [B_SGT[si], B_PS[b2]], [B_HID[hi][hc]])
                for m0 in (0, 4):
                    banks = [ps_next() for _ in range(4)]
                    for mi in range(4):
                        m = m0 + mi
                        b = banks[mi]
                        if n > 1:
                            mm_group(PS[:, b, 0:N], [(wd[:, hc, m * 128:(m + 1) * 128], HID[hi][:, hc, 0:N])
                                                     for hc in range(n - 1)], [bd] + B_HID[hi][0:n - 1], [B_PS[b]],
                                     first=True, last=False)
                    for mi in range(4):
                        m = m0 + mi
                        b = banks[mi]
                        mm_group(PS[:, b, 0:N], [(wd[:, n - 1, m * 128:(m + 1) * 128], HID[hi][:, n - 1, 0:N])],
                                 [bd, B_HID[hi][n - 1]], [B_PS[b]], first=(n == 1), last=True)
                        x_update(T, l, 5, m, PS[:, b, 0:N], B_PS[b], TT_, B_TT, False)
                if l == 1 and u == len(units) - 1:
                    final_tile(T)
            adas.tick()
            for k in ('g', 'u', 'd'):
                AR.release((k, l, u))
        adas.drain()

    GC = {}

    def gmlp_setup(step):
        if step == 0:
            seed = S.all_tokens()
            o = FFN_EARLY_BYTES
            R, o = talloc_at(o, [128, EC, 128], F32)
            BVHL, o = talloc_at(o, [128, E], BF16)
            WT, o = talloc_at(o, [128, 8, 128], BF16)
            TRILB, o = talloc_at(o, [128, 128], BF16)
            assert o <= TEMP_BYTES, o
            GC.update(R=R, BVHL=BVHL, WT=WT, TRILB=TRILB, B_R=Buf("R", seed), B_HL=Buf("HL", seed),
                      B_WT=Buf("WT", seed), B_TRILB=Buf("TRILB", seed))
            S.dma('pool', WT, wsT_d[:, :, :], writes=[GC['B_WT']])
            S.dma('pool', TRILB, tril_d[:, :], writes=[GC['B_TRILB']])
            S.dma('pool', BVHL[0:2, :], rw_d[0:2, 0:E], writes=[GC['B_HL']])
            ROW = R[0:2, :, :].rearrange("p c t -> p (c t)")
            S.dma('sp', ROW, rw_d[0:2, 0:E], writes=[GC['B_R']])
        elif step == 1:
            R, BVHL, WT, TRILB = GC['R'], GC['BVHL'], GC['WT'], GC['TRILB']
            tt(WT, WT, TRILB.unsqueeze(1).broadcast_to([128, 8, 128]), ALU.mult, [GC['B_WT'], GC['B_TRILB']],
               [GC['B_WT']])
            o_n, _ = PP_OFF['negm']
            negm = PP[0:2, o_n:o_n + 1]
            ROW = R[0:2, :, :].rearrange("p c t -> p (c t)")
            stt(BVHL[0:2, :], BVHL[0:2, :], negm, ROW, ALU.mult, ALU.add, [GC['B_HL'], GC['B_R'], B_PP], [GC['B_HL']])
            Rv = R.rearrange("p (g two) t -> p g two t", two=2)
            S.dma('sp', Rv[:, :, 0, :], rw_d[:, 3 * E:3 * E + D].rearrange("p (g t) -> p g t", g=8), writes=[GC['B_R']])
        else:
            R, WT = GC['R'], GC['WT']
            for hb in range(2):
                b = ps_next()
                mm_multi([(PS[:, b, gg * 128:(gg + 1) * 128], [(ONESB[:, :], WT[:, hb * 4 + gg, :])])
                          for gg in range(4)], [GC['B_WT'], B_ID], [B_PS[b]])
                for gg in range(4):
                    g = hb * 4 + gg
                    for c in (2 * g + 1, 2 * g):
                        stt(R[:, c, :], PS[:, b, gg * 128:(gg + 1) * 128], ppc('glnb', c), R[:, 2 * g, :],
                            ALU.mult, ALU.add, [B_PS[b], B_PP, GC['B_R']], [GC['B_R']])

    def gmlp_phase():
        l = 1
        seed = phase_begin()
        H = talloc([128, KC, NT], BF16, seed)
        R, BVHL, WT, B_R, B_HL, B_WT = GC['R'], GC['BVHL'], GC['WT'], GC['B_R'], GC['B_HL'], GC['B_WT']
        V = [talloc([128, E], F32, seed) for _ in range(2)]
        VNB = talloc([128, 4, E], BF16, seed)
        UM = talloc([128, EC, NT], BF16, seed)
        ST = [talloc([128, 16], F32, seed) for _ in range(2)]
        assert tstate['off'] <= FFN_EARLY_BYTES, tstate['off']
        B_H = [Buf("H%d" % c, seed) for c in range(KC)]
        B_V = [Buf("V%d" % i, seed) for i in range(2)]
        B_VNB = [Buf("VNB%d" % q, seed) for q in range(4)]
        B_UM = [Buf("UM%d" % c, seed) for c in range(EC)]
        B_ST = [Buf("ST%d" % i, seed) for i in range(2)]
        Uf = [V[0][:, i * NT:(i + 1) * NT] for i in range(2)]
        MX = [V[0][:, (2 + i) * NT:(3 + i) * NT] for i in range(2)]
        TMPF = V[1][:, 0:NT]
        RSTD = V[1][:, NT:2 * NT]
        TTs = [V[1][:, (2 + i) * NT:(3 + i) * NT] for i in range(2)]
        B_Uf = [Buf("Uf%d" % i, seed) for i in range(2)]
        B_MX = [Buf("MX%d" % i, seed) for i in range(2)]
        B_TMPF = Buf("TMPFg", seed)
        B_RSTD = Buf("RSTDg", seed)
        B_TT = [Buf("TTg%d" % i, seed) for i in range(2)]
        ALIAS = [B_Uf + B_MX, [B_TMPF, B_RSTD] + B_TT]

        def vbufs(i):
            return [B_V[i]] + ALIAS[i]

        for h in range(4):
            AR.load(('wv', h))

        def wvk(kc):
            v, b = AR.get(('wv', kc // 2))
            return v[:, kc % 2, :], b

        wv_bufs = [AR.get(('wv', h))[1] for h in range(4)]

        def load_wu(ti, h, must=True):
            return AR.load(('wu', ti, h), must)

        def load_wo(ti, h, must=True):
            return AR.load(('wo', ti, h), must)

        def do_norm(T):
            norm_mod(T, l, 1, H, _Multi(B_H), RSTD, B_RSTD, TMPF, B_TMPF, (lambda c, N=T.N: H[:, c, 0:N]), B_H,
                     Uf[0], B_Uf[0])

        adas = AdaStream([(1, j, h) for j in (3, 4, 5) for h in (0, 1)])
        samp = [T for T in tiles if T.sample][0]
        pt = [T for T in tiles if not T.sample]
        HS = talloc([128, KC, NS], BF16, seed)
        UMS = talloc([128, EC, NS], BF16, seed)
        VBS = talloc([128, EC, NS], BF16, seed)
        B_VBS = Buf("VBS", seed)
        B_HS = [Buf("HS%d" % c, seed) for c in range(KC)]
        B_UMS = [Buf("UMS%d" % c, seed) for c in range(EC)]
        assert tstate['off'] <= FFN_EARLY_BYTES, tstate['off']

        def sample_norm():
            norm_mod(samp, l, 1, HS, _Multi(B_HS), RSTD, B_RSTD, TMPF, B_TMPF, (lambda c: HS[:, c, 0:NS]), B_HS)

        def sample_v():
            N = NS
            VF = UM_F32
            for c in range(EC):
                b = ps_next()
                pairs = []
                for kc in range(KC):
                    wv_, wb_ = wvk(kc)
                    pairs.append((wv_[:, c * 128:(c + 1) * 128], HS[:, kc, 0:N]))
                mm_group(PS[:, b, 0:N], pairs, wv_bufs + B_HS, [B_PS[b]])
                act(VF[:, c, :], PS[:, b, 0:N], AF.Gelu_apprx_tanh, [B_PS[b], B_PP], [B_VF], bias=ppc('binv', c))
            VSQ = UMS[:, :, 0:NS]
            VB = VBS
            act(VSQ, VF, AF.Square, [B_VF], B_UMS)
            cp(VB, VF, [B_VF], [B_VBS])
            MEAN = ST2[:, 0:NS]
            MSQ = ST2[:, NS:2 * NS]
            RS = ST2[:, 2 * NS:3 * NS]
            ln_stats(None, None, VB, B_VBS, VSQ, _Multi(B_UMS), EC, NS, MEAN, B_ST2, RS, B_ST2, MSQ, B_ST2, E)
            tt(VF, VF, MEAN.unsqueeze(1).broadcast_to([128, EC, NS]), ALU.subtract, [B_VF, B_ST2], [B_VF])
            tt(VF, VF, RS.unsqueeze(1).broadcast_to([128, EC, NS]), ALU.mult, [B_VF, B_ST2], [B_VF])
            tt(VF, VF, ppv('glng').unsqueeze(2).broadcast_to([128, EC, NS]), ALU.mult, [B_VF, B_PP], [B_VF])
            tt(VF, VF, ppv('glnb').unsqueeze(2).broadcast_to([128, EC, NS]), ALU.add, [B_VF, B_PP], [B_VF])
            S.dma('sp', gv_d[:, :, :], VF, reads=[B_VF])
            tt(VF, VF, ppv('ws00').unsqueeze(2).broadcast_to([128, EC, NS]), ALU.mult, [B_VF, B_PP], [B_VF])
            tt(VF, VF, ppv('bs00').unsqueeze(2).broadcast_to([128, EC, NS]), ALU.add, [B_VF, B_PP], [B_VF])

        sample_norm()
        do_norm(pt[0])
        uidx = 0
        for _pos, T in enumerate(pt):
            N = T.N
            ti = T.idx
            with_s = (_pos == 0)
            order = [('wu', ti, h) for h in range(4)] + [('wo', ti, h) for h in range(4)]
            if _pos + 1 < len(pt):
                order.append(('wu', pt[_pos + 1].idx, 0))

            def stream_get(key, order=order):
                (load_wu if key[0] == 'wu' else load_wo)(key[1], key[2], True)
                nx = order[order.index(key) + 1] if order.index(key) + 1 < len(order) else None
                if nx is not None:
                    (load_wu if nx[0] == 'wu' else load_wo)(nx[1], nx[2], False)
                return AR.get(key)
            for q in range(4):
                vi = q % 2
                b0 = ps_next4()
                for nb in range(4):
                    pairs = []
                    for kc in range(KC):
                        wv_, wb_ = wvk(kc)
                        pairs.append((H[:, kc, q * 128:(q + 1) * 128], wv_[:, nb * 512:(nb + 1) * 512]))
                    pairs.append((ONESB[0:2, :], BVHL[0:2, nb * 512:(nb + 1) * 512]))
                    mm_group(PS[:, b0 + nb, :], pairs, wv_bufs + B_H + [B_HL, B_ID], [B_PS[b0 + nb]])
                psv = PS[:, b0:b0 + 4, :].rearrange("p a b -> p (a b)")
                st = ST[vi]
                memset(st[:, 0:2], 0.0, [B_ST[vi]])
                act(V[vi], psv, AF.Gelu_apprx_tanh, [B_PS[b0 + i] for i in range(4)], vbufs(vi) + [B_ST[vi]],
                    accum=st[:, 0:1])
                act(VNB[:, q, :], V[vi], AF.Square, [B_V[vi]], [B_VNB[q], B_ST[vi]], accum=st[:, 1:2])
                ts(st[:, 2:3], st[:, 0:1], 1.0 / E, ALU.mult, [B_ST[vi]], [B_ST[vi]])
                tt(st[:, 3:4], st[:, 2:3], st[:, 2:3], ALU.mult, [B_ST[vi]], [B_ST[vi]])
                stt(st[:, 3:4], st[:, 1:2], 1.0 / E, st[:, 3:4], ALU.mult, ALU.subtract, [B_ST[vi]], [B_ST[vi]])
                act(st[:, 4:5], st[:, 3:4], AF.Sqrt, [B_ST[vi]], [B_ST[vi]], bias=EPS, scale=1.0)
                recip(st[:, 4:5], st[:, 4:5], [B_ST[vi]], [B_ST[vi]])
                ts(VNB[:, q, :], V[vi], st[:, 2:3], ALU.subtract, vbufs(vi) + [B_ST[vi]], [B_VNB[q]],
                   s2=st[:, 4:5], op1=ALU.mult)
                if with_s and q == 1:
                    sample_v()
            if T is pt[-1]:
                for h in range(4):
                    AR.release(('wv', h))
            for c in range(EC):
                g = c // 2
                ui = uidx % 2
                uidx += 1
                wu, bu = stream_get(('wu', ti, c // 4)) if c % 4 == 0 else AR.get(('wu', ti, c // 4))

                def issue_u(cc, wu=wu, bu=bu):
                    bb_ = ps_next()
                    mm_group(PS[:, bb_, 0:N], [(wu[:, kc, (cc % 4) * 128:(cc % 4 + 1) * 128], H[:, kc, 0:N])
                                               for kc in range(KC)], [bu] + B_H, [B_PS[bb_]])
                    return bb_
                if c % 4 == 0:
                    ubank = {c: issue_u(c)}
                if c % 4 != 3:
                    ubank[c + 1] = issue_u(c + 1)
                bu_ = ubank[c]
                bm = ps_next()
                mm_multi([(PS[:, bm, q * 128:(q + 1) * 128], [(VNB[:, q, c * 128:(c + 1) * 128], WT[:, g, :])])
                          for q in range(4)], B_VNB + [B_WT], [B_PS[bm]])
                act(Uf[ui], PS[:, bu_, 0:N], AF.Gelu_apprx_tanh, [B_PS[bu_], B_PP], [B_Uf[ui]], bias=ppc('binu', c))
                stt(MX[ui].rearrange("p (q t) -> p q t", q=4), PS[:, bm, :].rearrange("p (q t) -> p q t", q=4),
                    ppc('glng', c), R[:, c, :].unsqueeze(1).broadcast_to([128, 4, 128]), ALU.mult, ALU.add,
                    [B_PS[bm], B_PP, B_R], [B_MX[ui]])
                tt(UM[:, c, 0:N], Uf[ui], MX[ui], ALU.mult, [B_Uf[ui], B_MX[ui]], [B_UM[c]])
                if with_s:
                    ui = uidx % 2
                    uidx += 1
                    b = ps_next()
                    mm_group(PS[:, b, 0:NS], [(wu[:, kc, (c % 4) * 128:(c % 4 + 1) * 128], HS[:, kc, 0:NS])
                                              for kc in range(KC)], [bu] + B_HS, [B_PS[b]])
                    act(Uf[ui][:, 0:NS], PS[:, b, 0:NS], AF.Gelu_apprx_tanh, [B_PS[b], B_PP], [B_Uf[ui]],
                        bias=ppc('binu', c))
                    tt(UMS[:, c, 0:NS], Uf[ui][:, 0:NS], UM_F32[:, c, :], ALU.mult, [B_Uf[ui], B_VF], [B_UMS[c]])
                if c % 4 == 3:
                    AR.release(('wu', ti, c // 4))
            gen = None
            if _pos + 1 < len(pt):
                Tn = pt[_pos + 1]
                gen = norm_mod_gen(Tn, l, 1, H, _Multi(B_H), RSTD, B_RSTD, TMPF, B_TMPF,
                                   (lambda c: H[:, c, 0:NT]), B_H, Uf[0], B_Uf[0])
                next(gen, None)
            adas.tick()
            if _pos == 0:
                adas.tick()
                adas.tick()
            for m in range(KC):
                if gen is not None:
                    next(gen, None)
                    if m in (3, 7):
                        next(gen, None)
                wo, bo = stream_get(('wo', ti, m // 2)) if m % 2 == 0 else AR.get(('wo', ti, m // 2))
                b = ps_next()
                mm_group(PS[:, b, 0:N], [(wo[:, c, (m % 2) * 128:(m % 2 + 1) * 128], UM[:, c, 0:N])
                                         for c in range(EC)], [bo] + B_UM, [B_PS[b]])
                x_update(T, l, 2, m, PS[:, b, 0:N], B_PS[b], TTs[m % 2], B_TT[m % 2], True)
                if with_s:
                    b = ps_next()
                    mm_group(PS[:, b, 0:NS], [(wo[:, c, (m % 2) * 128:(m % 2 + 1) * 128], UMS[:, c, 0:NS])
                                              for c in range(EC)], [bo] + B_UMS, [B_PS[b]])
                    x_update(samp, l, 2, m, PS[:, b, 0:NS], B_PS[b], TTs[(m + 1) % 2], B_TT[(m + 1) % 2], True)
                if m % 2 == 1:
                    AR.release(('wo', ti, m // 2))
            if gen is not None:
                for _ in gen:
                    pass
            if T is pt[-1]:
                adas.drain()

    def final_phase():
        seed = phase_begin()
        SQ = talloc([128, KC, NT], BF16, seed)
        RSTD = talloc([128, NT], F32, seed)
        YF = [talloc([128, KC, NT], F32, seed) for _ in range(2)]
        B_SQ = Buf("SQ", seed)
        B_RSTD = Buf("RSTD", seed)
        B_YF = [Buf("YF%d" % i, seed) for i in range(2)]
        for T in tiles:
            N = T.N
            yi = T.idx % 2
            rms_stats(T, SQ, B_SQ, RSTD, B_RSTD)
            for c in range(KC):
                stt(YF[yi][:, c, 0:N], T.xc[c], ppc('fng', c), RSTD[:, 0:N], ALU.mult, ALU.mult,
                    [T.xb[c], B_PP, B_RSTD], [B_YF[yi]])
            if not T.sample:
                S.dma('sp', yp_d[:, :, T.t0:T.t0 + N], YF[yi][:, :, 0:N], reads=[B_YF[yi]])
            else:
                S.dma('sp', ys_d[:, :, :], YF[yi][:, :, 0:N], reads=[B_YF[yi]])

    conv_tail = conv_phase()
    ffn_phase(0, prev_tail=conv_tail, mid_hook=gmlp_setup)
    gmlp_phase()
    ffn_phase(1)
    S.final_wait('sp')

    semnames = set()
    for e in Sched.ENG:
        for w, fn, inc in S.stream[e]:
            for k, _ in w:
                semnames.add(k)
            if inc:
                semnames.add(inc[0])
    semnames = sorted(semnames)
    import contextlib
    with contextlib.ExitStack() as es:
        SEM = {n: es.enter_context(nc.semaphore(n)) for n in semnames}
        block = es.enter_context(nc.Block())

        def replay(eng_obj, name):
            for w, fn, inc in S.stream[name]:
                for k, v in w:
                    eng_obj.wait_ge(SEM[k], v)
                if fn is not None:
                    ins = fn(eng_obj)
                    ins.then_inc(SEM[inc[0]], inc[1])

        @block.tensor
        def _(pe):
            replay(pe, 'pe')

        @block.scalar
        def _(a):
            replay(a, 'act')

        @block.vector
        def _(v):
            replay(v, 'dve')

        @block.gpsimd
        def _(g):
            replay(g, 'pool')

        @block.sync
        def _(s):
            replay(s, 'sp')
    return nc, AR.order


_NC_CACHE = {}


def _cols(v):
    v = np.asarray(v, dtype=np.float32).reshape(-1, 128)
    return np.ascontiguousarray(v.T)


def kernel(x_prompt, x_sample, c_prompt, c_sample, state_conv, w_ada, b_ada, norm_mix_g, norm_ffn_g,
           final_norm_g, conv_w_pw1, conv_b_pw1, conv_w_dw, conv_b_dw, conv_ln_g, conv_ln_b, conv_w_pw2,
           conv_b_pw2, gmlp_w_in, gmlp_b_in, gmlp_ln_g, gmlp_ln_b, gmlp_w_s, gmlp_b_s, gmlp_w_out,
           gmlp_b_out, ffn_w_gate, ffn_w_up, ffn_w_down):
    f = lambda a: np.ascontiguousarray(np.asarray(a, dtype=np.float32))
    x_prompt, x_sample, c_prompt, c_sample, state_conv = map(f, (x_prompt, x_sample, c_prompt, c_sample, state_conv))
    parts = {
        'nmg0': _cols(norm_mix_g[0]), 'nmg1': _cols(norm_mix_g[1]),
        'nfg0': _cols(norm_ffn_g[0]), 'nfg1': _cols(norm_ffn_g[1]), 'fng': _cols(final_norm_g),
        'bada0': _cols(b_ada[0]), 'bada1': _cols(b_ada[1]),
        'b1': _cols(conv_b_pw1[0]), 'bdw': _cols(conv_b_dw[0]), 'lng': _cols(conv_ln_g[0]),
        'lnb': _cols(conv_ln_b[0]), 'b2': _cols(conv_b_pw2[0]),
        'binu': _cols(np.asarray(gmlp_b_in)[0, :E]), 'binv': _cols(np.asarray(gmlp_b_in)[0, E:]),
        'glng': _cols(gmlp_ln_g[0]), 'glnb': _cols(gmlp_ln_b[0]), 'bout': _cols(gmlp_b_out[0]),
        'wdw': np.ascontiguousarray(np.asarray(conv_w_dw, np.float32)[0].reshape(CW, KC, 128).transpose(2, 1, 0)).reshape(128, KC * CW),
        'ws00': np.ascontiguousarray(np.broadcast_to(np.repeat(np.asarray(gmlp_w_s, np.float32)[0, :, 0, 0], 2)[None, :], (128, EC))),
        'bs00': np.ascontiguousarray(np.broadcast_to(np.repeat(np.asarray(gmlp_b_s, np.float32)[0, :, 0], 2)[None, :], (128, EC))),
        'negm': np.array([0.0, -1.0] + [0.0] * 126, np.float32).reshape(128, 1),
    }
    pp = np.concatenate([parts[n].reshape(128, w) for n, w in PP_LAYOUT], axis=1).astype(np.float32)
    assert pp.shape == (128, NPP)
    gb = np.asarray(gmlp_b_in, np.float32)[0, E:]
    rw = np.concatenate([gb, np.asarray(gmlp_ln_g, np.float32)[0], np.asarray(gmlp_ln_b, np.float32)[0],
                         np.asarray(gmlp_b_s, np.float32)[0].reshape(-1)])
    rw = np.ascontiguousarray(np.broadcast_to(rw[None, :], (128, rw.shape[0]))).astype(np.float32)
    wsT = np.ascontiguousarray(np.asarray(gmlp_w_s, np.float32)[0].transpose(2, 0, 1))
    tril = np.ascontiguousarray(np.triu(np.ones((128, 128), np.float32)))
    ident = np.eye(128, dtype=np.float32)
    shared = {
        'pp': pp, 'rw': rw, 'wsT': wsT, 'tril': tril, 'ident': ident,
        'w_ada': f(w_ada), 'w_pw1': f(conv_w_pw1[0]), 'w_pw2': f(conv_w_pw2[0]),
        'w_in': f(gmlp_w_in[0]), 'w_out': f(gmlp_w_out[0]),
        'w_gate': f(ffn_w_gate), 'w_up': f(ffn_w_up), 'w_down': f(ffn_w_down),
    }
    in_maps = []
    for i in range(NCORES):
        xp = np.ascontiguousarray(x_prompt[i].reshape(SEQ, KC, 128).transpose(2, 1, 0))
        xs = np.ascontiguousarray(x_sample[i * NS:(i + 1) * NS, 0].reshape(NS, KC, 128).transpose(2, 1, 0))
        cc = np.concatenate([c_prompt[i:i + 1], c_sample[i * NS:(i + 1) * NS]], axis=0)
        cin = np.ascontiguousarray(cc.reshape(1 + NS, KC, 128).transpose(2, 1, 0))
        sc = np.ascontiguousarray(state_conv[0, i * NS:(i + 1) * NS])
        scT = np.ascontiguousarray(sc.reshape(NS, CTX, KC, 128).transpose(3, 2, 0, 1))
        m = dict(shared)
        m.update({'xp': xp, 'xs': xs, 'cin': cin, 'sc': sc, 'scT': scT})
        in_maps.append(m)
    if 'nc' not in _NC_CACHE:
        _, _order = build_program()
        _NC_CACHE['nc'] = build_program(_order)[0]
    nc = _NC_CACHE['nc']
    res = run_bass_kernel_spmd(nc, in_maps, core_ids=list(range(NCORES)))
    R = res.results
    y_prompt = np.stack([R[i]['yp'].transpose(2, 1, 0).reshape(SEQ, D) for i in range(NCORES)], 0)
    y_sample = np.concatenate([R[i]['ys'].transpose(2, 1, 0).reshape(NS, 1, D) for i in range(NCORES)], 0)
    csp = np.stack([R[i]['csp'].transpose(2, 1, 0).reshape(CTX, D) for i in range(NCORES)], 0)[None]
    css = np.concatenate([np.concatenate([R[i]['cssc'], R[i]['zs'].transpose(2, 1, 0).reshape(NS, 1, D)], axis=1)
                          for i in range(NCORES)], 0)[None]
    gv = np.concatenate([R[i]['gv'].transpose(2, 1, 0).reshape(NS, 1, E) for i in range(NCORES)], 0)[None]
    return (np.ascontiguousarray(y_prompt, dtype=np.float32), np.ascontiguousarray(y_sample, dtype=np.float32),
            np.ascontiguousarray(csp, dtype=np.float32), np.ascontiguousarray(css, dtype=np.float32),
            np.ascontiguousarray(gv, dtype=np.float32))
```
